# Optimizing a Trainium2 kernel written in Bass

```python
import jax, jax.numpy as jnp
from jax import lax
import numpy as np

D_MODEL = 2048
BATCH = 2
SEQ = 4096
DEPTH = 1

GRID_W = 64
CTX_LEN = 256
D_GLA = 1024
D_CONV = 1024
GLA_HEADS = 4
GLA_DK = 128
GLA_DV = 256
GLA_KEY = GLA_HEADS * GLA_DK
GATE_RANK = 16
GATE_NORMALIZER = 16.0
CHUNK = 64
CONV_WIDTH = 31
D_FF = 5632
N_MOD = 9
RMS_EPS = 1e-6
HEAD_NORM_EPS = 1e-5
LN_EPS = 1e-5

P_K = GLA_KEY
P_V = D_GLA
P_Q = GLA_KEY
P_G = D_GLA
P_GLU = 2 * D_CONV
OFF_V = P_K
OFF_GKF = OFF_V + P_V
OFF_GKB = OFF_GKF + GATE_RANK
CTX_COLS = OFF_GKB + GATE_RANK
OFF_Q = CTX_COLS
OFF_G = OFF_Q + P_Q
OFF_GLU = OFF_G + P_G
D_IN = OFF_GLU + P_GLU

kernel_name = "hybrid_gla_conformer_dit_block"


def rmsnorm(h, gain, eps=RMS_EPS):
    hf = h.astype(jnp.float32)
    hf = hf * lax.rsqrt(jnp.mean(hf * hf, axis=-1, keepdims=True) + eps)
    return (hf * gain.astype(jnp.float32)).astype(h.dtype)


def modulate(h, gain, shift, scale):
    return rmsnorm(h, gain) * (1 + scale) + shift


def swiglu(h, w_in, w_out):
    gate, up = jnp.split(h @ w_in, 2, axis=-1)
    return (jax.nn.silu(gate) * up) @ w_out


def heads(t, d):
    b, t_len, _ = t.shape
    return t.reshape(b, t_len, -1, d).transpose(0, 2, 1, 3)


def flip(t):
    return jnp.flip(t, axis=2)


def log_decay(p_gk, w2, b2):
    z = (p_gk.astype(jnp.float32) @ w2.astype(jnp.float32)) + b2.astype(jnp.float32)
    return jax.nn.log_sigmoid(z) / GATE_NORMALIZER


def gla_final_state(k, v, logd):
    b = jnp.cumsum(logd, axis=2)
    k_dec = k * jnp.exp(b[:, :, -1:] - b)
    return jnp.einsum('bhld,bhle->bhde', k_dec, v)


def gla_chunked(q, k, v, logd, s0):
    bsz, nh, t_len, dk = q.shape
    dv = v.shape[-1]
    n = t_len // CHUNK
    q = q.reshape(bsz, nh, n, CHUNK, dk)
    k = k.reshape(bsz, nh, n, CHUNK, dk)
    v = v.reshape(bsz, nh, n, CHUNK, dv)
    b = jnp.cumsum(logd.reshape(bsz, nh, n, CHUNK, dk), axis=3)
    b_mid = b[:, :, :, CHUNK // 2 - 1:CHUNK // 2]
    b_last = b[:, :, :, -1:]
    qs = q * jnp.exp(b - b_mid)
    ks = k * jnp.exp(b_mid - b)
    mask = jnp.tril(jnp.ones((CHUNK, CHUNK), dtype=bool))
    att = jnp.where(mask, jnp.einsum('bhncd,bhnsd->bhncs', qs, ks), 0.0)
    o = jnp.einsum('bhncs,bhnse->bhnce', att, v)
    chunk_kv = jnp.einsum('bhncd,bhnce->bhnde', k * jnp.exp(b_last - b), v)
    chunk_decay = jnp.exp(b_last[:, :, :, 0])

    def step(s, inp):
        dec, kv = inp
        return dec[..., None] * s + kv, s

    _, s_in = lax.scan(step, s0, (jnp.moveaxis(chunk_decay, 2, 0), jnp.moveaxis(chunk_kv, 2, 0)))
    s_in = jnp.moveaxis(s_in, 0, 2)
    o = o + jnp.einsum('bhncd,bhnde->bhnce', q * jnp.exp(b), s_in)
    return o.reshape(bsz, nh, t_len, dv)


def gla_inputs_kv(p, w_gk2, b_gk2):
    f32 = jnp.float32
    k = heads(p[..., :OFF_V].astype(f32), GLA_DK)
    v = heads(p[..., OFF_V:OFF_GKF].astype(f32), GLA_DV)
    ld_f = heads(log_decay(p[..., OFF_GKF:OFF_GKB], w_gk2[0], b_gk2[0]), GLA_DK)
    ld_b = heads(log_decay(p[..., OFF_GKB:CTX_COLS], w_gk2[1], b_gk2[1]), GLA_DK)
    return k, v, ld_f, ld_b


def context_states(pc, w_gk2, b_gk2):
    k, v, ld_f, ld_b = gla_inputs_kv(pc, w_gk2, b_gk2)
    return gla_final_state(k, v, ld_f), gla_final_state(flip(k), flip(v), flip(ld_b))


def gla_group(p, w_gk2, b_gk2, norm_g, s_f, s_b):
    k, v, ld_f, ld_b = gla_inputs_kv(p, w_gk2, b_gk2)
    q = heads(p[..., OFF_Q:OFF_G].astype(jnp.float32), GLA_DK) * (GLA_DK ** -0.5)
    g = p[..., OFF_G:OFF_GLU]
    o = gla_chunked(q, k, v, ld_f, s_f) + flip(gla_chunked(flip(q), flip(k), flip(v), flip(ld_b), s_b))
    o = o * lax.rsqrt(jnp.mean(o * o, axis=-1, keepdims=True) + HEAD_NORM_EPS) * norm_g.astype(jnp.float32)
    bsz, _, t_len, _ = o.shape
    o = o.transpose(0, 2, 1, 3).reshape(bsz, t_len, D_GLA).astype(p.dtype)
    return o * jax.nn.silu(g)


def conv_latent(u, w, b):
    bsz, t_len, ch = u.shape
    rows = t_len // GRID_W
    half = ch // 2
    grid = u.reshape(bsz, rows, GRID_W, ch)
    dn = ('NHWC', 'HWIO', 'NHWC')
    row_part = lax.conv_general_dilated(grid[..., :half], w[:, :half].reshape(1, CONV_WIDTH, 1, half).astype(u.dtype),
                                        (1, 1), 'SAME', dimension_numbers=dn, feature_group_count=half)
    col_part = lax.conv_general_dilated(grid[..., half:], w[:, half:].reshape(CONV_WIDTH, 1, 1, half).astype(u.dtype),
                                        (1, 1), 'SAME', dimension_numbers=dn, feature_group_count=half)
    return jnp.concatenate([row_part, col_part], axis=-1).reshape(bsz, t_len, ch) + b


def conv_seq(u, w, b):
    ch = u.shape[-1]
    y = lax.conv_general_dilated(u, w.reshape(CONV_WIDTH, 1, ch).astype(u.dtype), (1,), 'SAME',
                                 dimension_numbers=('NWC', 'WIO', 'NWC'), feature_group_count=ch)
    return y + b


def conformer_group(p_glu, conv_fn, conv_w, conv_b, ln_g, ln_b):
    a, gate = jnp.split(p_glu, 2, axis=-1)
    y = conv_fn(a * jax.nn.sigmoid(gate), conv_w, conv_b)
    yf = y.astype(jnp.float32)
    mu = jnp.mean(yf, axis=-1, keepdims=True)
    var = jnp.mean(jnp.square(yf - mu), axis=-1, keepdims=True)
    yf = (yf - mu) * lax.rsqrt(var + LN_EPS) * ln_g.astype(jnp.float32) + ln_b.astype(jnp.float32)
    return jax.nn.silu(yf).astype(p_glu.dtype)


def setup_inputs(seed: int = 0) -> dict:
    key = jax.random.key(seed)
    ks = jax.random.split(key, 26)
    f32 = jnp.float32

    def nrm(k, shape, scale):
        return jax.random.normal(k, shape, f32) * scale

    def gain(k, shape):
        return 1.0 + 0.02 * jax.random.normal(k, shape, f32)

    d = D_MODEL
    return {
        "x": nrm(ks[0], (BATCH, SEQ, d), 1.0),
        "c": nrm(ks[1], (BATCH, d), 1.0),
        "ctx": nrm(ks[2], (BATCH, CTX_LEN, d), 1.0),
        "c_ctx": nrm(ks[3], (d,), 1.0),
        "w_mod": nrm(ks[4], (DEPTH, d, N_MOD * d), 0.5 * d ** -0.5),
        "b_mod": nrm(ks[5], (DEPTH, N_MOD * d), 0.01),
        "norm_ffn1": gain(ks[6], (DEPTH, d)),
        "w_ffn1_in": nrm(ks[7], (DEPTH, d, 2 * D_FF), d ** -0.5),
        "w_ffn1_out": nrm(ks[8], (DEPTH, D_FF, d), D_FF ** -0.5),
        "norm_mix": gain(ks[9], (DEPTH, d)),
        "w_in": nrm(ks[10], (DEPTH, d, D_IN), d ** -0.5),
        "w_gk2": nrm(ks[11], (DEPTH, 2, GATE_RANK, GLA_KEY), GATE_RANK ** -0.5),
        "b_gk2": nrm(ks[12], (DEPTH, 2, GLA_KEY), 0.5),
        "gla_norm": gain(ks[13], (DEPTH, GLA_DV)),
        "conv_w": nrm(ks[14], (DEPTH, CONV_WIDTH, D_CONV), CONV_WIDTH ** -0.5),
        "conv_b": nrm(ks[15], (DEPTH, D_CONV), 0.01),
        "conv_ln_g": gain(ks[16], (DEPTH, D_CONV)),
        "conv_ln_b": nrm(ks[17], (DEPTH, D_CONV), 0.01),
        "w_out": nrm(ks[18], (DEPTH, D_GLA + D_CONV, d), (D_GLA + D_CONV) ** -0.5),
        "norm_ffn2": gain(ks[19], (DEPTH, d)),
        "w_ffn2_in": nrm(ks[20], (DEPTH, d, 2 * D_FF), d ** -0.5),
        "w_ffn2_out": nrm(ks[21], (DEPTH, D_FF, d), D_FF ** -0.5),
        "norm_final": gain(ks[22], (d,)),
    }


def reference(x, c, ctx, c_ctx, w_mod, b_mod, norm_ffn1, w_ffn1_in, w_ffn1_out, norm_mix, w_in,
              w_gk2, b_gk2, gla_norm, conv_w, conv_b, conv_ln_g, conv_ln_b, w_out, norm_ffn2,
              w_ffn2_in, w_ffn2_out, norm_final):
    bsz = x.shape[0]
    h = x
    hc = ctx
    for l in range(DEPTH):
        mod = (jax.nn.silu(c) @ w_mod[l] + b_mod[l]).reshape(bsz, N_MOD, D_MODEL)[:, :, None, :]
        mc = (jax.nn.silu(c_ctx) @ w_mod[l] + b_mod[l]).reshape(N_MOD, D_MODEL)

        h = h + 0.5 * mod[:, 2] * swiglu(modulate(h, norm_ffn1[l], mod[:, 0], mod[:, 1]), w_ffn1_in[l], w_ffn1_out[l])
        hc = hc + 0.5 * mc[2] * swiglu(modulate(hc, norm_ffn1[l], mc[0], mc[1]), w_ffn1_in[l], w_ffn1_out[l])

        hx = modulate(h, norm_mix[l], mod[:, 3], mod[:, 4])
        hcm = modulate(hc, norm_mix[l], mc[3], mc[4])
        px = hx @ w_in[l]
        pc_kv = hcm @ w_in[l][:, :CTX_COLS]
        s_f, s_b = context_states(pc_kv, w_gk2[l], b_gk2[l])
        o_gla = gla_group(px, w_gk2[l], b_gk2[l], gla_norm[l], s_f, s_b)
        o_conv = conformer_group(px[..., OFF_GLU:], conv_latent, conv_w[l], conv_b[l], conv_ln_g[l], conv_ln_b[l])
        h = h + mod[:, 5] * (jnp.concatenate([o_gla, o_conv], axis=-1) @ w_out[l])

        h = h + 0.5 * mod[:, 8] * swiglu(modulate(h, norm_ffn2[l], mod[:, 6], mod[:, 7]), w_ffn2_in[l], w_ffn2_out[l])

        if l + 1 < DEPTH:
            pc = hcm @ w_in[l]
            zeros = jnp.zeros((bsz, GLA_HEADS, GLA_DK, GLA_DV), jnp.float32)
            oc_gla = gla_group(pc, w_gk2[l], b_gk2[l], gla_norm[l], zeros, zeros)
            oc_conv = conformer_group(pc[..., OFF_GLU:], conv_seq, conv_w[l], conv_b[l], conv_ln_g[l], conv_ln_b[l])
            hc = hc + mc[5] * (jnp.concatenate([oc_gla, oc_conv], axis=-1) @ w_out[l])
            hc = hc + 0.5 * mc[8] * swiglu(modulate(hc, norm_ffn2[l], mc[6], mc[7]), w_ffn2_in[l], w_ffn2_out[l])
    return rmsnorm(h, norm_final)
```

```python
import numpy as np
import concourse.bass as bass
import concourse.mybir as mybir
from concourse.bass_utils import run_bass_kernel_spmd
from contextlib import ExitStack

F32 = mybir.dt.float32
BF16 = mybir.dt.bfloat16
AF = mybir.ActivationFunctionType
ALU = mybir.AluOpType

D = 2048
KD = 16
NL = 1024
NC_ = 64
NT = NL + NC_
DFF = 5632
NFC = 44
RMS_EPS = 1e-6

SAME_ENGINE_SYNC = True


class Prog:
    ENGS = ("pe", "act", "dve", "pool", "sp")

    def __init__(self, nc):
        self.nc = nc
        self.ops = {e: [] for e in self.ENGS}
        self.count = {e: 0 for e in self.ENGS}
        self.waited = {e: {} for e in self.ENGS}
        self.res_w = {}
        self.res_r = {}
        self.dma_cnt = {}
        self.total_keys = set()
        self.max_wait = {}

    def _deps(self, eng, reads, writes):
        deps = {}

        def add(src, val):
            if deps.get(src, 0) < val:
                deps[src] = val

        for r in reads:
            w = self.res_w.get(r)
            if w is not None:
                add(*w)
        for w_ in writes:
            w = self.res_w.get(w_)
            if w is not None:
                add(*w)
            for src, val in self.res_r.get(w_, {}).items():
                add(src, val)
        out = []
        for src, val in deps.items():
            if src == eng:
                if not SAME_ENGINE_SYNC or eng == "pe":
                    continue
                if val > self.count[eng]:
                    continue
            if self.waited[eng].get(src, 0) >= val:
                continue
            self.waited[eng][src] = val
            out.append((src, val))
        return out

    def op(self, eng, fn, reads=(), writes=(), signal=True):
        for src, val in self._deps(eng, reads, writes):
            self.ops[eng].append(("wait", src, val))
            if self.max_wait.get(src, 0) < val:
                self.max_wait[src] = val
        if signal:
            self.count[eng] += 1
            seq = self.count[eng]
        else:
            seq = self.count[eng] + 1
        self.ops[eng].append(("op", fn, signal))
        for r in reads:
            self.res_r.setdefault(r, {})[eng] = seq
        for w in writes:
            self.res_w[w] = (eng, seq)
            self.res_r[w] = {}
        return seq

    def dma(self, q, out, in_, reads=(), writes=(), key=None, total=False, **kw):
        assert key is not None
        for src, val in self._deps(q, reads, writes):
            self.ops[q].append(("wait", src, val))
            if self.max_wait.get(src, 0) < val:
                self.max_wait[src] = val
        self.dma_cnt[key] = self.dma_cnt.get(key, 0) + 1
        val = 16 * self.dma_cnt[key]
        if total:
            self.total_keys.add(key)
        src = ("dma", key)
        self.ops[q].append(("dma", out, in_, key, kw))
        for r in reads:
            self.res_r.setdefault(r, {})[src] = val
        for w in writes:
            self.res_w[w] = (src, val)
            self.res_r[w] = {}

    def custom(self, q, fn, reads=(), writes=(), key=None):
        for src, val in self._deps(q, reads, writes):
            self.ops[q].append(("wait", src, val))
        self.dma_cnt[key] = self.dma_cnt.get(key, 0) + 1
        val = self.dma_cnt[key]
        src = ("dma", key)
        self.ops[q].append(("custom", fn, key))
        for r in reads:
            self.res_r.setdefault(r, {})[src] = val
        for w in writes:
            self.res_w[w] = (src, val)
            self.res_r[w] = {}

    def wait_all(self, eng, resources):
        for src, val in self._deps(eng, resources, ()):
            self.ops[eng].append(("wait", src, val))


class Cfg:
    def __init__(self, D=2048, DFF=5632, mstop=99):
        self.mstop = mstop
        self.D = D
        self.KD = D // 128
        self.DFF = DFF
        self.NFC = DFF // 128
        self.CPC = 9 * self.KD // 4
        assert 9 * self.KD % 4 == 0 and self.NFC % 2 == 0 and D % 512 == 0


OFF_K, OFF_V, OFF_GKF, OFF_GKB, OFF_Q, OFF_G, OFF_GA, OFF_GB = 0, 512, 1536, 1552, 1568, 2080, 3104, 4128
D_IN = 5152
EW = 1028
GC_MTB = 0
GC_CE = 256
GC_CH = 768
GC_MASK = 776
GC_ID = 1032
GC_ONE = 1160
GC_N = 1288


def gla_consts():
    g = np.zeros((128, GC_N), np.float32)
    s = np.arange(128)[:, None]
    t = np.arange(128)[None, :]
    same = (s // 64) == (t // 64)
    sc = -1.0 / 16.0
    g[:, GC_MTB:GC_MTB + 128] = sc * (same & (s > t))
    g[:, GC_MTB + 128:GC_MTB + 256] = sc * (same & (s < t))
    cum_f = sc * (same & (s <= t))
    cum_b = sc * (same & (s >= t))
    mid_f = sc * (same & ((s % 64) <= 31))
    mid_b = sc * (same & ((s % 64) >= 32))
    g[:, GC_CE:GC_CE + 128] = cum_f
    g[:, GC_CE + 128:GC_CE + 256] = cum_f - mid_f
    g[:, GC_CE + 256:GC_CE + 384] = cum_b
    g[:, GC_CE + 384:GC_CE + 512] = cum_b - mid_b
    g[:, GC_CH] = sc * (np.arange(128) < 64)
    g[:, GC_CH + 1] = sc * (np.arange(128) >= 64)
    g[:, GC_MASK:GC_MASK + 128] = (same & (t >= s))
    g[:, GC_MASK + 128:GC_MASK + 256] = (same & (t <= s))
    g[:, GC_ID:GC_ID + 128] = np.eye(128)
    g[:, GC_ONE:GC_ONE + 128] = 1.0 / 1024.0
    return g


def build_program(cfg=None, stage=99, debug=False):
    cfg = cfg or Cfg()
    D, KD, DFF, NFC, CPC = cfg.D, cfg.KD, cfg.DFF, cfg.NFC, cfg.CPC
    nc = bass.Bass("TRN2", target_bir_lowering=False)
    es = ExitStack()
    pg = Prog(nc)

    def dram_in(name, shape, dt=F32):
        return nc.dram_tensor(name, list(shape), dt, kind="ExternalInput").ap()

    def dram_tmp(name, shape, dt=F32):
        return nc.dram_tensor(name, list(shape), dt, kind="Internal").ap()

    xT = dram_in("xT", [D, NT])
    cT = dram_in("cT", [128, KD, 2])
    wmod = dram_in("wmod", [D, CPC * 128])
    bmod = dram_in("bmod", [128, CPC])
    gains = dram_in("gains", [128, 4, KD])
    w1i = dram_in("w_ffn1_in", [D, 2 * DFF])
    w1o = dram_in("w_ffn1_out", [DFF, D])
    w2i = dram_in("w_ffn2_in", [D, 2 * DFF])
    w2o = dram_in("w_ffn2_out", [DFF, D])
    w_in = dram_in("w_in", [D, D_IN])
    w_out = dram_in("w_out", [2048, D])
    gconst_d = dram_in("gconst", [128, GC_N])
    w2aug_d = dram_in("w2aug", [17, 2, 512])
    gnb_d = dram_in("gnb", [128, 256])
    convw_d = dram_in("convw", [128, 8, 31])
    convp_d = dram_in("convp", [128, 8, 3])
    segmask_d = dram_in("segmask", [128, 16])
    outT = nc.dram_tensor("outT", [D, NL], F32, kind="ExternalOutput").ap()
    mod_in = dram_tmp("mod_in", [128, 2 * CPC])
    mod_out = dram_tmp("mod_out", [4 * 128, 2 * CPC])
    h_spill = dram_tmp("h_spill", [128, KD * NL])
    sg_spill = dram_tmp("sg_spill", [NL, 1024], BF16)
    u_row = dram_tmp("u_row", [512, NL])
    ucol_in = [dram_tmp("ucol_in%d" % i, [256, NL]) for i in range(2)]
    ucol_g = [dram_tmp("ucol_g%d" % i, [4 * 256, NL]) for i in range(2)]
    st_in = [dram_tmp("st_in%d" % i, [128, EW]) for i in range(4)]
    st_out = [dram_tmp("st_out%d" % i, [4 * 128, EW]) for i in range(4)]
    dbg = {}

    def dbg_out(name, shape, dt=F32):
        dbg[name] = nc.dram_tensor("dbg_" + name, list(shape), dt, kind="ExternalOutput").ap()
        return dbg[name]

    def sb(name, shape, dt):
        return es.enter_context(nc.sbuf_tensor(name, list(shape), dt))

    HN = max(KD * NT, 17408)
    H = sb("H", [128, HN], F32)
    hT = H[:, 0:KD * NT].rearrange("p (k t) -> p k t", t=NT)
    HMN = max(KD * NT, 17408)
    HM = sb("HM", [128, HMN], BF16)
    hm = HM[:, 0:KD * NT].rearrange("p (k t) -> p k t", t=NT)
    wr = [sb("wr%d" % i, [128, 8192], BF16) for i in range(2)]
    NRING = 2
    ACTA = sb("ACTA", [128, 12 * NT], BF16)
    act = ACTA[:, :].rearrange("p (f t) -> p f t", t=NT)
    TMPA = sb("TMPA", [128, 3072], F32)
    ones_bf = sb("ones_bf", [128, 128], BF16)
    cs = sb("cs", [128, KD, 2], F32)
    cs_bf = sb("cs_bf", [128, KD, 2], BF16)
    bmod_sb = sb("bmod_sb", [128, CPC], F32)
    modloc = sb("modloc", [128, 2, CPC], F32)
    modL = sb("modL", [128, 9 * KD], F32)
    modC = sb("modC", [128, 9 * KD], F32)
    gains_sb = sb("gains_sb", [128, 4, KD], F32)
    coef = sb("coef", [128, 2, 3, 3, KD], F32)
    rb = sb("rb", [128, NT], F32)
    epsc = sb("epsc", [128, 4], F32)
    gconst = sb("gconst_sb", [128, GC_N], F32)
    w2aug = sb("w2aug_sb", [17, 2, 512], F32)
    gnb = sb("gnb_sb", [128, 256], F32)
    convw = sb("convw_sb", [128, 8, 31], F32)
    convp = sb("convp_sb", [128, 8, 3], F32)
    segmask = sb("segmask_sb", [128, 16], F32)
    pgk = sb("pgk", [32, 2, NT], F32)
    ps = [es.enter_context(nc.psum_tensor("ps%d" % i, [128, 512], F32)) for i in range(8)]

    sq = [TMPA[:, 0:256].bitcast(BF16), TMPA[:, 256:512].bitcast(BF16)]
    rtmp = TMPA[:, 512:1024]
    ntmp = [TMPA[:, 1024:1536], TMPA[:, 1536:2048]]
    sg = [TMPA[:, 2048:2560], TMPA[:, 2560:3072]]

    TILES3 = [(0, 512), (512, 512), (1024, 64)]
    TILES2 = [(0, 512), (512, 512)]
    HALL = [("h", k) for k in range(KD)]
    HMALL = [("hm", k) for k in range(KD)]

    def transfer(old, new):
        u = {}
        for o in old:
            w = pg.res_w.get(o)
            if w is not None:
                u[w[0]] = max(u.get(w[0], 0), w[1])
            for src, val in pg.res_r.get(o, {}).items():
                u[src] = max(u.get(src, 0), val)
        for n in new:
            pg.res_w.pop(n, None)
            pg.res_r[n] = dict(u)

    def dma_custom(q, fn, reads, writes, key):
        for src, val in pg._deps(q, reads, writes):
            pg.ops[q].append(("wait", src, val))
        pg.dma_cnt[key] = pg.dma_cnt.get(key, 0) + 1
        val = 16 * pg.dma_cnt[key]
        src = ("dma", key)
        pg.ops[q].append(("custom16", fn, key))
        for r in reads:
            pg.res_r.setdefault(r, {})[src] = val
        for w in writes:
            pg.res_w[w] = (src, val)
            pg.res_r[w] = {}

    xTv = xT.rearrange("(k p) t -> p k t", p=128)
    kstep = max(1, KD // 4)
    for k0 in range(0, KD, kstep):
        pg.dma("sp", hT[:, k0:k0 + kstep, :], xTv[:, k0:k0 + kstep, :],
               writes=[("h", k) for k in range(k0, k0 + kstep)], key="ld_x", total=True)
    for dst, src, name in ((cs[:], cT[:, :, :], "cs"), (bmod_sb[:], bmod[:, :], "bmod"),
                           (gains_sb[:], gains[:, :, :], "gains"), (gconst[:], gconst_d[:, :], "gconst"),
                           (w2aug[:], w2aug_d[:, :, :], "w2aug"), (gnb[:], gnb_d[:, :], "gnb"),
                           (convw[:], convw_d[:, :, :], "convw"), (convp[:], convp_d[:, :, :], "convp"),
                           (segmask[:], segmask_d[:, :], "segmask")):
        pg.dma("sp", dst, src, writes=[name], key="const", total=True)
    pg.op("dve", lambda e: e.memset(ones_bf[:], 1.0 / D), writes=["ones"])
    pg.op("dve", lambda e: e.memset(epsc[:, 0:1], RMS_EPS), writes=["epsc"])
    pg.op("dve", lambda e: e.memset(epsc[:, 1:2], 1e-5), writes=["epsc"])
    pg.op("dve", lambda e: e.memset(epsc[:, 2:3], 1.0), writes=["epsc"])
    pg.op("dve", lambda e: e.memset(pgk[:], 1.0), writes=["pgk"])

    ring = {"n": 0}

    def load_slab(pieces):
        s = ring["n"] % NRING
        ring["n"] += 1
        for vf, src in pieces:
            pg.dma("pool", vf(wr[s]), src, writes=[("wr", s)], key=("wr", s))
        return s

    pg.op("act", lambda e: e.activation(out=cs_bf[:], in_=cs[:], func=AF.Silu), reads=["cs"], writes=["cs_bf"])
    wmv = wmod.rearrange("(k p) n -> p k n", p=128)
    for gl in range(CPC):
        s = load_slab([(lambda t: t[:, 0:KD * 128].rearrange("p (k n) -> p k n", n=128),
                        wmv[:, :, gl * 128:(gl + 1) * 128])])
        wv = wr[s][:, 0:KD * 128].rearrange("p (k n) -> p k n", n=128)
        for k in range(KD):
            pg.op("pe", (lambda e, wv=wv, k=k, gl=gl: e.matmul(
                ps[7][:, gl * 2:gl * 2 + 2], lhsT=wv[:, k, :], rhs=cs_bf[:, k, :],
                start=(k == 0), stop=(k == KD - 1))),
                reads=[("wr", s), "cs_bf"], writes=[("ps", 7)], signal=(k == KD - 1))
    pg.op("dve", lambda e: e.tensor_tensor(
        out=modloc[:].rearrange("p v g -> p g v"),
        in0=ps[7][:, 0:2 * CPC].rearrange("p (g v) -> p g v", v=2),
        in1=bmod_sb[:].unsqueeze(2).to_broadcast([128, CPC, 2]), op=ALU.add),
        reads=[("ps", 7), "bmod"], writes=["modloc"])
    pg.dma("sp", mod_in[:, :], modloc[:].rearrange("p v g -> p (v g)"), reads=["modloc"], writes=["mod_in"],
           key="mod_io")
    pg.custom("pool", lambda e: e.collective_compute(
        "AllGather", ALU.bypass, replica_groups=[[0, 1, 2, 3], [4, 5, 6, 7]],
        ins=[mod_in[:, :]], outs=[mod_out[:, :]]), reads=["mod_in"], writes=["mod_out"], key="cc_mod")
    mov = mod_out.rearrange("(r p) n -> p r n", p=128)

    pidc = {}

    def get_pid(eng):
        if "pid" not in pidc:
            pidc["pid"] = eng.partition_id()
        return pidc["pid"]

    pg.dma("sp", modL[:].rearrange("p (r g) -> p r g", g=CPC), mov[:, :, 0:CPC], reads=["mod_out"],
           writes=["modL"], key="ld_mod", total=True)
    pg.dma("sp", modC[:].rearrange("p (r g) -> p r g", g=CPC), mov[:, :, CPC:2 * CPC], reads=["mod_out"],
           writes=["modC"], key="ld_mod", total=True)
    if debug:
        dm = dbg_out("mod", [128, 2, 9 * KD])
        pg.dma("sp", dm[:, 0, :], modL[:], reads=["modL"], key=("dbg", 1))
        pg.dma("sp", dm[:, 1, :], modC[:], reads=["modC"], key=("dbg", 2))

    for wi, m in enumerate((modL, modC)):
        mv = m[:].rearrange("p (m k) -> p m k", k=KD)
        for s_ in range(3):
            fac = 1.0 if s_ == 1 else 0.5
            pg.op("dve", (lambda e, wi=wi, mv=mv, s_=s_: e.scalar_tensor_tensor(
                out=coef[:, wi, s_, 0, :], in0=mv[:, 3 * s_ + 1, :], scalar=1.0, in1=gains_sb[:, s_, :],
                op0=ALU.add, op1=ALU.mult)), reads=["modL", "modC", "gains"], writes=["coef"])
            pg.op("dve", (lambda e, wi=wi, mv=mv, s_=s_: e.tensor_copy(
                out=coef[:, wi, s_, 1, :], in_=mv[:, 3 * s_, :])), reads=["modL", "modC"], writes=["coef"])
            pg.op("dve", (lambda e, wi=wi, mv=mv, s_=s_, fac=fac: e.tensor_scalar(
                out=coef[:, wi, s_, 2, :], in0=mv[:, 3 * s_ + 2, :], scalar1=fac, scalar2=None, op0=ALU.mult)),
                reads=["modL", "modC"], writes=["coef"])

    def rms_stats(tiles):
        cnt = 0
        for (t0, n) in tiles:
            for k in range(KD):
                si = cnt % 2
                sqb = sq[si]
                cnt += 1
                pg.op("act", (lambda e, sqb=sqb, k=k, t0=t0, n=n: e.activation(
                    out=sqb[:, 0:n], in_=hT[:, k, t0:t0 + n], func=AF.Square)),
                    reads=[("h", k)], writes=[("sq", si)])
                pg.op("pe", (lambda e, sqb=sqb, k=k, n=n: e.matmul(
                    ps[6][:, 0:n], lhsT=ones_bf[:, :], rhs=sqb[:, 0:n], start=(k == 0), stop=(k == KD - 1))),
                    reads=[("sq", si), "ones"], writes=[("ps", 6)], signal=True)
            pg.op("act", (lambda e, n=n: e.activation(out=rtmp[:, 0:n], in_=ps[6][:, 0:n], func=AF.Sqrt,
                                                      bias=epsc[:, 0:1], scale=1.0)),
                  reads=[("ps", 6), "epsc"], writes=["rtmp"])
            pg.op("dve", (lambda e, t0=t0, n=n: e.reciprocal(out=rb[:, t0:t0 + n], in_=rtmp[:, 0:n])),
                  reads=["rtmp"], writes=[("rb", t0)])

    def norm_mod(s_, tiles):
        rms_stats(tiles)
        cnt = 0
        for (t0, n) in tiles:
            wi = 1 if t0 >= NL else 0
            for k in range(KD):
                ti = cnt % 2
                tb = ntmp[ti]
                cnt += 1
                pg.op("dve", (lambda e, tb=tb, k=k, t0=t0, n=n: e.tensor_tensor(
                    out=tb[:, 0:n], in0=hT[:, k, t0:t0 + n], in1=rb[:, t0:t0 + n], op=ALU.mult)),
                    reads=[("h", k), ("rb", t0)], writes=[("ntmp", ti)])
                pg.op("act", (lambda e, tb=tb, k=k, t0=t0, n=n, wi=wi: e.activation(
                    out=hm[:, k, t0:t0 + n], in_=tb[:, 0:n], func=AF.Identity,
                    bias=coef[:, wi, s_, 1, k:k + 1], scale=coef[:, wi, s_, 0, k:k + 1])),
                    reads=[("ntmp", ti), "coef"], writes=[("hm", k)])

    def ffn(w_i, w_o, s_, tiles):
        wiv = w_i.rearrange("(k p) n -> p k n", p=128)
        wov = w_o.rearrange("(f p) n -> p f n", p=128)
        nsl_tot = NFC // 2
        ngr = min(4, nsl_tot)
        groups = []
        a = 0
        for gi in range(ngr):
            n_ = nsl_tot // ngr + (1 if gi < nsl_tot % ngr else 0)
            groups.append((a, n_))
            a += n_
        assert max(n_ for _, n_ in groups) * 2 <= 12
        pair = 0
        oc = 0
        for (sl0, nsl) in groups:
            nf = 2 * nsl
            for sl in range(sl0, sl0 + nsl):
                s = load_slab([
                    (lambda t: t[:, 0:KD * 512].rearrange("p (k g n) -> p k g n", g=2, n=256)[:, :, 0, :],
                     wiv[:, :, sl * 256:(sl + 1) * 256]),
                    (lambda t: t[:, 0:KD * 512].rearrange("p (k g n) -> p k g n", g=2, n=256)[:, :, 1, :],
                     wiv[:, :, DFF + sl * 256:DFF + (sl + 1) * 256]),
                ])
                wv = wr[s][:, 0:KD * 512].rearrange("p (k g n) -> p k g n", g=2, n=256)
                for fi in range(2):
                    fl = (sl - sl0) * 2 + fi
                    for (t0, n) in tiles:
                        pG = 2 * (pair % 2)
                        pU = pG + 1
                        sgi = pair % 2
                        pair += 1
                        for g_, pb in ((0, pG), (1, pU)):
                            for k in range(KD):
                                pg.op("pe", (lambda e, wv=wv, k=k, g_=g_, fi=fi, pb=pb, t0=t0, n=n: e.matmul(
                                    ps[pb][:, 0:n], lhsT=wv[:, k, g_, fi * 128:(fi + 1) * 128],
                                    rhs=hm[:, k, t0:t0 + n], start=(k == 0), stop=(k == KD - 1))),
                                    reads=[("wr", s), ("hm", k)], writes=[("ps", pb)], signal=(k == KD - 1))
                        pg.op("act", (lambda e, pG=pG, sgi=sgi, n=n: e.activation(
                            out=sg[sgi][:, 0:n], in_=ps[pG][:, 0:n], func=AF.Silu)),
                            reads=[("ps", pG)], writes=[("sg", sgi)])
                        pg.op("dve", (lambda e, pU=pU, sgi=sgi, fl=fl, t0=t0, n=n: e.tensor_tensor(
                            out=act[:, fl, t0:t0 + n], in0=sg[sgi][:, 0:n], in1=ps[pU][:, 0:n], op=ALU.mult)),
                            reads=[("sg", sgi), ("ps", pU)], writes=[("act", fl)])
            f0 = 2 * sl0
            for ds_ in range(D // 512):
                s = load_slab([(lambda t, nf=nf: t[:, 0:nf * 512].rearrange("p (f n) -> p f n", n=512),
                                wov[:, f0:f0 + nf, ds_ * 512:(ds_ + 1) * 512])])
                wv = wr[s][:, 0:nf * 512].rearrange("p (f n) -> p f n", n=512)
                for dc in range(4):
                    dg = ds_ * 4 + dc
                    for (t0, n) in tiles:
                        wi = 1 if t0 >= NL else 0
                        pb = 4 + (oc % 2)
                        oc += 1
                        for fl in range(nf):
                            pg.op("pe", (lambda e, wv=wv, fl=fl, dc=dc, pb=pb, t0=t0, n=n: e.matmul(
                                ps[pb][:, 0:n], lhsT=wv[:, fl, dc * 128:(dc + 1) * 128],
                                rhs=act[:, fl, t0:t0 + n], start=(fl == 0), stop=(fl == nf - 1))),
                                reads=[("wr", s), ("act", fl)], writes=[("ps", pb)], signal=(fl == nf - 1))
                        pg.op("dve", (lambda e, pb=pb, dg=dg, t0=t0, n=n, wi=wi: e.scalar_tensor_tensor(
                            out=hT[:, dg, t0:t0 + n], in0=ps[pb][:, 0:n], scalar=coef[:, wi, s_, 2, dg:dg + 1],
                            in1=hT[:, dg, t0:t0 + n], op0=ALU.mult, op1=ALU.add)),
                            reads=[("ps", pb), "coef", ("h", dg)], writes=[("h", dg)])

    norm_mod(0, TILES3)
    ffn(w1i, w1o, 0, TILES3)
    if debug:
        d1 = dbg_out("h1", [D, NT])
        pg.dma("sp", d1.rearrange("(k p) t -> p k t", p=128), hT, reads=HALL, key=("dbg", 3))

    if stage >= 2:
        mixer_args = dict(locals())
        _mixer(mixer_args)

    if stage >= 3:
        norm_mod(2, TILES2)
        ffn(w2i, w2o, 2, TILES2)

    rms_stats(TILES2)
    for k in range(KD):
        for (t0, n) in TILES2:
            pg.op("dve", (lambda e, k=k, t0=t0, n=n: e.scalar_tensor_tensor(
                out=hT[:, k, t0:t0 + n], in0=hT[:, k, t0:t0 + n], scalar=gains_sb[:, 3, k:k + 1],
                in1=rb[:, t0:t0 + n], op0=ALU.mult, op1=ALU.mult)),
                reads=[("h", k), ("rb", t0), "gains"], writes=[("h", k)])
    outv = outT.rearrange("(k p) t -> p k t", p=128)
    for k0 in range(0, KD, kstep):
        pg.dma("sp", outv[:, k0:k0 + kstep, :], hT[:, k0:k0 + kstep, 0:NL],
               reads=[("h", k) for k in range(k0, k0 + kstep)], writes=[("out", k0)], key="st_out", total=True)
    pg.wait_all("sp", [("out", k0) for k0 in range(0, KD, kstep)])
    for key in list(pg.dma_cnt):
        if isinstance(key, tuple) and key[0] == "dbg":
            pg.ops["sp"].append(("wait", ("dma", key), 16 * pg.dma_cnt[key]))
    return nc, pg, es


def _mixer(a):
    from types import SimpleNamespace
    v = SimpleNamespace(**a)
    pg, nc, KD, D = v.pg, v.nc, v.KD, v.D
    H, HM, ACTA, TMPA, ps, wr = v.H, v.HM, v.ACTA, v.TMPA, v.ps, v.wr
    hT, hm, gconst, pgk = v.hT, v.hm, v.gconst, v.pgk
    TILES2, TILES3 = v.TILES2, v.TILES3
    load_slab, transfer, dma_custom = v.load_slab, v.transfer, v.dma_custom
    sq, ntmp, sg, rtmp, rb, epsc = v.sq, v.ntmp, v.sg, v.rtmp, v.rb, v.epsc
    debug = v.debug
    w_in_v = v.w_in.rearrange("(k p) n -> p k n", p=128)

    def bail():
        names = set(pg.res_w) | set(pg.res_r)
        transfer(list(names), v.HALL + v.HMALL + [("act", f) for f in range(12)]
                 + [("sq", 0), ("sq", 1), "rtmp", ("ntmp", 0), ("ntmp", 1), ("sg", 0), ("sg", 1)])
        pg.dma("sp", hT[:, :, 0:NL], v.h_spill.rearrange("p (k t) -> p k t", t=NL), reads=["hspill"], writes=v.HALL,
               key="unspill")

    mstop = v.cfg.mstop

    def hmb(off, nbytes, dt=F32):
        x = HM[:, off // 2:(off + nbytes) // 2]
        return x.bitcast(F32) if dt == F32 else x

    v.norm_mod(1, TILES3)
    pg.dma("sp", v.h_spill.rearrange("p (k t) -> p k t", t=NL), hT[:, :, 0:NL], reads=v.HALL, writes=["hspill"],
           key="spill")
    o_acc = H[:, 0:8192].rearrange("p (i c) -> p i c", c=1024)
    v_tm = H[:, 8192:12800].bitcast(BF16).rearrange("p (i c) -> p i c", c=1024)
    mixg = H[:, 12800:16896].bitcast(BF16).rearrange("p (i t) -> p i t", t=1024)
    uext = H[:, 8192:8192 + 46 * 64]
    OACC = [("oacc", i) for i in range(8)]
    VTM = [("vtm", i) for i in range(9)]
    transfer(v.HALL, OACC + VTM + ["mixg"])
    kT = ACTA[:, 0:4352].rearrange("p (h t) -> p h t", t=NT)
    qT = ACTA[:, 4352:8448].rearrange("p (h t) -> p h t", t=NL)
    k_tm = ACTA[:, 8448:13056].rearrange("p (i c) -> p i c", c=512)
    mixc = ACTA[:, 0:8192].rearrange("p (i t) -> p i t", t=1024)
    KT = [("kT", h) for h in range(4)]
    QT = [("qT", h) for h in range(4)]
    KTM = [("ktm", i) for i in range(9)]
    transfer([("act", f) for f in range(12)], KT + QT + KTM)

    PADS = []

    def slab512(col0, ncols=512):
        s = load_slab([(lambda t: t[:, 0:KD * ncols].rearrange("p (k n) -> p k n", n=ncols),
                        w_in_v[:, :, col0:col0 + ncols])])
        return s, wr[s][:, 0:KD * ncols].rearrange("p (k n) -> p k n", n=ncols)

    pair = 0
    for sl in range(4):
        s = load_slab([
            (lambda t: t[:, 0:KD * 512].rearrange("p (k g n) -> p k g n", g=2, n=256)[:, :, 0, :],
             w_in_v[:, :, OFF_GA + sl * 256:OFF_GA + (sl + 1) * 256]),
            (lambda t: t[:, 0:KD * 512].rearrange("p (k g n) -> p k g n", g=2, n=256)[:, :, 1, :],
             w_in_v[:, :, OFF_GB + sl * 256:OFF_GB + (sl + 1) * 256]),
        ])
        wv = wr[s][:, 0:KD * 512].rearrange("p (k g n) -> p k g n", g=2, n=256)
        for fi in range(2):
            cc = sl * 2 + fi
            for (t0, n) in TILES2:
                pA = 2 * (pair % 2)
                pB = pA + 1
                bi = pair % 2
                pair += 1
                for g_, pb in ((0, pA), (1, pB)):
                    for k in range(KD):
                        pg.op("pe", (lambda e, wv=wv, k=k, g_=g_, fi=fi, pb=pb, t0=t0, n=n: e.matmul(
                            ps[pb][:, 0:n], lhsT=wv[:, k, g_, fi * 128:(fi + 1) * 128], rhs=hm[:, k, t0:t0 + n],
                            start=(k == 0), stop=(k == KD - 1))),
                            reads=[("wr", s), ("hm", k)], writes=[("ps", pb)], signal=(k == KD - 1))
                pg.op("act", (lambda e, pB=pB, bi=bi, n=n: e.activation(
                    out=sg[bi][:, 0:n], in_=ps[pB][:, 0:n], func=AF.Sigmoid)),
                    reads=[("ps", pB)], writes=[("sg", bi)])
                pg.op("dve", (lambda e, pA=pA, bi=bi, n=n: e.tensor_tensor(
                    out=ntmp[bi][:, 0:n], in0=sg[bi][:, 0:n], in1=ps[pA][:, 0:n], op=ALU.mult)),
                    reads=[("sg", bi), ("ps", pA)], writes=[("ntmp", bi)])
                c4 = cc % 4
                if cc < 4:
                    dst = v.u_row[c4 * 128:(c4 + 1) * 128, t0:t0 + n]
                else:
                    dst = v.ucol_in[c4 // 2][(c4 % 2) * 128:(c4 % 2 + 1) * 128, t0:t0 + n]
                pg.dma("sp", dst, ntmp[bi][:, 0:n], reads=[("ntmp", bi)],
                       writes=[("udram", cc, t0)], key=("ust", bi))
    UCOL = [("udram", cc, t0) for cc in range(4, 8) for (t0, n) in TILES2]
    UROW = [("udram", cc, t0) for cc in range(4) for (t0, n) in TILES2]
    for i_ in range(2):
        pg.custom("pool", (lambda e, i_=i_: e.collective_compute(
            "AllGather", ALU.bypass, replica_groups=[[0, 1, 2, 3], [4, 5, 6, 7]],
            ins=[v.ucol_in[i_][:, :]], outs=[v.ucol_g[i_][:, :]])), reads=UCOL, writes=[("ucolg", i_)],
            key=("cc_u", i_))


    if mstop <= 1:
        return bail()
    cnt = 0
    for sl in range(2):
        s, wv = slab512(OFF_G + sl * 512)
        for ti in range(8):
            pb = 4 + (cnt % 2)
            bi = cnt % 2
            cnt += 1
            for k in range(KD):
                pg.op("pe", (lambda e, wv=wv, k=k, pb=pb, ti=ti: e.matmul(
                    ps[pb][:, :], lhsT=hm[:, k, ti * 128:(ti + 1) * 128], rhs=wv[:, k, :],
                    start=(k == 0), stop=(k == KD - 1))),
                    reads=[("wr", s), ("hm", k)], writes=[("ps", pb)], signal=(k == KD - 1))
            pg.op("act", (lambda e, pb=pb, bi=bi: e.activation(out=sq[bi][:, :], in_=ps[pb][:, :], func=AF.Silu)),
                  reads=[("ps", pb)], writes=[("sq", bi)])
            pg.dma("sp", v.sg_spill[ti * 128:(ti + 1) * 128, sl * 512:(sl + 1) * 512], sq[bi][:, :],
                   reads=[("sq", bi)], writes=[("sgsp", ti, sl)], key=("sgst", bi))

    for sl in range(2):
        s, wv = slab512(OFF_V + sl * 512)
        for ti in range(9):
            np_ = 128 if ti < 8 else 64
            pb = 4 + (cnt % 2)
            cnt += 1
            for k in range(KD):
                pg.op("pe", (lambda e, wv=wv, k=k, pb=pb, ti=ti, np_=np_: e.matmul(
                    ps[pb][0:np_, :], lhsT=hm[:, k, ti * 128:ti * 128 + np_], rhs=wv[:, k, :],
                    start=(k == 0), stop=(k == KD - 1))),
                    reads=[("wr", s), ("hm", k)], writes=[("ps", pb)], signal=(k == KD - 1))
            pg.op("act", (lambda e, pb=pb, ti=ti, sl=sl, np_=np_: e.copy(
                out=v_tm[0:np_, ti, sl * 512:(sl + 1) * 512], in_=ps[pb][0:np_, :])),
                reads=[("ps", pb)], writes=[("vtm", ti)])

    s, wv = slab512(OFF_K)
    for ti in range(9):
        np_ = 128 if ti < 8 else 64
        pb = 4 + (cnt % 2)
        cnt += 1
        for k in range(KD):
            pg.op("pe", (lambda e, wv=wv, k=k, pb=pb, ti=ti, np_=np_: e.matmul(
                ps[pb][0:np_, :], lhsT=hm[:, k, ti * 128:ti * 128 + np_], rhs=wv[:, k, :],
                start=(k == 0), stop=(k == KD - 1))),
                reads=[("wr", s), ("hm", k)], writes=[("ps", pb)], signal=(k == KD - 1))
        pg.op("dve", (lambda e, pb=pb, ti=ti, np_=np_: e.tensor_copy(out=k_tm[0:np_, ti, :], in_=ps[pb][0:np_, :])),
              reads=[("ps", pb)], writes=[("ktm", ti)])
    for h in range(4):
        for (t0, n) in TILES3:
            pb = 4 + (cnt % 2)
            cnt += 1
            for k in range(KD):
                pg.op("pe", (lambda e, wv=wv, k=k, pb=pb, h=h, t0=t0, n=n: e.matmul(
                    ps[pb][:, 0:n], lhsT=wv[:, k, h * 128:(h + 1) * 128], rhs=hm[:, k, t0:t0 + n],
                    start=(k == 0), stop=(k == KD - 1))),
                    reads=[("wr", s), ("hm", k)], writes=[("ps", pb)], signal=(k == KD - 1))
            pg.op("act", (lambda e, pb=pb, h=h, t0=t0, n=n: e.copy(out=kT[:, h, t0:t0 + n], in_=ps[pb][:, 0:n])),
                  reads=[("ps", pb)], writes=[("kT", h)])
    s, wv = slab512(OFF_Q)
    for h in range(4):
        for (t0, n) in TILES2:
            pb = 4 + (cnt % 2)
            cnt += 1
            for k in range(KD):
                pg.op("pe", (lambda e, wv=wv, k=k, pb=pb, h=h, t0=t0, n=n: e.matmul(
                    ps[pb][:, 0:n], lhsT=wv[:, k, h * 128:(h + 1) * 128], rhs=hm[:, k, t0:t0 + n],
                    start=(k == 0), stop=(k == KD - 1))),
                    reads=[("wr", s), ("hm", k)], writes=[("ps", pb)], signal=(k == KD - 1))
            pg.op("act", (lambda e, pb=pb, h=h, t0=t0, n=n: e.mul(out=qT[:, h, t0:t0 + n], in_=ps[pb][:, 0:n],
                                                                 mul=float(128 ** -0.5))),
                  reads=[("ps", pb)], writes=[("qT", h)])
    s, wv = slab512(OFF_GKF, 32)
    for d in range(2):
        for (t0, n) in TILES3:
            pb = 4 + (cnt % 2)
            cnt += 1
            for k in range(KD):
                pg.op("pe", (lambda e, wv=wv, k=k, pb=pb, d=d, t0=t0, n=n: e.matmul(
                    ps[pb][0:16, 0:n], lhsT=wv[:, k, d * 16:(d + 1) * 16], rhs=hm[:, k, t0:t0 + n],
                    start=(k == 0), stop=(k == KD - 1))),
                    reads=[("wr", s), ("hm", k)], writes=[("ps", pb)], signal=(k == KD - 1))
            pg.op("dve", (lambda e, pb=pb, d=d, t0=t0, n=n: e.tensor_copy(out=pgk[0:16, d, t0:t0 + n],
                                                                        in_=ps[pb][0:16, 0:n])),
                  reads=[("ps", pb)], writes=["pgk"])

    if mstop <= 2:
        return bail()
    st_loc = hmb(0, 4 * EW * 4).rearrange("p (e w) -> p e w", w=EW)
    Sbf = HM[:, 8224:8224 + 2048].rearrange("p (d h e) -> p d h e", d=2, h=4)
    off = 20544
    spb = hmb(off, 2048); off += 2048
    edb = hmb(off, 2048); off += 2048
    kdb = HM[:, off // 2:off // 2 + 512]; off += 1024
    decb = hmb(off, 32); off += 32
    ebe = [hmb(off + i * 1024, 1024) for i in range(2)]; off += 2048
    e1n = [hmb(off + i * 512, 512) for i in range(2)]; off += 1024
    qks = [[HM[:, (off + (i * 3 + j) * 256) // 2:(off + (i * 3 + j) * 256) // 2 + 128] for j in range(3)]
           for i in range(2)]; off += 1536
    attT = [HM[:, (off + i * 256) // 2:(off + i * 256) // 2 + 128] for i in range(2)]; off += 512
    assert off <= 34816
    og = TMPA[:, 0:1024]
    sgt = TMPA[:, 1024:1536].bitcast(BF16)
    ebuf = TMPA[:, 1536:1536 + EW]
    t1 = TMPA[:, 2564:2820]
    ssb = TMPA[:, 2820:2828]
    GT = ["stloc", "Sbf0", "Sbf1", "spb", "edb", "kdb", "decb", ("ebe", 0), ("ebe", 1), ("e1n", 0), ("e1n", 1),
          ("qks", 0), ("qks", 1), ("attT", 0), ("attT", 1)]
    transfer(v.HMALL, GT)
    TM = ["og", "sgt", "ebuf", "t1", "ssb"]
    transfer([("sq", 0), ("sq", 1), "rtmp", ("ntmp", 0), ("ntmp", 1), ("sg", 0), ("sg", 1)], TM)

    def Sv(d):
        return st_loc[:, 2 * d + 1, 0:1024].rearrange("p (h e) -> p h e", e=256)

    def Sx(d):
        return st_loc[:, 2 * d, 0:1024].rearrange("p (h e) -> p h e", e=256)

    pg.op("dve", lambda e: e.memset(st_loc[:, :, 0:1024], 0.0), writes=["stloc"])
    pg.op("dve", lambda e: e.memset(st_loc[:, :, 1024:1028], 1.0), writes=["stloc"])

    Mtb = lambda d: gconst[:, GC_MTB + d * 128:GC_MTB + (d + 1) * 128]
    CE = lambda d: gconst[:, GC_CE + d * 256:GC_CE + (d + 1) * 256]
    CH = gconst[:, GC_CH:GC_CH + 2]
    MASK = lambda d: gconst[:, GC_MASK + d * 128:GC_MASK + (d + 1) * 128]
    IDM = gconst[:, GC_ID:GC_ID + 128]
    ONEF = gconst[:, GC_ONE:GC_ONE + 128]
    hcnt = {"n": 0}

    def gla_tile(ti, d, phase):
        np_ = 128 if ti < 8 else 64
        tok0 = ti * 128
        pg.op("pe", lambda e: e.matmul(ps[0][0:np_, :], lhsT=pgk[0:17, d, tok0:tok0 + np_], rhs=v.w2aug[0:17, d, :],
                                       start=True, stop=True), reads=["pgk", "w2aug"], writes=[("ps", 0)])
        pg.op("act", lambda e: e.activation(out=spb[0:np_, :], in_=ps[0][0:np_, :], func=AF.Exp, scale=-1.0),
              reads=[("ps", 0)], writes=["spb"])
        pg.op("act", lambda e: e.activation(out=spb[0:np_, :], in_=spb[0:np_, :], func=AF.Ln, bias=epsc[0:np_, 2:3],
                                            scale=1.0), reads=["spb", "epsc"], writes=["spb"])
        pg.op("pe", lambda e: e.matmul(ps[1][0:np_, :], lhsT=Mtb(d)[0:np_, 0:np_], rhs=spb[0:np_, :],
                                       start=True, stop=True), reads=["spb", "gconst"], writes=[("ps", 1)])
        pg.op("act", lambda e: e.activation(out=edb[0:np_, :], in_=ps[1][0:np_, :], func=AF.Exp),
              reads=[("ps", 1)], writes=["edb"])
        pg.op("dve", lambda e: e.tensor_tensor(out=kdb[0:np_, :], in0=k_tm[0:np_, ti, :], in1=edb[0:np_, :],
                                               op=ALU.mult), reads=[("ktm", ti), "edb"], writes=["kdb"])
        chunks = [0, 1] if d == 0 else [1, 0]
        if ti == 8:
            chunks = [0]
        if phase == "A":
            for h in range(4):
                pg.op("pe", (lambda e, h=h: e.matmul(ps[3][:, 2 * h:2 * h + 2], lhsT=spb[0:np_, h * 128:(h + 1) * 128],
                                                     rhs=CH[0:np_, :], start=True, stop=True)),
                      reads=["spb", "gconst"], writes=[("ps", 3)], signal=(h == 3))
            pg.op("act", lambda e: e.activation(out=decb[:, 0:8], in_=ps[3][:, 0:8], func=AF.Exp),
                  reads=[("ps", 3)], writes=["decb"])
            dview = decb[:, 0:8].rearrange("p (h x) -> p h x", x=2)
            for X in chunks:
                r0 = X * 64
                for h in range(4):
                    pb = 6 + (h // 2)
                    c0 = (h % 2) * 256
                    pg.op("pe", (lambda e, h=h, pb=pb, c0=c0, r0=r0: e.matmul(
                        ps[pb][:, c0:c0 + 256], lhsT=kdb[r0:r0 + 64, h * 128:(h + 1) * 128],
                        rhs=v_tm[r0:r0 + 64, ti, h * 256:(h + 1) * 256], start=True, stop=True)),
                        reads=["kdb", ("vtm", ti)], writes=[("ps", pb)])
                    if ti < 8:
                        pg.op("dve", (lambda e, h=h, pb=pb, c0=c0, X=X: e.scalar_tensor_tensor(
                            out=Sv(d)[:, h, :], in0=Sv(d)[:, h, :], scalar=decb[:, 2 * h + X:2 * h + X + 1],
                            in1=ps[pb][:, c0:c0 + 256], op0=ALU.mult, op1=ALU.add)),
                            reads=[("ps", pb), "decb", "stloc"], writes=["stloc"])
                    else:
                        pg.op("dve", (lambda e, h=h, pb=pb, c0=c0: e.tensor_copy(
                            out=Sx(d)[:, h, :], in_=ps[pb][:, c0:c0 + 256])),
                            reads=[("ps", pb)], writes=["stloc"])
                if ti < 8:
                    pg.op("dve", (lambda e, X=X: e.tensor_tensor(
                        out=st_loc[:, 2 * d + 1, 1024:1028], in0=st_loc[:, 2 * d + 1, 1024:1028],
                        in1=dview[:, :, X], op=ALU.mult)), reads=["decb", "stloc"], writes=["stloc"])
                else:
                    pg.op("dve", (lambda e, X=X: e.tensor_copy(out=st_loc[:, 2 * d, 1024:1028], in_=dview[:, :, X])),
                          reads=["decb"], writes=["stloc"])
            return
        S = Sv(d)
        sbn = "Sbf%d" % d
        hb = []
        for h in range(4):
            bi = hcnt["n"] % 2
            hcnt["n"] += 1
            hb.append(bi)
            pbe = 3 + (h // 2)
            cb = (h % 2) * 256
            pg.op("pe", (lambda e, h=h, pbe=pbe, cb=cb: e.matmul(
                ps[pbe][:, cb:cb + 256], lhsT=spb[:, h * 128:(h + 1) * 128], rhs=CE(d), start=True, stop=True)),
                reads=["spb", "gconst"], writes=[("ps", pbe)])
            pg.op("act", (lambda e, bi=bi, pbe=pbe, cb=cb: e.activation(
                out=ebe[bi][:, :], in_=ps[pbe][:, cb:cb + 256], func=AF.Exp)),
                reads=[("ps", pbe)], writes=[("ebe", bi)])
            pg.op("act", (lambda e, bi=bi, pbe=pbe, cb=cb: e.activation(
                out=e1n[bi][:, :], in_=ps[pbe][:, cb + 128:cb + 256], func=AF.Exp, scale=-1.0)),
                reads=[("ps", pbe)], writes=[("e1n", bi)])
            qb, qs, ks = qks[bi]
            pg.op("dve", (lambda e, h=h, bi=bi, qb=qb: e.tensor_tensor(
                out=qb[:, :], in0=qT[:, h, tok0:tok0 + 128], in1=ebe[bi][:, 0:128], op=ALU.mult)),
                reads=[("qT", h), ("ebe", bi)], writes=[("qks", bi)])
            pg.op("dve", (lambda e, h=h, bi=bi, qs=qs: e.tensor_tensor(
                out=qs[:, :], in0=qT[:, h, tok0:tok0 + 128], in1=ebe[bi][:, 128:256], op=ALU.mult)),
                reads=[("qT", h), ("ebe", bi)], writes=[("qks", bi)])
            pg.op("dve", (lambda e, h=h, bi=bi, ks=ks: e.tensor_tensor(
                out=ks[:, :], in0=kT[:, h, tok0:tok0 + 128], in1=e1n[bi][:, :], op=ALU.mult)),
                reads=[("kT", h), ("e1n", bi)], writes=[("qks", bi)])
            pg.op("pe", (lambda e, h=h, ks=ks, qs=qs: e.matmul(
                ps[5][:, h * 128:(h + 1) * 128], lhsT=ks[:, :], rhs=qs[:, :], start=True, stop=True)),
                reads=[("qks", bi)], writes=[("ps", 5)])
            pg.op("dve", (lambda e, h=h, bi=bi: e.tensor_tensor(
                out=attT[bi][:, :], in0=ps[5][:, h * 128:(h + 1) * 128], in1=MASK(d), op=ALU.mult)),
                reads=[("ps", 5), "gconst"], writes=[("attT", bi)])
            po = 6 + (h // 2)
            co = (h % 2) * 256
            pg.op("pe", (lambda e, h=h, bi=bi, po=po, co=co: e.matmul(
                ps[po][:, co:co + 256], lhsT=attT[bi][:, :], rhs=v_tm[:, ti, h * 256:(h + 1) * 256],
                start=(h % 2 == 0), stop=False, skip_group_check=True)),
                reads=[("attT", bi), ("vtm", ti)], writes=[("ps", po)])
            for xi, X in enumerate(chunks):
                r0 = X * 64
                pg.op("pe", (lambda e, h=h, qb=qb, po=po, co=co, r0=r0, xi=xi: e.matmul(
                    ps[po][r0:r0 + 64, co:co + 256], lhsT=qb[:, r0:r0 + 64], rhs=Sbf[:, d, h, :],
                    start=False, stop=(xi == 1), skip_group_check=True)),
                    reads=[("qks", bi), sbn], writes=[("ps", po)])
                kc = (h % 2) * 256
                pg.op("pe", (lambda e, h=h, r0=r0, kc=kc: e.matmul(
                    ps[2][:, kc:kc + 256], lhsT=kdb[r0:r0 + 64, h * 128:(h + 1) * 128],
                    rhs=v_tm[r0:r0 + 64, ti, h * 256:(h + 1) * 256], start=True, stop=True)),
                    reads=["kdb", ("vtm", ti)], writes=[("ps", 2)])
                col = (r0 + 63) if d == 0 else r0
                pg.op("dve", (lambda e, h=h, bi=bi, kc=kc, col=col: e.scalar_tensor_tensor(
                    out=S[:, h, :], in0=S[:, h, :], scalar=ebe[bi][:, col:col + 1], in1=ps[2][:, kc:kc + 256],
                    op0=ALU.mult, op1=ALU.add)), reads=[("ps", 2), ("ebe", bi), "stloc"], writes=["stloc"])
                pg.op("act", (lambda e, h=h: e.copy(out=Sbf[:, d, h, :], in_=S[:, h, :])),
                      reads=["stloc"], writes=[sbn])

    for d in range(2):
        order = list(range(8)) if d == 0 else list(range(7, -1, -1))
        for ti in order + [8]:
            gla_tile(ti, d, "A")
    if debug:
        dd = v.dbg_out("stloc", [128, 4 * EW])
        pg.dma("sp", dd[:, :], st_loc.rearrange("p e w -> p (e w)"), reads=["stloc"], key=("dbg", 4))
    for e_ in range(4):
        pg.dma("sp", v.st_in[e_][:, :], st_loc[:, e_, :], reads=["stloc"], writes=[("st_in", e_)], key="st_io",
               total=True)
    for e_ in range(4):
        pg.custom("pool", (lambda e, e_=e_: e.collective_compute(
            "AllGather", ALU.bypass, replica_groups=[[0, 1, 2, 3], [4, 5, 6, 7]],
            ins=[v.st_in[e_][:, :]], outs=[v.st_out[e_][:, :]])), reads=[("st_in", e_)], writes=[("st_out", e_)],
            key=("cc_st", e_))
    for d in range(2):
        S = Sv(d)
        pg.op("dve", (lambda e, S=S: e.memset(S[:, :, :], 0.0)), reads=["stloc"], writes=["stloc"])
        ctx_order = [0, 1, 2, 3] if d == 0 else [3, 2, 1, 0]
        seg_order = [0, 1, 2] if d == 0 else [3, 2, 1]
        for kind, order in ((0, ctx_order), (1, seg_order)):
            for r in order:
                e_ = 2 * d + kind
                pg.dma("sp", ebuf[:, :], v.st_out[e_][r * 128:(r + 1) * 128, :], reads=[("st_out", e_)],
                       writes=["ebuf"], key="ld_st")
                if kind == 0:
                    for h in range(4):
                        pg.op("dve", (lambda e, h=h, S=S: e.scalar_tensor_tensor(
                            out=S[:, h, :], in0=S[:, h, :], scalar=ebuf[:, 1024 + h:1025 + h],
                            in1=ebuf[:, h * 256:(h + 1) * 256], op0=ALU.mult, op1=ALU.add)),
                            reads=["ebuf", "stloc"], writes=["stloc"])
                else:
                    mcol = (0 if d == 0 else 8) + r
                    pg.op("dve", (lambda e, mcol=mcol: e.tensor_scalar(
                        out=t1[:, 0:4], in0=ebuf[:, 1024:1028], scalar1=v.segmask[:, mcol:mcol + 1],
                        scalar2=v.segmask[:, mcol + 4:mcol + 5], op0=ALU.mult, op1=ALU.add)),
                        reads=["ebuf", "segmask"], writes=["t1"])
                    pg.op("dve", (lambda e, mcol=mcol: e.tensor_scalar(
                        out=og[:, :], in0=ebuf[:, 0:1024], scalar1=v.segmask[:, mcol:mcol + 1], scalar2=None,
                        op0=ALU.mult)), reads=["ebuf", "segmask"], writes=["og"])
                    for h in range(4):
                        pg.op("dve", (lambda e, h=h, S=S: e.scalar_tensor_tensor(
                            out=S[:, h, :], in0=S[:, h, :], scalar=t1[:, h:h + 1], in1=og[:, h * 256:(h + 1) * 256],
                            op0=ALU.mult, op1=ALU.add)), reads=["t1", "og", "stloc"], writes=["stloc"])
        pg.op("act", (lambda e, S=S, d=d: e.copy(out=Sbf[:, d, :, :], in_=S[:, :, :])), reads=["stloc"],
              writes=["Sbf%d" % d])
    if debug:
        dd = v.dbg_out("s0", [128, 4 * EW])
        pg.dma("sp", dd[:, :], st_loc.rearrange("p e w -> p (e w)"), reads=["stloc"], key=("dbg", 5))

    if mstop <= 3:
        return bail()
    for ti in range(8):
        gla_tile(ti, 0, "C")
        for half in range(2):
            pg.op("act", (lambda e, ti=ti, half=half: e.copy(out=o_acc[:, ti, half * 512:(half + 1) * 512],
                                                             in_=ps[6 + half][:, :])),
                  reads=[("ps", 6 + half)], writes=[("oacc", ti)])
    for ti in range(7, -1, -1):
        gla_tile(ti, 1, "C")
        for half in range(2):
            pg.op("dve", (lambda e, ti=ti, half=half: e.tensor_tensor(
                out=o_acc[:, ti, half * 512:(half + 1) * 512], in0=o_acc[:, ti, half * 512:(half + 1) * 512],
                in1=ps[6 + half][:, :], op=ALU.add)), reads=[("ps", 6 + half), ("oacc", ti)], writes=[("oacc", ti)])
        pg.dma("sp", sgt[:, :], v.sg_spill[ti * 128:(ti + 1) * 128, :],
               reads=[("sgsp", ti, 0), ("sgsp", ti, 1)], writes=["sgt"], key="ld_sg")
        for h in range(4):
            pg.op("act", (lambda e, ti=ti, h=h: e.activation(
                out=t1[:, :], in_=o_acc[:, ti, h * 256:(h + 1) * 256], func=AF.Square,
                accum_out=ssb[:, h:h + 1])), reads=[("oacc", ti)], writes=["t1", "ssb"])
        pg.op("act", lambda e: e.activation(out=ssb[:, 4:8], in_=ssb[:, 0:4], func=AF.Sqrt, bias=epsc[:, 1:2],
                                            scale=1.0 / 256.0), reads=["ssb", "epsc"], writes=["ssb"])
        pg.op("dve", lambda e: e.reciprocal(out=ssb[:, 4:8], in_=ssb[:, 4:8]), reads=["ssb"], writes=["ssb"])
        for h in range(4):
            pg.op("dve", (lambda e, ti=ti, h=h: e.scalar_tensor_tensor(
                out=t1[:, :], in0=o_acc[:, ti, h * 256:(h + 1) * 256], scalar=ssb[:, 4 + h:5 + h], in1=v.gnb[:, :],
                op0=ALU.mult, op1=ALU.mult)), reads=[("oacc", ti), "ssb", "gnb"], writes=["t1"])
            pg.op("dve", (lambda e, h=h: e.tensor_tensor(
                out=og[:, h * 256:(h + 1) * 256], in0=t1[:, :], in1=sgt[:, h * 256:(h + 1) * 256], op=ALU.mult)),
                reads=["t1", "sgt"], writes=["og"])
        for half in range(2):
            pt = 5 if half == 0 else 2
            for b4 in range(4):
                blk = half * 4 + b4
                pg.op("pe", (lambda e, pt=pt, b4=b4, blk=blk: e.transpose(
                    out=ps[pt][:, b4 * 128:(b4 + 1) * 128], in_=og[:, blk * 128:(blk + 1) * 128], identity=IDM)),
                    reads=["og", "gconst"], writes=[("ps", pt)], signal=(b4 == 3))
            pg.op("act", (lambda e, pt=pt, half=half, ti=ti: e.copy(
                out=mixg[:, half * 4:(half + 1) * 4, ti * 128:(ti + 1) * 128],
                in_=ps[pt][:, :].rearrange("p (b t) -> p b t", t=128))), reads=[("ps", pt)], writes=["mixg"])
    if debug:
        dd = v.dbg_out("mixg", [1024, NL], BF16)
        pg.dma("sp", dd.rearrange("(i p) t -> p i t", p=128), mixg, reads=["mixg"], key=("dbg", 6))

    if mstop <= 4:
        return bail()
    y = hmb(0, 32768).rearrange("p (c t) -> p c t", t=1024)
    Y = [("y", c) for c in range(8)]
    transfer(GT, Y)
    transfer(VTM, ["uext"])
    transfer(KT + QT + KTM, ["mixc"])
    for cc in range(8):
        yv = y[:, cc, :]
        if cc < 4:
            pg.dma("sp", uext[:, 0:1024], v.u_row[cc * 128:(cc + 1) * 128, :], reads=UROW, writes=["uext"],
                   key="ld_u")
            ctr = uext[:, 0:1024]
        else:
            c4 = cc - 4

            def mk(which, c4=c4):
                UG = v.ucol_g[c4 // 2]
                ro = (c4 % 2) * 128

                def f(eng):
                    pid = v.get_pid(eng)
                    myr = pid % 4
                    if which == 0:
                        return eng.dma_start(out=uext[:, 0:960],
                                             in_=UG[bass.ds(((myr + 3) % 4) * 256 + ro, 128), 64:1024])
                    if which == 1:
                        return eng.dma_start(out=uext[:, 960:1984],
                                             in_=UG[bass.ds(myr * 256 + ro, 128), 0:1024])
                    return eng.dma_start(out=uext[:, 1984:2944],
                                         in_=UG[bass.ds(((myr + 1) % 4) * 256 + ro, 128), 0:960])
                return f
            for which in range(3):
                dma_custom("sp", mk(which), [("ucolg", c4 // 2)], ["uext"], "ld_uc")
            pg.op("dve", lambda e: e.tensor_scalar(out=uext[:, 0:960], in0=uext[:, 0:960], scalar1=v.segmask[:, 0:1],
                                                   scalar2=None, op0=ALU.mult), reads=["uext", "segmask"],
                  writes=["uext"])
            pg.op("dve", lambda e: e.tensor_scalar(out=uext[:, 1984:2944], in0=uext[:, 1984:2944],
                                                   scalar1=v.segmask[:, 11:12], scalar2=None, op0=ALU.mult),
                  reads=["uext", "segmask"], writes=["uext"])
            ctr = uext[:, 960:1984]
        pg.op("dve", (lambda e, cc=cc, ctr=ctr, yv=yv: e.tensor_scalar(
            out=yv, in0=ctr, scalar1=v.convw[:, cc, 15:16], scalar2=v.convp[:, cc, 0:1], op0=ALU.mult, op1=ALU.add)),
            reads=["uext", "convw", "convp"], writes=[("y", cc)])
        for j in range(31):
            if j == 15:
                continue
            s_ = j - 15
            if cc < 4:
                c0, c1 = max(0, -s_), min(64, 64 - s_)
                y3 = yv.rearrange("p (r c) -> p r c", c=64)[:, :, c0:c1]
                u3 = uext[:, 0:1024].rearrange("p (r c) -> p r c", c=64)[:, :, c0 + s_:c1 + s_]
            else:
                y3 = yv
                u3 = uext[:, j * 64:j * 64 + 1024]
            pg.op("dve", (lambda e, cc=cc, j=j, y3=y3, u3=u3: e.scalar_tensor_tensor(
                out=y3, in0=u3, scalar=v.convw[:, cc, j:j + 1], in1=y3, op0=ALU.mult, op1=ALU.add)),
                reads=["uext", "convw", ("y", cc)], writes=[("y", cc)])
    if debug:
        dd = v.dbg_out("y", [1024, NL])
        pg.dma("sp", dd.rearrange("(i p) t -> p i t", p=128), y, reads=Y, key=("dbg", 7))
    if mstop <= 5:
        return bail()
    transfer(TM, [("sq", 0), ("sq", 1), "rtmp", ("ntmp", 0), ("ntmp", 1), ("sg", 0), ("sg", 1)])
    cnt = 0
    for (t0, n) in TILES2:
        for cc in range(8):
            pg.op("pe", (lambda e, cc=cc, t0=t0, n=n: e.matmul(ps[0][:, 0:n], lhsT=ONEF, rhs=y[:, cc, t0:t0 + n],
                                                             start=(cc == 0), stop=(cc == 7))),
                  reads=[("y", cc), "gconst"], writes=[("ps", 0)], signal=(cc == 7))
        for cc in range(8):
            pg.op("dve", (lambda e, cc=cc, t0=t0, n=n: e.tensor_tensor(
                out=y[:, cc, t0:t0 + n], in0=y[:, cc, t0:t0 + n], in1=ps[0][:, 0:n], op=ALU.subtract)),
                reads=[("ps", 0), ("y", cc)], writes=[("y", cc)])
        for cc in range(8):
            bi = cnt % 2
            cnt += 1
            pg.op("act", (lambda e, cc=cc, bi=bi, t0=t0, n=n: e.activation(
                out=ntmp[bi][:, 0:n], in_=y[:, cc, t0:t0 + n], func=AF.Square)),
                reads=[("y", cc)], writes=[("ntmp", bi)])
            pg.op("pe", (lambda e, cc=cc, bi=bi, n=n: e.matmul(ps[1][:, 0:n], lhsT=ONEF, rhs=ntmp[bi][:, 0:n],
                                                             start=(cc == 0), stop=(cc == 7))),
                  reads=[("ntmp", bi), "gconst"], writes=[("ps", 1)], signal=True)
        pg.op("act", (lambda e, n=n: e.activation(out=rtmp[:, 0:n], in_=ps[1][:, 0:n], func=AF.Sqrt,
                                                  bias=epsc[:, 1:2], scale=1.0)),
              reads=[("ps", 1), "epsc"], writes=["rtmp"])
        pg.op("dve", (lambda e, t0=t0, n=n: e.reciprocal(out=rb[:, t0:t0 + n], in_=rtmp[:, 0:n])),
              reads=["rtmp"], writes=[("rb", t0)])
        for cc in range(8):
            bi = cnt % 2
            cnt += 1
            pg.op("dve", (lambda e, cc=cc, bi=bi, t0=t0, n=n: e.tensor_tensor(
                out=ntmp[bi][:, 0:n], in0=y[:, cc, t0:t0 + n], in1=rb[:, t0:t0 + n], op=ALU.mult)),
                reads=[("y", cc), ("rb", t0)], writes=[("ntmp", bi)])
            pg.op("act", (lambda e, cc=cc, bi=bi, t0=t0, n=n: e.activation(
                out=mixc[:, cc, t0:t0 + n], in_=ntmp[bi][:, 0:n], func=AF.Silu,
                bias=v.convp[:, cc, 2:3], scale=v.convp[:, cc, 1:2])),
                reads=[("ntmp", bi), "convp"], writes=["mixc"])
    if debug:
        dd = v.dbg_out("mixc", [1024, NL], BF16)
        pg.dma("sp", dd.rearrange("(i p) t -> p i t", p=128), mixc, reads=["mixc"], key=("dbg", 8))

    mixg2 = HM[:, 0:8192].rearrange("p (i t) -> p i t", t=1024)
    transfer(Y, ["mixg2"])
    pg.op("act", lambda e: e.copy(out=mixg2[:, :, :], in_=mixg[:, :, :]), reads=["mixg"], writes=["mixg2"])
    transfer(OACC + ["uext", "mixg"], v.HALL)
    pg.dma("sp", hT[:, :, 0:NL], v.h_spill.rearrange("p (k t) -> p k t", t=NL), reads=["hspill"], writes=v.HALL,
           key="unspill")

    wov = v.w_out.rearrange("(i p) n -> p i n", p=128)
    oc = 0
    for ds_ in range(D // 512):
        s = load_slab([(lambda t: t[:, 0:8192].rearrange("p (i n) -> p i n", n=512),
                        wov[:, :, ds_ * 512:(ds_ + 1) * 512])])
        wv = wr[s][:, 0:8192].rearrange("p (i n) -> p i n", n=512)
        for dc in range(4):
            dg = ds_ * 4 + dc
            for (t0, n) in TILES2:
                pb = 4 + (oc % 2)
                oc += 1
                for i in range(16):
                    src = mixg2 if i < 8 else mixc
                    rn = "mixg2" if i < 8 else "mixc"
                    pg.op("pe", (lambda e, wv=wv, i=i, dc=dc, pb=pb, t0=t0, n=n, src=src: e.matmul(
                        ps[pb][:, 0:n], lhsT=wv[:, i, dc * 128:(dc + 1) * 128], rhs=src[:, i % 8, t0:t0 + n],
                        start=(i == 0), stop=(i == 15))),
                        reads=[("wr", s), rn], writes=[("ps", pb)], signal=(i == 15))
                pg.op("dve", (lambda e, pb=pb, dg=dg, t0=t0, n=n: e.scalar_tensor_tensor(
                    out=hT[:, dg, t0:t0 + n], in0=ps[pb][:, 0:n], scalar=v.coef[:, 0, 1, 2, dg:dg + 1],
                    in1=hT[:, dg, t0:t0 + n], op0=ALU.mult, op1=ALU.add)),
                    reads=[("ps", pb), "coef", ("h", dg)], writes=[("h", dg)])
    transfer(["mixg2"], v.HMALL)
    transfer(["mixc"], [("act", f) for f in range(12)])
    if debug:
        d2 = v.dbg_out("h2", [D, NL])
        pg.dma("sp", d2.rearrange("(k p) t -> p k t", p=128), hT[:, :, 0:NL], reads=v.HALL, key=("dbg", 9))


def _emit(pg, es):
    nc = pg.nc
    sems = {}
    for e in pg.ENGS:
        sems[e] = es.enter_context(nc.semaphore("s_" + e))
    for i, key in enumerate(sorted(pg.dma_cnt, key=str)):
        sems[("dma", key)] = es.enter_context(nc.semaphore("d%d" % i))
    for e in pg.ENGS:
        assert pg.max_wait.get(e, 0) <= pg.count[e], (e, pg.max_wait.get(e), pg.count[e])
    block = es.enter_context(nc.Block())
    deco = {"pe": block.tensor, "act": block.scalar, "dve": block.vector, "pool": block.gpsimd, "sp": block.sync}

    def make(e):
        def body(eng):
            for item in pg.ops[e]:
                kind = item[0]
                if kind == "wait":
                    _, src, val = item
                    if isinstance(src, tuple) and src[1] in pg.total_keys:
                        val = 16 * pg.dma_cnt[src[1]]
                    eng.wait_ge(sems[src], val)
                elif kind == "op":
                    _, fn, signal = item
                    ins = fn(eng)
                    if signal:
                        ins.then_inc(sems[e], 1)
                elif kind == "dma":
                    _, out, in_, key, kw = item
                    eng.dma_start(out=out, in_=in_, **kw).then_inc(sems[("dma", key)], 16)
                elif kind == "custom":
                    _, fn, key = item
                    fn(eng).then_inc(sems[("dma", key)], 1)
                elif kind == "custom16":
                    _, fn, key = item
                    fn(eng).then_inc(sems[("dma", key)], 16)
        return body

    for e in pg.ENGS:
        deco[e](make(e))


def _fm(v, kd):
    return np.ascontiguousarray(np.asarray(v, np.float32).reshape(kd, 128).T)


def prepare_inputs(inputs, cfg):
    KD, D, CPC = cfg.KD, cfg.D, cfg.CPC
    f32 = lambda a: np.asarray(a, np.float32)
    x, ctx, c, c_ctx = f32(inputs["x"]), f32(inputs["ctx"]), f32(inputs["c"]), f32(inputs["c_ctx"])
    w_mod, b_mod = f32(inputs["w_mod"])[0], f32(inputs["b_mod"])[0]
    cTs = [np.ascontiguousarray(np.stack([_fm(c[b_], KD), _fm(c_ctx, KD)], axis=-1)) for b_ in range(2)]
    gains = np.stack([_fm(inputs["norm_ffn1"][0], KD), _fm(inputs["norm_mix"][0], KD),
                      _fm(inputs["norm_ffn2"][0], KD), _fm(inputs["norm_final"], KD)], axis=1)
    w_gk2, b_gk2 = f32(inputs["w_gk2"])[0], f32(inputs["b_gk2"])[0]
    w2aug = np.concatenate([w_gk2.transpose(1, 0, 2), b_gk2[None]], axis=0)
    conv_w = f32(inputs["conv_w"])[0]
    convw = np.ascontiguousarray(conv_w.T.reshape(8, 128, 31).transpose(1, 0, 2))
    convp = np.stack([f32(inputs["conv_b"])[0].reshape(8, 128).T, f32(inputs["conv_ln_g"])[0].reshape(8, 128).T,
                      f32(inputs["conv_ln_b"])[0].reshape(8, 128).T], axis=-1)
    shared = {
        "gains": np.ascontiguousarray(gains),
        "w_ffn1_in": f32(inputs["w_ffn1_in"])[0], "w_ffn1_out": f32(inputs["w_ffn1_out"])[0],
        "w_ffn2_in": f32(inputs["w_ffn2_in"])[0], "w_ffn2_out": f32(inputs["w_ffn2_out"])[0],
        "w_in": f32(inputs["w_in"])[0], "w_out": f32(inputs["w_out"])[0],
        "gconst": gla_consts(), "w2aug": np.ascontiguousarray(w2aug),
        "gnb": np.ascontiguousarray(np.broadcast_to(f32(inputs["gla_norm"])[0][None, :], (128, 256))),
        "convw": convw, "convp": np.ascontiguousarray(convp),
    }
    in_maps = []
    for core in range(8):
        b, j = core // 4, core % 4
        xt = np.concatenate([x[b, j * NL:(j + 1) * NL], ctx[b, j * NC_:(j + 1) * NC_]], axis=0)
        m = dict(shared)
        m["xT"] = np.ascontiguousarray(xt.T)
        m["cT"] = cTs[b]
        m["wmod"] = np.ascontiguousarray(w_mod[:, j * CPC * 128:(j + 1) * CPC * 128])
        m["bmod"] = np.ascontiguousarray(b_mod[j * CPC * 128:(j + 1) * CPC * 128].reshape(CPC, 128).T)
        sm = np.zeros((128, 16), np.float32)
        for r in range(4):
            sm[:, r] = 1.0 if r < j else 0.0
            sm[:, 8 + r] = 1.0 if r > j else 0.0
        sm[:, 4:8] = 1.0 - sm[:, 0:4]
        sm[:, 12:16] = 1.0 - sm[:, 8:12]
        m["segmask"] = sm
        in_maps.append(m)
    return in_maps


def build(cfg=None, stage=99, debug=False):
    cfg = cfg or Cfg()
    nc, pg, es = build_program(cfg, stage=stage, debug=debug)
    with es:
        _emit(pg, es)
    return nc, pg


def run(inputs, cfg=None, stage=99, debug=False, trace=False):
    cfg = cfg or Cfg()
    nc, pg = build(cfg, stage=stage, debug=debug)
    in_maps = prepare_inputs(inputs, cfg)
    if stage < 2:
        for m in in_maps:
            pass
    return run_bass_kernel_spmd(nc, in_maps, core_ids=list(range(8)), trace=trace)


def kernel(**inputs):
    cfg = Cfg()
    res = run(inputs, cfg)
    out = np.empty((2, 4096, cfg.D), np.float32)
    for core in range(8):
        b, j = core // 4, core % 4
        out[b, j * NL:(j + 1) * NL, :] = res.results[core]["outT"].T
    return out
```

```python
import numpy as np
import concourse.bass as bass
import concourse.mybir as mybir
from concourse.bass_utils import run_bass_kernel_spmd
from contextlib import ExitStack

F32 = mybir.dt.float32
BF16 = mybir.dt.bfloat16
AF = mybir.ActivationFunctionType
ALU = mybir.AluOpType

D = 2048
KD = 16
NL = 1024
NC_ = 64
NT = NL + NC_
DFF = 5632
NFC = 44
RMS_EPS = 1e-6

import os
SAME_ENGINE_SYNC = os.environ.get("KERNEL_SES", "1") == "1"


class Prog:
    ENGS = ("pe", "act", "dve", "pool", "sp")

    def __init__(self, nc):
        self.nc = nc
        self.ops = {e: [] for e in self.ENGS}
        self.count = {e: 0 for e in self.ENGS}
        self.waited = {e: {} for e in self.ENGS}
        self.res_w = {}
        self.res_r = {}
        self.dma_cnt = {}
        self.total_keys = set()
        self.max_wait = {}

    def _deps(self, eng, reads, writes):
        deps = {}

        def add(src, val):
            if deps.get(src, 0) < val:
                deps[src] = val

        for r in reads:
            w = self.res_w.get(r)
            if w is not None:
                add(*w)
        for w_ in writes:
            w = self.res_w.get(w_)
            if w is not None:
                add(*w)
            for src, val in self.res_r.get(w_, {}).items():
                add(src, val)
        out = []
        for src, val in deps.items():
            if src == eng:
                if not SAME_ENGINE_SYNC or eng == "pe":
                    continue
                if val > self.count[eng]:
                    continue
            if self.waited[eng].get(src, 0) >= val:
                continue
            self.waited[eng][src] = val
            out.append((src, val))
        return out

    def op(self, eng, fn, reads=(), writes=(), signal=True):
        for src, val in self._deps(eng, reads, writes):
            self.ops[eng].append(("wait", src, val))
            if self.max_wait.get(src, 0) < val:
                self.max_wait[src] = val
        if signal:
            self.count[eng] += 1
            seq = self.count[eng]
        else:
            seq = self.count[eng] + 1
        self.ops[eng].append(("op", fn, signal))
        for r in reads:
            self.res_r.setdefault(r, {})[eng] = seq
        for w in writes:
            self.res_w[w] = (eng, seq)
            self.res_r[w] = {}
        return seq

    def dma(self, q, out, in_, reads=(), writes=(), key=None, total=False, **kw):
        assert key is not None
        for src, val in self._deps(q, reads, writes):
            self.ops[q].append(("wait", src, val))
            if self.max_wait.get(src, 0) < val:
                self.max_wait[src] = val
        self.dma_cnt[key] = self.dma_cnt.get(key, 0) + 1
        val = 16 * self.dma_cnt[key]
        if total:
            self.total_keys.add(key)
        src = ("dma", key)
        self.ops[q].append(("dma", out, in_, key, kw))
        for r in reads:
            self.res_r.setdefault(r, {})[src] = val
        for w in writes:
            self.res_w[w] = (src, val)
            self.res_r[w] = {}

    def custom(self, q, fn, reads=(), writes=(), key=None):
        for src, val in self._deps(q, reads, writes):
            self.ops[q].append(("wait", src, val))
        self.dma_cnt[key] = self.dma_cnt.get(key, 0) + 1
        val = self.dma_cnt[key]
        src = ("dma", key)
        self.ops[q].append(("custom", fn, key))
        for r in reads:
            self.res_r.setdefault(r, {})[src] = val
        for w in writes:
            self.res_w[w] = (src, val)
            self.res_r[w] = {}

    def wait_all(self, eng, resources):
        for src, val in self._deps(eng, resources, ()):
            self.ops[eng].append(("wait", src, val))


class Cfg:
    def __init__(self, D=2048, DFF=5632, mstop=99):
        self.mstop = mstop
        self.D = D
        self.KD = D // 128
        self.DFF = DFF
        self.NFC = DFF // 128
        self.CPC = 9 * self.KD // 4
        assert 9 * self.KD % 4 == 0 and self.NFC % 2 == 0 and D % 512 == 0


OFF_K, OFF_V, OFF_GKF, OFF_GKB, OFF_Q, OFF_G, OFF_GA, OFF_GB = 0, 512, 1536, 1552, 1568, 2080, 3104, 4128
D_IN = 5152
EW = 1028
GC_MTB = 0
GC_CE = 256
GC_CH = 768
GC_MASK = 776
GC_ID = 1032
GC_ONE = 1160
GC_N = 1288


def gla_consts():
    g = np.zeros((128, GC_N), np.float32)
    s = np.arange(128)[:, None]
    t = np.arange(128)[None, :]
    same = (s // 64) == (t // 64)
    sc = -1.0 / 16.0
    g[:, GC_MTB:GC_MTB + 128] = sc * (same & (s > t))
    g[:, GC_MTB + 128:GC_MTB + 256] = sc * (same & (s < t))
    cum_f = sc * (same & (s <= t))
    cum_b = sc * (same & (s >= t))
    mid_f = sc * (same & ((s % 64) <= 31))
    mid_b = sc * (same & ((s % 64) >= 32))
    g[:, GC_CE:GC_CE + 128] = cum_f
    g[:, GC_CE + 128:GC_CE + 256] = cum_f - mid_f
    g[:, GC_CE + 256:GC_CE + 384] = cum_b
    g[:, GC_CE + 384:GC_CE + 512] = cum_b - mid_b
    g[:, GC_CH] = sc * (np.arange(128) < 64)
    g[:, GC_CH + 1] = sc * (np.arange(128) >= 64)
    g[:, GC_MASK:GC_MASK + 128] = (same & (t >= s))
    g[:, GC_MASK + 128:GC_MASK + 256] = (same & (t <= s))
    g[:, GC_ID:GC_ID + 128] = np.eye(128)
    g[:, GC_ONE:GC_ONE + 128] = 1.0 / 1024.0
    return g


def build_program(cfg=None, stage=99, debug=False):
    cfg = cfg or Cfg()
    D, KD, DFF, NFC, CPC = cfg.D, cfg.KD, cfg.DFF, cfg.NFC, cfg.CPC
    nc = bass.Bass("TRN2", target_bir_lowering=False)
    es = ExitStack()
    pg = Prog(nc)

    def dram_in(name, shape, dt=F32):
        return nc.dram_tensor(name, list(shape), dt, kind="ExternalInput").ap()

    def dram_tmp(name, shape, dt=F32):
        return nc.dram_tensor(name, list(shape), dt, kind="Internal").ap()

    xT = dram_in("xT", [D, NT])
    cT = dram_in("cT", [128, KD, 2])
    wmod = dram_in("wmod", [D, CPC * 128])
    bmod = dram_in("bmod", [128, CPC])
    gains = dram_in("gains", [128, 4, KD])
    w1i = dram_in("w_ffn1_in", [D, 2 * DFF])
    w1o = dram_in("w_ffn1_out", [DFF, D])
    w2i = dram_in("w_ffn2_in", [D, 2 * DFF])
    w2o = dram_in("w_ffn2_out", [DFF, D])
    w_in = dram_in("w_in", [D, D_IN])
    w_out = dram_in("w_out", [2048, D])
    gconst_d = dram_in("gconst", [128, GC_N])
    w2aug_d = dram_in("w2aug", [17, 2, 512])
    gnb_d = dram_in("gnb", [128, 256])
    convw_d = dram_in("convw", [128, 8, 31])
    convp_d = dram_in("convp", [128, 8, 3])
    segmask_d = dram_in("segmask", [128, 16])
    outT = nc.dram_tensor("outT", [D, NL], F32, kind="ExternalOutput").ap()
    mod_in = dram_tmp("mod_in", [128, 2 * CPC])
    mod_out = dram_tmp("mod_out", [4 * 128, 2 * CPC])
    h_spill = dram_tmp("h_spill", [128, KD * NL])
    sg_spill = dram_tmp("sg_spill", [NL, 1024], BF16)
    u_row = dram_tmp("u_row", [512, NL])
    ucol_in = [dram_tmp("ucol_in%d" % i, [256, NL]) for i in range(2)]
    ucol_g = [dram_tmp("ucol_g%d" % i, [4 * 256, NL]) for i in range(2)]
    st_in = [dram_tmp("st_in%d" % i, [128, EW]) for i in range(4)]
    st_out = [dram_tmp("st_out%d" % i, [4 * 128, EW]) for i in range(4)]
    dbg = {}

    def dbg_out(name, shape, dt=F32):
        dbg[name] = nc.dram_tensor("dbg_" + name, list(shape), dt, kind="ExternalOutput").ap()
        return dbg[name]

    def sb(name, shape, dt):
        return es.enter_context(nc.sbuf_tensor(name, list(shape), dt))

    HN = max(KD * NT, 17408)
    H = sb("H", [128, HN], F32)
    hT = H[:, 0:KD * NT].rearrange("p (k t) -> p k t", t=NT)
    HMN = max(KD * NT, 17408)
    HM = sb("HM", [128, HMN], BF16)
    hm = HM[:, 0:KD * NT].rearrange("p (k t) -> p k t", t=NT)
    wr = [sb("wr%d" % i, [128, 8192], BF16) for i in range(2)]
    NRING = 2
    ACTA = sb("ACTA", [128, 12 * NT], BF16)
    act = ACTA[:, :].rearrange("p (f t) -> p f t", t=NT)
    TMPA = sb("TMPA", [128, 3072], F32)
    ones_bf = sb("ones_bf", [128, 128], BF16)
    cs = sb("cs", [128, KD, 2], F32)
    cs_bf = sb("cs_bf", [128, KD, 2], BF16)
    bmod_sb = sb("bmod_sb", [128, CPC], F32)
    modloc = sb("modloc", [128, 2, CPC], F32)
    modL = sb("modL", [128, 9 * KD], F32)
    modC = sb("modC", [128, 9 * KD], F32)
    gains_sb = sb("gains_sb", [128, 4, KD], F32)
    coef = sb("coef", [128, 2, 3, 3, KD], F32)
    rb = sb("rb", [128, NT], F32)
    epsc = sb("epsc", [128, 4], F32)
    gconst = sb("gconst_sb", [128, GC_N], F32)
    w2aug = sb("w2aug_sb", [17, 2, 512], F32)
    gnb = sb("gnb_sb", [128, 256], F32)
    convw = sb("convw_sb", [128, 8, 31], F32)
    convp = sb("convp_sb", [128, 8, 3], F32)
    segmask = sb("segmask_sb", [128, 16], F32)
    pgk = sb("pgk", [32, 2, NT], F32)
    ps = [es.enter_context(nc.psum_tensor("ps%d" % i, [128, 512], F32)) for i in range(8)]

    sq = [TMPA[:, 0:256].bitcast(BF16), TMPA[:, 256:512].bitcast(BF16)]
    rtmp = TMPA[:, 512:1024]
    ntmp = [TMPA[:, 1024:1536], TMPA[:, 1536:2048]]
    sg = [TMPA[:, 2048:2560], TMPA[:, 2560:3072]]

    TILES3 = [(0, 512), (512, 512), (1024, 64)]
    TILES2 = [(0, 512), (512, 512)]
    HALL = [("h", k) for k in range(KD)]
    HMALL = [("hm", k) for k in range(KD)]

    def transfer(old, new):
        u = {}
        for o in old:
            w = pg.res_w.get(o)
            if w is not None:
                u[w[0]] = max(u.get(w[0], 0), w[1])
            for src, val in pg.res_r.get(o, {}).items():
                u[src] = max(u.get(src, 0), val)
        for n in new:
            pg.res_w.pop(n, None)
            pg.res_r[n] = dict(u)

    def dma_custom(q, fn, reads, writes, key):
        for src, val in pg._deps(q, reads, writes):
            pg.ops[q].append(("wait", src, val))
        pg.dma_cnt[key] = pg.dma_cnt.get(key, 0) + 1
        val = 16 * pg.dma_cnt[key]
        src = ("dma", key)
        pg.ops[q].append(("custom16", fn, key))
        for r in reads:
            pg.res_r.setdefault(r, {})[src] = val
        for w in writes:
            pg.res_w[w] = (src, val)
            pg.res_r[w] = {}

    xTv = xT.rearrange("(k p) t -> p k t", p=128)
    kstep = max(1, KD // 4)
    for k0 in range(0, KD, kstep):
        pg.dma("sp", hT[:, k0:k0 + kstep, :], xTv[:, k0:k0 + kstep, :],
               writes=[("h", k) for k in range(k0, k0 + kstep)], key="ld_x", total=True)
    for dst, src, name in ((cs[:], cT[:, :, :], "cs"), (bmod_sb[:], bmod[:, :], "bmod"),
                           (gains_sb[:], gains[:, :, :], "gains"), (gconst[:], gconst_d[:, :], "gconst"),
                           (w2aug[:], w2aug_d[:, :, :], "w2aug"), (gnb[:], gnb_d[:, :], "gnb"),
                           (convw[:], convw_d[:, :, :], "convw"), (convp[:], convp_d[:, :, :], "convp"),
                           (segmask[:], segmask_d[:, :], "segmask")):
        pg.dma("sp", dst, src, writes=[name], key="const", total=True)
    pg.op("dve", lambda e: e.memset(ones_bf[:], 1.0 / D), writes=["ones"])
    pg.op("dve", lambda e: e.memset(epsc[:, 0:1], RMS_EPS), writes=["epsc"])
    pg.op("dve", lambda e: e.memset(epsc[:, 1:2], 1e-5), writes=["epsc"])
    pg.op("dve", lambda e: e.memset(epsc[:, 2:3], 1.0), writes=["epsc"])
    pg.op("dve", lambda e: e.memset(pgk[:], 1.0), writes=["pgk"])

    ring = {"n": 0}

    def load_slab(pieces):
        s = ring["n"] % NRING
        ring["n"] += 1
        for vf, src in pieces:
            pg.dma("pool", vf(wr[s]), src, writes=[("wr", s)], key=("wr", s))
        return s

    pg.op("act", lambda e: e.activation(out=cs_bf[:], in_=cs[:], func=AF.Silu), reads=["cs"], writes=["cs_bf"])
    wmv = wmod.rearrange("(k p) n -> p k n", p=128)
    MC = 4 if CPC % 4 == 0 else 2
    for sl in range(CPC // MC):
        s = load_slab([(lambda t: t[:, 0:KD * 128 * MC].rearrange("p (k n) -> p k n", n=128 * MC),
                        wmv[:, :, sl * 128 * MC:(sl + 1) * 128 * MC])])
        wv = wr[s][:, 0:KD * 128 * MC].rearrange("p (k n) -> p k n", n=128 * MC)
        for gi in range(MC):
            gl = sl * MC + gi
            for k in range(KD):
                pg.op("pe", (lambda e, wv=wv, k=k, gl=gl, gi=gi: e.matmul(
                    ps[7][:, gl * 2:gl * 2 + 2], lhsT=wv[:, k, gi * 128:(gi + 1) * 128], rhs=cs_bf[:, k, :],
                    start=(k == 0), stop=(k == KD - 1))),
                    reads=[("wr", s), "cs_bf"], writes=[("ps", 7)], signal=(k == KD - 1))
    pg.op("dve", lambda e: e.tensor_tensor(
        out=modloc[:].rearrange("p v g -> p g v"),
        in0=ps[7][:, 0:2 * CPC].rearrange("p (g v) -> p g v", v=2),
        in1=bmod_sb[:].unsqueeze(2).to_broadcast([128, CPC, 2]), op=ALU.add),
        reads=[("ps", 7), "bmod"], writes=["modloc"])
    pg.dma("sp", mod_in[:, :], modloc[:].rearrange("p v g -> p (v g)"), reads=["modloc"], writes=["mod_in"],
           key="mod_io")
    pg.custom("pool", lambda e: e.collective_compute(
        "AllGather", ALU.bypass, replica_groups=[[0, 1, 2, 3], [4, 5, 6, 7]],
        ins=[mod_in[:, :]], outs=[mod_out[:, :]]), reads=["mod_in"], writes=["mod_out"], key="cc_mod")
    mov = mod_out.rearrange("(r p) n -> p r n", p=128)

    pidc = {}

    def get_pid(eng):
        if "pid" not in pidc:
            pidc["pid"] = eng.partition_id()
        return pidc["pid"]

    pg.dma("sp", modL[:].rearrange("p (r g) -> p r g", g=CPC), mov[:, :, 0:CPC], reads=["mod_out"],
           writes=["modL"], key="ld_mod", total=True)
    pg.dma("sp", modC[:].rearrange("p (r g) -> p r g", g=CPC), mov[:, :, CPC:2 * CPC], reads=["mod_out"],
           writes=["modC"], key="ld_mod", total=True)
    if debug:
        dm = dbg_out("mod", [128, 2, 9 * KD])
        pg.dma("sp", dm[:, 0, :], modL[:], reads=["modL"], key=("dbg", 1))
        pg.dma("sp", dm[:, 1, :], modC[:], reads=["modC"], key=("dbg", 2))

    for wi, m in enumerate((modL, modC)):
        mv = m[:].rearrange("p (m k) -> p m k", k=KD)
        for s_ in range(3):
            fac = 1.0 if s_ == 1 else 0.5
            pg.op("dve", (lambda e, wi=wi, mv=mv, s_=s_: e.scalar_tensor_tensor(
                out=coef[:, wi, s_, 0, :], in0=mv[:, 3 * s_ + 1, :], scalar=1.0, in1=gains_sb[:, s_, :],
                op0=ALU.add, op1=ALU.mult)), reads=["modL", "modC", "gains"], writes=["coef"])
            pg.op("dve", (lambda e, wi=wi, mv=mv, s_=s_: e.tensor_copy(
                out=coef[:, wi, s_, 1, :], in_=mv[:, 3 * s_, :])), reads=["modL", "modC"], writes=["coef"])
            pg.op("dve", (lambda e, wi=wi, mv=mv, s_=s_, fac=fac: e.tensor_scalar(
                out=coef[:, wi, s_, 2, :], in0=mv[:, 3 * s_ + 2, :], scalar1=fac, scalar2=None, op0=ALU.mult)),
                reads=["modL", "modC"], writes=["coef"])

    def rms_stats(tiles):
        cnt = 0
        for (t0, n) in tiles:
            for k in range(KD):
                si = cnt % 2
                sqb = sq[si]
                cnt += 1
                pg.op("act", (lambda e, sqb=sqb, k=k, t0=t0, n=n: e.activation(
                    out=sqb[:, 0:n], in_=hT[:, k, t0:t0 + n], func=AF.Square)),
                    reads=[("h", k)], writes=[("sq", si)])
                pg.op("pe", (lambda e, sqb=sqb, k=k, n=n: e.matmul(
                    ps[6][:, 0:n], lhsT=ones_bf[:, :], rhs=sqb[:, 0:n], start=(k == 0), stop=(k == KD - 1))),
                    reads=[("sq", si), "ones"], writes=[("ps", 6)], signal=True)
            pg.op("act", (lambda e, n=n: e.activation(out=rtmp[:, 0:n], in_=ps[6][:, 0:n], func=AF.Sqrt,
                                                      bias=epsc[:, 0:1], scale=1.0)),
                  reads=[("ps", 6), "epsc"], writes=["rtmp"])
            pg.op("dve", (lambda e, t0=t0, n=n: e.reciprocal(out=rb[:, t0:t0 + n], in_=rtmp[:, 0:n])),
                  reads=["rtmp"], writes=[("rb", t0)])

    def norm_mod(s_, tiles):
        rms_stats(tiles)
        cnt = 0
        for (t0, n) in tiles:
            wi = 1 if t0 >= NL else 0
            for k in range(KD):
                ti = cnt % 2
                tb = ntmp[ti]
                cnt += 1
                pg.op("dve", (lambda e, tb=tb, k=k, t0=t0, n=n: e.tensor_tensor(
                    out=tb[:, 0:n], in0=hT[:, k, t0:t0 + n], in1=rb[:, t0:t0 + n], op=ALU.mult)),
                    reads=[("h", k), ("rb", t0)], writes=[("ntmp", ti)])
                pg.op("act", (lambda e, tb=tb, k=k, t0=t0, n=n, wi=wi: e.activation(
                    out=hm[:, k, t0:t0 + n], in_=tb[:, 0:n], func=AF.Identity,
                    bias=coef[:, wi, s_, 1, k:k + 1], scale=coef[:, wi, s_, 0, k:k + 1])),
                    reads=[("ntmp", ti), "coef"], writes=[("hm", k)])

    def ffn(w_i, w_o, s_, tiles):
        wiv = w_i.rearrange("(k p) n -> p k n", p=128)
        wov = w_o.rearrange("(f p) n -> p f n", p=128)
        nsl_tot = NFC // 2
        ngr = min(4, nsl_tot)
        groups = []
        a = 0
        for gi in range(ngr):
            n_ = nsl_tot // ngr + (1 if gi < nsl_tot % ngr else 0)
            groups.append((a, n_))
            a += n_
        assert max(n_ for _, n_ in groups) * 2 <= 12
        pair = 0
        oc = 0
        for (sl0, nsl) in groups:
            nf = 2 * nsl
            for sl in range(sl0, sl0 + nsl):
                s = load_slab([
                    (lambda t: t[:, 0:KD * 512].rearrange("p (k g n) -> p k g n", g=2, n=256)[:, :, 0, :],
                     wiv[:, :, sl * 256:(sl + 1) * 256]),
                    (lambda t: t[:, 0:KD * 512].rearrange("p (k g n) -> p k g n", g=2, n=256)[:, :, 1, :],
                     wiv[:, :, DFF + sl * 256:DFF + (sl + 1) * 256]),
                ])
                wv = wr[s][:, 0:KD * 512].rearrange("p (k g n) -> p k g n", g=2, n=256)
                for fi in range(2):
                    fl = (sl - sl0) * 2 + fi
                    for (t0, n) in tiles:
                        pG = 2 * (pair % 2)
                        pU = pG + 1
                        sgi = pair % 2
                        pair += 1
                        for g_, pb in ((0, pG), (1, pU)):
                            for k in range(KD):
                                pg.op("pe", (lambda e, wv=wv, k=k, g_=g_, fi=fi, pb=pb, t0=t0, n=n: e.matmul(
                                    ps[pb][:, 0:n], lhsT=wv[:, k, g_, fi * 128:(fi + 1) * 128],
                                    rhs=hm[:, k, t0:t0 + n], start=(k == 0), stop=(k == KD - 1))),
                                    reads=[("wr", s), ("hm", k)], writes=[("ps", pb)], signal=(k == KD - 1))
                        pg.op("act", (lambda e, pG=pG, sgi=sgi, n=n: e.activation(
                            out=sg[sgi][:, 0:n], in_=ps[pG][:, 0:n], func=AF.Silu)),
                            reads=[("ps", pG)], writes=[("sg", sgi)])
                        pg.op("dve", (lambda e, pU=pU, sgi=sgi, fl=fl, t0=t0, n=n: e.tensor_tensor(
                            out=act[:, fl, t0:t0 + n], in0=sg[sgi][:, 0:n], in1=ps[pU][:, 0:n], op=ALU.mult)),
                            reads=[("sg", sgi), ("ps", pU)], writes=[("act", fl)])
            f0 = 2 * sl0
            for ds_ in range(D // 512):
                s = load_slab([(lambda t, nf=nf: t[:, 0:nf * 512].rearrange("p (f n) -> p f n", n=512),
                                wov[:, f0:f0 + nf, ds_ * 512:(ds_ + 1) * 512])])
                wv = wr[s][:, 0:nf * 512].rearrange("p (f n) -> p f n", n=512)
                for dc in range(4):
                    dg = ds_ * 4 + dc
                    for (t0, n) in tiles:
                        wi = 1 if t0 >= NL else 0
                        pb = 4 + (oc % 2)
                        oc += 1
                        for fl in range(nf):
                            pg.op("pe", (lambda e, wv=wv, fl=fl, dc=dc, pb=pb, t0=t0, n=n: e.matmul(
                                ps[pb][:, 0:n], lhsT=wv[:, fl, dc * 128:(dc + 1) * 128],
                                rhs=act[:, fl, t0:t0 + n], start=(fl == 0), stop=(fl == nf - 1))),
                                reads=[("wr", s), ("act", fl)], writes=[("ps", pb)], signal=(fl == nf - 1))
                        pg.op("dve", (lambda e, pb=pb, dg=dg, t0=t0, n=n, wi=wi: e.scalar_tensor_tensor(
                            out=hT[:, dg, t0:t0 + n], in0=ps[pb][:, 0:n], scalar=coef[:, wi, s_, 2, dg:dg + 1],
                            in1=hT[:, dg, t0:t0 + n], op0=ALU.mult, op1=ALU.add)),
                            reads=[("ps", pb), "coef", ("h", dg)], writes=[("h", dg)])

    norm_mod(0, TILES3)
    ffn(w1i, w1o, 0, TILES3)
    if debug:
        d1 = dbg_out("h1", [D, NT])
        pg.dma("sp", d1.rearrange("(k p) t -> p k t", p=128), hT, reads=HALL, key=("dbg", 3))

    if stage >= 2:
        mixer_args = dict(locals())
        _mixer(mixer_args)

    if stage >= 3:
        norm_mod(2, TILES2)
        ffn(w2i, w2o, 2, TILES2)

    rms_stats(TILES2)
    for k in range(KD):
        for (t0, n) in TILES2:
            pg.op("dve", (lambda e, k=k, t0=t0, n=n: e.scalar_tensor_tensor(
                out=hT[:, k, t0:t0 + n], in0=hT[:, k, t0:t0 + n], scalar=gains_sb[:, 3, k:k + 1],
                in1=rb[:, t0:t0 + n], op0=ALU.mult, op1=ALU.mult)),
                reads=[("h", k), ("rb", t0), "gains"], writes=[("h", k)])
    outv = outT.rearrange("(k p) t -> p k t", p=128)
    for k0 in range(0, KD, kstep):
        pg.dma("sp", outv[:, k0:k0 + kstep, :], hT[:, k0:k0 + kstep, 0:NL],
               reads=[("h", k) for k in range(k0, k0 + kstep)], writes=[("out", k0)], key="st_out", total=True)
    pg.wait_all("sp", [("out", k0) for k0 in range(0, KD, kstep)])
    for key in list(pg.dma_cnt):
        if isinstance(key, tuple) and key[0] == "dbg":
            pg.ops["sp"].append(("wait", ("dma", key), 16 * pg.dma_cnt[key]))
    return nc, pg, es


def _mixer(a):
    from types import SimpleNamespace
    v = SimpleNamespace(**a)
    pg, nc, KD, D = v.pg, v.nc, v.KD, v.D
    H, HM, ACTA, TMPA, ps, wr = v.H, v.HM, v.ACTA, v.TMPA, v.ps, v.wr
    hT, hm, gconst, pgk = v.hT, v.hm, v.gconst, v.pgk
    TILES2, TILES3 = v.TILES2, v.TILES3
    load_slab, transfer, dma_custom = v.load_slab, v.transfer, v.dma_custom
    sq, ntmp, sg, rtmp, rb, epsc = v.sq, v.ntmp, v.sg, v.rtmp, v.rb, v.epsc
    debug = v.debug
    w_in_v = v.w_in.rearrange("(k p) n -> p k n", p=128)

    def bail():
        names = set(pg.res_w) | set(pg.res_r)
        transfer(list(names), v.HALL + v.HMALL + [("act", f) for f in range(12)]
                 + [("sq", 0), ("sq", 1), "rtmp", ("ntmp", 0), ("ntmp", 1), ("sg", 0), ("sg", 1)])
        pg.dma("sp", hT[:, :, 0:NL], v.h_spill.rearrange("p (k t) -> p k t", t=NL), reads=["hspill"], writes=v.HALL,
               key="unspill")

    mstop = v.cfg.mstop

    def hmb(off, nbytes, dt=F32):
        x = HM[:, off // 2:(off + nbytes) // 2]
        return x.bitcast(F32) if dt == F32 else x

    v.norm_mod(1, TILES3)
    pg.dma("sp", v.h_spill.rearrange("p (k t) -> p k t", t=NL), hT[:, :, 0:NL], reads=v.HALL, writes=["hspill"],
           key="spill")
    o_acc = H[:, 0:8192].rearrange("p (i c) -> p i c", c=1024)
    v_tm = H[:, 8192:12800].bitcast(BF16).rearrange("p (i c) -> p i c", c=1024)
    mixg = H[:, 12800:16896].bitcast(BF16).rearrange("p (i t) -> p i t", t=1024)
    uext = H[:, 8192:8192 + 46 * 64]
    OACC = [("oacc", i) for i in range(8)]
    VTM = [("vtm", i) for i in range(9)]
    transfer(v.HALL, OACC + VTM + ["mixg"])
    kT = ACTA[:, 0:4352].rearrange("p (h t) -> p h t", t=NT)
    qT = ACTA[:, 4352:8448].rearrange("p (h t) -> p h t", t=NL)
    k_tm = ACTA[:, 8448:13056].rearrange("p (i c) -> p i c", c=512)
    mixc = ACTA[:, 0:8192].rearrange("p (i t) -> p i t", t=1024)
    KT = [("kT", h) for h in range(4)]
    QT = [("qT", h) for h in range(4)]
    KTM = [("ktm", i) for i in range(9)]
    transfer([("act", f) for f in range(12)], KT + QT + KTM)

    PADS = []

    def slab512(col0, ncols=512):
        s = load_slab([(lambda t: t[:, 0:KD * ncols].rearrange("p (k n) -> p k n", n=ncols),
                        w_in_v[:, :, col0:col0 + ncols])])
        return s, wr[s][:, 0:KD * ncols].rearrange("p (k n) -> p k n", n=ncols)

    pair = 0
    for sl in range(4):
        s = load_slab([
            (lambda t: t[:, 0:KD * 512].rearrange("p (k g n) -> p k g n", g=2, n=256)[:, :, 0, :],
             w_in_v[:, :, OFF_GA + sl * 256:OFF_GA + (sl + 1) * 256]),
            (lambda t: t[:, 0:KD * 512].rearrange("p (k g n) -> p k g n", g=2, n=256)[:, :, 1, :],
             w_in_v[:, :, OFF_GB + sl * 256:OFF_GB + (sl + 1) * 256]),
        ])
        wv = wr[s][:, 0:KD * 512].rearrange("p (k g n) -> p k g n", g=2, n=256)
        for fi in range(2):
            cc = sl * 2 + fi
            for (t0, n) in TILES2:
                pA = 2 * (pair % 2)
                pB = pA + 1
                bi = pair % 2
                pair += 1
                for g_, pb in ((0, pA), (1, pB)):
                    for k in range(KD):
                        pg.op("pe", (lambda e, wv=wv, k=k, g_=g_, fi=fi, pb=pb, t0=t0, n=n: e.matmul(
                            ps[pb][:, 0:n], lhsT=wv[:, k, g_, fi * 128:(fi + 1) * 128], rhs=hm[:, k, t0:t0 + n],
                            start=(k == 0), stop=(k == KD - 1))),
                            reads=[("wr", s), ("hm", k)], writes=[("ps", pb)], signal=(k == KD - 1))
                pg.op("act", (lambda e, pB=pB, bi=bi, n=n: e.activation(
                    out=sg[bi][:, 0:n], in_=ps[pB][:, 0:n], func=AF.Sigmoid)),
                    reads=[("ps", pB)], writes=[("sg", bi)])
                pg.op("dve", (lambda e, pA=pA, bi=bi, n=n: e.tensor_tensor(
                    out=ntmp[bi][:, 0:n], in0=sg[bi][:, 0:n], in1=ps[pA][:, 0:n], op=ALU.mult)),
                    reads=[("sg", bi), ("ps", pA)], writes=[("ntmp", bi)])
                c4 = cc % 4
                if cc < 4:
                    dst = v.u_row[c4 * 128:(c4 + 1) * 128, t0:t0 + n]
                else:
                    dst = v.ucol_in[c4 // 2][(c4 % 2) * 128:(c4 % 2 + 1) * 128, t0:t0 + n]
                pg.dma("sp", dst, ntmp[bi][:, 0:n], reads=[("ntmp", bi)],
                       writes=[("udram", cc, t0)], key=("ust", bi))
    UCOL = [("udram", cc, t0) for cc in range(4, 8) for (t0, n) in TILES2]
    UROW = [("udram", cc, t0) for cc in range(4) for (t0, n) in TILES2]
    if mstop <= 1:
        return bail()
    cnt = 0
    for sl in range(2):
        s, wv = slab512(OFF_G + sl * 512)
        for ti in range(8):
            pb = 4 + (cnt % 2)
            bi = cnt % 2
            cnt += 1
            for k in range(KD):
                pg.op("pe", (lambda e, wv=wv, k=k, pb=pb, ti=ti: e.matmul(
                    ps[pb][:, :], lhsT=hm[:, k, ti * 128:(ti + 1) * 128], rhs=wv[:, k, :],
                    start=(k == 0), stop=(k == KD - 1))),
                    reads=[("wr", s), ("hm", k)], writes=[("ps", pb)], signal=(k == KD - 1))
            pg.op("act", (lambda e, pb=pb, bi=bi: e.activation(out=sq[bi][:, :], in_=ps[pb][:, :], func=AF.Silu)),
                  reads=[("ps", pb)], writes=[("sq", bi)])
            pg.dma("sp", v.sg_spill[ti * 128:(ti + 1) * 128, sl * 512:(sl + 1) * 512], sq[bi][:, :],
                   reads=[("sq", bi)], writes=[("sgsp", ti, sl)], key=("sgst", bi))

    for sl in range(2):
        s, wv = slab512(OFF_V + sl * 512)
        for ti in range(9):
            np_ = 128 if ti < 8 else 64
            pb = 4 + (cnt % 2)
            cnt += 1
            for k in range(KD):
                pg.op("pe", (lambda e, wv=wv, k=k, pb=pb, ti=ti, np_=np_: e.matmul(
                    ps[pb][0:np_, :], lhsT=hm[:, k, ti * 128:ti * 128 + np_], rhs=wv[:, k, :],
                    start=(k == 0), stop=(k == KD - 1))),
                    reads=[("wr", s), ("hm", k)], writes=[("ps", pb)], signal=(k == KD - 1))
            pg.op("act", (lambda e, pb=pb, ti=ti, sl=sl, np_=np_: e.copy(
                out=v_tm[0:np_, ti, sl * 512:(sl + 1) * 512], in_=ps[pb][0:np_, :])),
                reads=[("ps", pb)], writes=[("vtm", ti)])

    s, wv = slab512(OFF_K)
    for ti in range(9):
        np_ = 128 if ti < 8 else 64
        pb = 4 + (cnt % 2)
        cnt += 1
        for k in range(KD):
            pg.op("pe", (lambda e, wv=wv, k=k, pb=pb, ti=ti, np_=np_: e.matmul(
                ps[pb][0:np_, :], lhsT=hm[:, k, ti * 128:ti * 128 + np_], rhs=wv[:, k, :],
                start=(k == 0), stop=(k == KD - 1))),
                reads=[("wr", s), ("hm", k)], writes=[("ps", pb)], signal=(k == KD - 1))
        pg.op("dve", (lambda e, pb=pb, ti=ti, np_=np_: e.tensor_copy(out=k_tm[0:np_, ti, :], in_=ps[pb][0:np_, :])),
              reads=[("ps", pb)], writes=[("ktm", ti)])
    for h in range(4):
        for (t0, n) in TILES3:
            pb = 4 + (cnt % 2)
            cnt += 1
            for k in range(KD):
                pg.op("pe", (lambda e, wv=wv, k=k, pb=pb, h=h, t0=t0, n=n: e.matmul(
                    ps[pb][:, 0:n], lhsT=wv[:, k, h * 128:(h + 1) * 128], rhs=hm[:, k, t0:t0 + n],
                    start=(k == 0), stop=(k == KD - 1))),
                    reads=[("wr", s), ("hm", k)], writes=[("ps", pb)], signal=(k == KD - 1))
            pg.op("act", (lambda e, pb=pb, h=h, t0=t0, n=n: e.copy(out=kT[:, h, t0:t0 + n], in_=ps[pb][:, 0:n])),
                  reads=[("ps", pb)], writes=[("kT", h)])
    s, wv = slab512(OFF_Q)
    for h in range(4):
        for (t0, n) in TILES2:
            pb = 4 + (cnt % 2)
            cnt += 1
            for k in range(KD):
                pg.op("pe", (lambda e, wv=wv, k=k, pb=pb, h=h, t0=t0, n=n: e.matmul(
                    ps[pb][:, 0:n], lhsT=wv[:, k, h * 128:(h + 1) * 128], rhs=hm[:, k, t0:t0 + n],
                    start=(k == 0), stop=(k == KD - 1))),
                    reads=[("wr", s), ("hm", k)], writes=[("ps", pb)], signal=(k == KD - 1))
            pg.op("act", (lambda e, pb=pb, h=h, t0=t0, n=n: e.mul(out=qT[:, h, t0:t0 + n], in_=ps[pb][:, 0:n],
                                                                 mul=float(128 ** -0.5))),
                  reads=[("ps", pb)], writes=[("qT", h)])
    s, wv = slab512(OFF_GKF, 32)
    for d in range(2):
        for (t0, n) in TILES3:
            pb = 4 + (cnt % 2)
            cnt += 1
            for k in range(KD):
                pg.op("pe", (lambda e, wv=wv, k=k, pb=pb, d=d, t0=t0, n=n: e.matmul(
                    ps[pb][0:16, 0:n], lhsT=wv[:, k, d * 16:(d + 1) * 16], rhs=hm[:, k, t0:t0 + n],
                    start=(k == 0), stop=(k == KD - 1))),
                    reads=[("wr", s), ("hm", k)], writes=[("ps", pb)], signal=(k == KD - 1))
            pg.op("dve", (lambda e, pb=pb, d=d, t0=t0, n=n: e.tensor_copy(out=pgk[0:16, d, t0:t0 + n],
                                                                        in_=ps[pb][0:16, 0:n])),
                  reads=[("ps", pb)], writes=["pgk"])

    for i_ in range(2):
        pg.custom("pool", (lambda e, i_=i_: e.collective_compute(
            "AllGather", ALU.bypass, replica_groups=[[0, 1, 2, 3], [4, 5, 6, 7]],
            ins=[v.ucol_in[i_][:, :]], outs=[v.ucol_g[i_][:, :]])), reads=UCOL, writes=[("ucolg", i_)],
            key=("cc_u", i_))


    if mstop <= 2:
        return bail()
    st_loc = hmb(0, 4 * EW * 4).rearrange("p (e w) -> p e w", w=EW)
    Sbf = HM[:, 8224:8224 + 2048].rearrange("p (d h e) -> p d h e", d=2, h=4)
    off = 20544
    spb = hmb(off, 2048); off += 2048
    edb = hmb(off, 2048); off += 2048
    kdb = HM[:, off // 2:off // 2 + 512]; off += 1024
    decb = hmb(off, 32); off += 32
    ebe = [hmb(off + i * 1024, 1024) for i in range(2)]; off += 2048
    e1n = [hmb(off + i * 512, 512) for i in range(2)]; off += 1024
    qks = [[HM[:, (off + (i * 3 + j) * 256) // 2:(off + (i * 3 + j) * 256) // 2 + 128] for j in range(3)]
           for i in range(2)]; off += 1536
    attT = [HM[:, (off + i * 256) // 2:(off + i * 256) // 2 + 128] for i in range(2)]; off += 512
    assert off <= 34816
    og = TMPA[:, 0:1024]
    sgt = TMPA[:, 1024:1536].bitcast(BF16)
    ebuf = TMPA[:, 1536:1536 + EW]
    t1 = TMPA[:, 2564:2820]
    ssb = TMPA[:, 2820:2828]
    GT = [("stloc", 0), ("stloc", 1), "Sbf0", "Sbf1", "spb", "edb", "kdb", "decb", ("ebe", 0), ("ebe", 1), ("e1n", 0), ("e1n", 1),
          ("qks", 0), ("qks", 1), ("attT", 0), ("attT", 1)]
    transfer(v.HMALL, GT)
    TM = ["og", "sgt", "ebuf", "t1", "ssb"]
    transfer([("sq", 0), ("sq", 1), "rtmp", ("ntmp", 0), ("ntmp", 1), ("sg", 0), ("sg", 1)], TM)

    def Sv(d):
        return st_loc[:, 2 * d + 1, 0:1024].rearrange("p (h e) -> p h e", e=256)

    def Sx(d):
        return st_loc[:, 2 * d, 0:1024].rearrange("p (h e) -> p h e", e=256)

    pg.op("dve", lambda e: e.memset(st_loc[:, :, 0:1024], 0.0), writes=[("stloc", 0), ("stloc", 1)])
    pg.op("dve", lambda e: e.memset(st_loc[:, :, 1024:1028], 1.0), writes=[("stloc", 0), ("stloc", 1)])

    Mtb = lambda d: gconst[:, GC_MTB + d * 128:GC_MTB + (d + 1) * 128]
    CE = lambda d: gconst[:, GC_CE + d * 256:GC_CE + (d + 1) * 256]
    CH = gconst[:, GC_CH:GC_CH + 2]
    MASK = lambda d: gconst[:, GC_MASK + d * 128:GC_MASK + (d + 1) * 128]
    IDM = gconst[:, GC_ID:GC_ID + 128]
    ONEF = gconst[:, GC_ONE:GC_ONE + 128]
    hcnt = {"n": 0}

    def gla_tile(ti, d, phase):
        np_ = 128 if ti < 8 else 64
        tok0 = ti * 128
        pg.op("pe", lambda e: e.matmul(ps[0][0:np_, :], lhsT=pgk[0:17, d, tok0:tok0 + np_], rhs=v.w2aug[0:17, d, :],
                                       start=True, stop=True), reads=["pgk", "w2aug"], writes=[("ps", 0)])
        pg.op("act", lambda e: e.activation(out=spb[0:np_, :], in_=ps[0][0:np_, :], func=AF.Exp, scale=-1.0),
              reads=[("ps", 0)], writes=["spb"])
        pg.op("act", lambda e: e.activation(out=spb[0:np_, :], in_=spb[0:np_, :], func=AF.Ln, bias=epsc[0:np_, 2:3],
                                            scale=1.0), reads=["spb", "epsc"], writes=["spb"])
        pg.op("pe", lambda e: e.matmul(ps[1][0:np_, :], lhsT=Mtb(d)[0:np_, 0:np_], rhs=spb[0:np_, :],
                                       start=True, stop=True), reads=["spb", "gconst"], writes=[("ps", 1)])
        pg.op("act", lambda e: e.activation(out=edb[0:np_, :], in_=ps[1][0:np_, :], func=AF.Exp),
              reads=[("ps", 1)], writes=["edb"])
        pg.op("dve", lambda e: e.tensor_tensor(out=kdb[0:np_, :], in0=k_tm[0:np_, ti, :], in1=edb[0:np_, :],
                                               op=ALU.mult), reads=[("ktm", ti), "edb"], writes=["kdb"])
        chunks = [0, 1] if d == 0 else [1, 0]
        if ti == 8:
            chunks = [0]
        if phase == "A":
            for h in range(4):
                pg.op("pe", (lambda e, h=h: e.matmul(ps[3][:, 2 * h:2 * h + 2], lhsT=spb[0:np_, h * 128:(h + 1) * 128],
                                                     rhs=CH[0:np_, :], start=True, stop=True)),
                      reads=["spb", "gconst"], writes=[("ps", 3)], signal=(h == 3))
            pg.op("act", lambda e: e.activation(out=decb[:, 0:8], in_=ps[3][:, 0:8], func=AF.Exp),
                  reads=[("ps", 3)], writes=["decb"])
            dview = decb[:, 0:8].rearrange("p (h x) -> p h x", x=2)
            for X in chunks:
                r0 = X * 64
                for h in range(4):
                    pb = 6 + (h // 2)
                    c0 = (h % 2) * 256
                    pg.op("pe", (lambda e, h=h, pb=pb, c0=c0, r0=r0: e.matmul(
                        ps[pb][:, c0:c0 + 256], lhsT=kdb[r0:r0 + 64, h * 128:(h + 1) * 128],
                        rhs=v_tm[r0:r0 + 64, ti, h * 256:(h + 1) * 256], start=True, stop=True)),
                        reads=["kdb", ("vtm", ti)], writes=[("ps", pb)])
                    if ti < 8:
                        pg.op("dve", (lambda e, h=h, pb=pb, c0=c0, X=X: e.scalar_tensor_tensor(
                            out=Sv(d)[:, h, :], in0=Sv(d)[:, h, :], scalar=decb[:, 2 * h + X:2 * h + X + 1],
                            in1=ps[pb][:, c0:c0 + 256], op0=ALU.mult, op1=ALU.add)),
                            reads=[("ps", pb), "decb", ("stloc", d)], writes=[("stloc", d)])
                    else:
                        pg.op("dve", (lambda e, h=h, pb=pb, c0=c0: e.tensor_copy(
                            out=Sx(d)[:, h, :], in_=ps[pb][:, c0:c0 + 256])),
                            reads=[("ps", pb)], writes=[("stloc", d)])
                if ti < 8:
                    pg.op("dve", (lambda e, X=X: e.tensor_tensor(
                        out=st_loc[:, 2 * d + 1, 1024:1028], in0=st_loc[:, 2 * d + 1, 1024:1028],
                        in1=dview[:, :, X], op=ALU.mult)), reads=["decb", ("stloc", d)], writes=[("stloc", d)])
                else:
                    pg.op("dve", (lambda e, X=X: e.tensor_copy(out=st_loc[:, 2 * d, 1024:1028], in_=dview[:, :, X])),
                          reads=["decb"], writes=[("stloc", d)])
            return
        S = Sv(d)
        sbn = "Sbf%d" % d
        hb = []
        for h in range(4):
            bi = hcnt["n"] % 2
            hcnt["n"] += 1
            hb.append(bi)
            pbe = 3 + (h // 2)
            cb = (h % 2) * 256
            pg.op("pe", (lambda e, h=h, pbe=pbe, cb=cb: e.matmul(
                ps[pbe][:, cb:cb + 256], lhsT=spb[:, h * 128:(h + 1) * 128], rhs=CE(d), start=True, stop=True)),
                reads=["spb", "gconst"], writes=[("ps", pbe)])
            pg.op("act", (lambda e, bi=bi, pbe=pbe, cb=cb: e.activation(
                out=ebe[bi][:, :], in_=ps[pbe][:, cb:cb + 256], func=AF.Exp)),
                reads=[("ps", pbe)], writes=[("ebe", bi)])
            pg.op("act", (lambda e, bi=bi, pbe=pbe, cb=cb: e.activation(
                out=e1n[bi][:, :], in_=ps[pbe][:, cb + 128:cb + 256], func=AF.Exp, scale=-1.0)),
                reads=[("ps", pbe)], writes=[("e1n", bi)])
            qb, qs, ks = qks[bi]
            pg.op("dve", (lambda e, h=h, bi=bi, qb=qb: e.tensor_tensor(
                out=qb[:, :], in0=qT[:, h, tok0:tok0 + 128], in1=ebe[bi][:, 0:128], op=ALU.mult)),
                reads=[("qT", h), ("ebe", bi)], writes=[("qks", bi)])
            pg.op("dve", (lambda e, h=h, bi=bi, qs=qs: e.tensor_tensor(
                out=qs[:, :], in0=qT[:, h, tok0:tok0 + 128], in1=ebe[bi][:, 128:256], op=ALU.mult)),
                reads=[("qT", h), ("ebe", bi)], writes=[("qks", bi)])
            pg.op("dve", (lambda e, h=h, bi=bi, ks=ks: e.tensor_tensor(
                out=ks[:, :], in0=kT[:, h, tok0:tok0 + 128], in1=e1n[bi][:, :], op=ALU.mult)),
                reads=[("kT", h), ("e1n", bi)], writes=[("qks", bi)])
            pg.op("pe", (lambda e, h=h, ks=ks, qs=qs: e.matmul(
                ps[5][:, h * 128:(h + 1) * 128], lhsT=ks[:, :], rhs=qs[:, :], start=True, stop=True)),
                reads=[("qks", bi)], writes=[("ps", 5)])
            pg.op("dve", (lambda e, h=h, bi=bi: e.tensor_tensor(
                out=attT[bi][:, :], in0=ps[5][:, h * 128:(h + 1) * 128], in1=MASK(d), op=ALU.mult)),
                reads=[("ps", 5), "gconst"], writes=[("attT", bi)])
            po = 6 + (h // 2)
            co = (h % 2) * 256
            pg.op("pe", (lambda e, h=h, bi=bi, po=po, co=co: e.matmul(
                ps[po][:, co:co + 256], lhsT=attT[bi][:, :], rhs=v_tm[:, ti, h * 256:(h + 1) * 256],
                start=(h % 2 == 0), stop=False, skip_group_check=True)),
                reads=[("attT", bi), ("vtm", ti)], writes=[("ps", po)])
            for xi, X in enumerate(chunks):
                r0 = X * 64
                pg.op("pe", (lambda e, h=h, qb=qb, po=po, co=co, r0=r0, xi=xi: e.matmul(
                    ps[po][r0:r0 + 64, co:co + 256], lhsT=qb[:, r0:r0 + 64], rhs=Sbf[:, d, h, :],
                    start=False, stop=(xi == 1), skip_group_check=True)),
                    reads=[("qks", bi), sbn], writes=[("ps", po)])
                kc = (h % 2) * 256
                pg.op("pe", (lambda e, h=h, r0=r0, kc=kc: e.matmul(
                    ps[2][:, kc:kc + 256], lhsT=kdb[r0:r0 + 64, h * 128:(h + 1) * 128],
                    rhs=v_tm[r0:r0 + 64, ti, h * 256:(h + 1) * 256], start=True, stop=True)),
                    reads=["kdb", ("vtm", ti)], writes=[("ps", 2)])
                col = (r0 + 63) if d == 0 else r0
                pg.op("dve", (lambda e, h=h, bi=bi, kc=kc, col=col: e.scalar_tensor_tensor(
                    out=S[:, h, :], in0=S[:, h, :], scalar=ebe[bi][:, col:col + 1], in1=ps[2][:, kc:kc + 256],
                    op0=ALU.mult, op1=ALU.add)), reads=[("ps", 2), ("ebe", bi), ("stloc", d)], writes=[("stloc", d)])
                pg.op("act", (lambda e, h=h: e.copy(out=Sbf[:, d, h, :], in_=S[:, h, :])),
                      reads=[("stloc", d)], writes=[sbn])

    def phase_a(d):
        order = list(range(8)) if d == 0 else list(range(7, -1, -1))
        for ti in order + [8]:
            gla_tile(ti, d, "A")

    def exchange(d):
        for e_ in (2 * d, 2 * d + 1):
            pg.dma("sp", v.st_in[e_][:, :], st_loc[:, e_, :], reads=[("stloc", d)], writes=[("st_in", e_)],
                   key=("st_io", e_))
        for e_ in (2 * d, 2 * d + 1):
            pg.custom("pool", (lambda e, e_=e_: e.collective_compute(
                "AllGather", ALU.bypass, replica_groups=[[0, 1, 2, 3], [4, 5, 6, 7]],
                ins=[v.st_in[e_][:, :]], outs=[v.st_out[e_][:, :]])), reads=[("st_in", e_)],
                writes=[("st_out", e_)], key=("cc_st", e_))

    def combine(d):
        S = Sv(d)
        pg.op("dve", (lambda e, S=S: e.memset(S[:, :, :], 0.0)), reads=[("stloc", d)], writes=[("stloc", d)])
        ctx_order = [0, 1, 2, 3] if d == 0 else [3, 2, 1, 0]
        seg_order = [0, 1, 2] if d == 0 else [3, 2, 1]
        for kind, order in ((0, ctx_order), (1, seg_order)):
            for r in order:
                e_ = 2 * d + kind
                pg.dma("sp", ebuf[:, :], v.st_out[e_][r * 128:(r + 1) * 128, :], reads=[("st_out", e_)],
                       writes=["ebuf"], key="ld_st")
                if kind == 0:
                    for h in range(4):
                        pg.op("dve", (lambda e, h=h, S=S: e.scalar_tensor_tensor(
                            out=S[:, h, :], in0=S[:, h, :], scalar=ebuf[:, 1024 + h:1025 + h],
                            in1=ebuf[:, h * 256:(h + 1) * 256], op0=ALU.mult, op1=ALU.add)),
                            reads=["ebuf", ("stloc", d)], writes=[("stloc", d)])
                else:
                    mcol = (0 if d == 0 else 8) + r
                    pg.op("dve", (lambda e, mcol=mcol: e.tensor_scalar(
                        out=t1[:, 0:4], in0=ebuf[:, 1024:1028], scalar1=v.segmask[:, mcol:mcol + 1],
                        scalar2=v.segmask[:, mcol + 4:mcol + 5], op0=ALU.mult, op1=ALU.add)),
                        reads=["ebuf", "segmask"], writes=["t1"])
                    pg.op("dve", (lambda e, mcol=mcol: e.tensor_scalar(
                        out=og[:, :], in0=ebuf[:, 0:1024], scalar1=v.segmask[:, mcol:mcol + 1], scalar2=None,
                        op0=ALU.mult)), reads=["ebuf", "segmask"], writes=["og"])
                    for h in range(4):
                        pg.op("dve", (lambda e, h=h, S=S: e.scalar_tensor_tensor(
                            out=S[:, h, :], in0=S[:, h, :], scalar=t1[:, h:h + 1], in1=og[:, h * 256:(h + 1) * 256],
                            op0=ALU.mult, op1=ALU.add)), reads=["t1", "og", ("stloc", d)], writes=[("stloc", d)])
        pg.op("act", (lambda e, S=S, d=d: e.copy(out=Sbf[:, d, :, :], in_=S[:, :, :])), reads=[("stloc", d)],
              writes=["Sbf%d" % d])

    phase_a(0)
    exchange(0)
    phase_a(1)
    exchange(1)
    combine(0)
    if mstop <= 3:
        return bail()
    for ti in range(8):
        gla_tile(ti, 0, "C")
        for half in range(2):
            pg.op("act", (lambda e, ti=ti, half=half: e.copy(out=o_acc[:, ti, half * 512:(half + 1) * 512],
                                                             in_=ps[6 + half][:, :])),
                  reads=[("ps", 6 + half)], writes=[("oacc", ti)])
    combine(1)
    for ti in range(7, -1, -1):
        gla_tile(ti, 1, "C")
        for half in range(2):
            pg.op("dve", (lambda e, ti=ti, half=half: e.tensor_tensor(
                out=o_acc[:, ti, half * 512:(half + 1) * 512], in0=o_acc[:, ti, half * 512:(half + 1) * 512],
                in1=ps[6 + half][:, :], op=ALU.add)), reads=[("ps", 6 + half), ("oacc", ti)], writes=[("oacc", ti)])
        pg.dma("sp", sgt[:, :], v.sg_spill[ti * 128:(ti + 1) * 128, :],
               reads=[("sgsp", ti, 0), ("sgsp", ti, 1)], writes=["sgt"], key="ld_sg")
        for h in range(4):
            pg.op("act", (lambda e, ti=ti, h=h: e.activation(
                out=t1[:, :], in_=o_acc[:, ti, h * 256:(h + 1) * 256], func=AF.Square,
                accum_out=ssb[:, h:h + 1])), reads=[("oacc", ti)], writes=["t1", "ssb"])
        pg.op("act", lambda e: e.activation(out=ssb[:, 4:8], in_=ssb[:, 0:4], func=AF.Sqrt, bias=epsc[:, 1:2],
                                            scale=1.0 / 256.0), reads=["ssb", "epsc"], writes=["ssb"])
        pg.op("dve", lambda e: e.reciprocal(out=ssb[:, 4:8], in_=ssb[:, 4:8]), reads=["ssb"], writes=["ssb"])
        for h in range(4):
            pg.op("dve", (lambda e, ti=ti, h=h: e.scalar_tensor_tensor(
                out=t1[:, :], in0=o_acc[:, ti, h * 256:(h + 1) * 256], scalar=ssb[:, 4 + h:5 + h], in1=v.gnb[:, :],
                op0=ALU.mult, op1=ALU.mult)), reads=[("oacc", ti), "ssb", "gnb"], writes=["t1"])
            pg.op("dve", (lambda e, h=h: e.tensor_tensor(
                out=og[:, h * 256:(h + 1) * 256], in0=t1[:, :], in1=sgt[:, h * 256:(h + 1) * 256], op=ALU.mult)),
                reads=["t1", "sgt"], writes=["og"])
        for half in range(2):
            pt = 5 if half == 0 else 2
            for b4 in range(4):
                blk = half * 4 + b4
                pg.op("pe", (lambda e, pt=pt, b4=b4, blk=blk: e.transpose(
                    out=ps[pt][:, b4 * 128:(b4 + 1) * 128], in_=og[:, blk * 128:(blk + 1) * 128], identity=IDM)),
                    reads=["og", "gconst"], writes=[("ps", pt)], signal=(b4 == 3))
            pg.op("act", (lambda e, pt=pt, half=half, ti=ti: e.copy(
                out=mixg[:, half * 4:(half + 1) * 4, ti * 128:(ti + 1) * 128],
                in_=ps[pt][:, :].rearrange("p (b t) -> p b t", t=128))), reads=[("ps", pt)], writes=["mixg"])
    if debug:
        dd = v.dbg_out("mixg", [1024, NL], BF16)
        pg.dma("sp", dd.rearrange("(i p) t -> p i t", p=128), mixg, reads=["mixg"], key=("dbg", 6))

    if mstop <= 4:
        return bail()
    y = hmb(0, 32768).rearrange("p (c t) -> p c t", t=1024)
    Y = [("y", c) for c in range(8)]
    transfer(GT, Y)
    transfer(VTM, ["uext"])
    transfer(KT + QT + KTM, ["mixc"])
    for cc in range(8):
        yv = y[:, cc, :]
        if cc < 4:
            pg.dma("sp", uext[:, 0:1024], v.u_row[cc * 128:(cc + 1) * 128, :], reads=UROW, writes=["uext"],
                   key="ld_u")
            ctr = uext[:, 0:1024]
        else:
            c4 = cc - 4

            def mk(which, c4=c4):
                UG = v.ucol_g[c4 // 2]
                ro = (c4 % 2) * 128

                def f(eng):
                    pid = v.get_pid(eng)
                    myr = pid % 4
                    if which == 0:
                        return eng.dma_start(out=uext[:, 0:960],
                                             in_=UG[bass.ds(((myr + 3) % 4) * 256 + ro, 128), 64:1024])
                    if which == 1:
                        return eng.dma_start(out=uext[:, 960:1984],
                                             in_=UG[bass.ds(myr * 256 + ro, 128), 0:1024])
                    return eng.dma_start(out=uext[:, 1984:2944],
                                         in_=UG[bass.ds(((myr + 1) % 4) * 256 + ro, 128), 0:960])
                return f
            for which in range(3):
                dma_custom("sp", mk(which), [("ucolg", c4 // 2)], ["uext"], "ld_uc")
            pg.op("dve", lambda e: e.tensor_scalar(out=uext[:, 0:960], in0=uext[:, 0:960], scalar1=v.segmask[:, 0:1],
                                                   scalar2=None, op0=ALU.mult), reads=["uext", "segmask"],
                  writes=["uext"])
            pg.op("dve", lambda e: e.tensor_scalar(out=uext[:, 1984:2944], in0=uext[:, 1984:2944],
                                                   scalar1=v.segmask[:, 11:12], scalar2=None, op0=ALU.mult),
                  reads=["uext", "segmask"], writes=["uext"])
            ctr = uext[:, 960:1984]
        pg.op("dve", (lambda e, cc=cc, ctr=ctr, yv=yv: e.tensor_scalar(
            out=yv, in0=ctr, scalar1=v.convw[:, cc, 15:16], scalar2=v.convp[:, cc, 0:1], op0=ALU.mult, op1=ALU.add)),
            reads=["uext", "convw", "convp"], writes=[("y", cc)])
        for j in range(31):
            if j == 15:
                continue
            s_ = j - 15
            if cc < 4:
                c0, c1 = max(0, -s_), min(64, 64 - s_)
                y3 = yv.rearrange("p (r c) -> p r c", c=64)[:, :, c0:c1]
                u3 = uext[:, 0:1024].rearrange("p (r c) -> p r c", c=64)[:, :, c0 + s_:c1 + s_]
            else:
                y3 = yv
                u3 = uext[:, j * 64:j * 64 + 1024]
            pg.op("dve", (lambda e, cc=cc, j=j, y3=y3, u3=u3: e.scalar_tensor_tensor(
                out=y3, in0=u3, scalar=v.convw[:, cc, j:j + 1], in1=y3, op0=ALU.mult, op1=ALU.add)),
                reads=["uext", "convw", ("y", cc)], writes=[("y", cc)])
    if debug:
        dd = v.dbg_out("y", [1024, NL])
        pg.dma("sp", dd.rearrange("(i p) t -> p i t", p=128), y, reads=Y, key=("dbg", 7))
    if mstop <= 5:
        return bail()
    transfer(TM, [("sq", 0), ("sq", 1), "rtmp", ("ntmp", 0), ("ntmp", 1), ("sg", 0), ("sg", 1)])
    cnt = 0
    for (t0, n) in TILES2:
        for cc in range(8):
            pg.op("pe", (lambda e, cc=cc, t0=t0, n=n: e.matmul(ps[0][:, 0:n], lhsT=ONEF, rhs=y[:, cc, t0:t0 + n],
                                                             start=(cc == 0), stop=(cc == 7))),
                  reads=[("y", cc), "gconst"], writes=[("ps", 0)], signal=(cc == 7))
        for cc in range(8):
            pg.op("dve", (lambda e, cc=cc, t0=t0, n=n: e.tensor_tensor(
                out=y[:, cc, t0:t0 + n], in0=y[:, cc, t0:t0 + n], in1=ps[0][:, 0:n], op=ALU.subtract)),
                reads=[("ps", 0), ("y", cc)], writes=[("y", cc)])
        for cc in range(8):
            bi = cnt % 2
            cnt += 1
            pg.op("act", (lambda e, cc=cc, bi=bi, t0=t0, n=n: e.activation(
                out=ntmp[bi][:, 0:n], in_=y[:, cc, t0:t0 + n], func=AF.Square)),
                reads=[("y", cc)], writes=[("ntmp", bi)])
            pg.op("pe", (lambda e, cc=cc, bi=bi, n=n: e.matmul(ps[1][:, 0:n], lhsT=ONEF, rhs=ntmp[bi][:, 0:n],
                                                             start=(cc == 0), stop=(cc == 7))),
                  reads=[("ntmp", bi), "gconst"], writes=[("ps", 1)], signal=True)
        pg.op("act", (lambda e, n=n: e.activation(out=rtmp[:, 0:n], in_=ps[1][:, 0:n], func=AF.Sqrt,
                                                  bias=epsc[:, 1:2], scale=1.0)),
              reads=[("ps", 1), "epsc"], writes=["rtmp"])
        pg.op("dve", (lambda e, t0=t0, n=n: e.reciprocal(out=rb[:, t0:t0 + n], in_=rtmp[:, 0:n])),
              reads=["rtmp"], writes=[("rb", t0)])
        for cc in range(8):
            bi = cnt % 2
            cnt += 1
            pg.op("dve", (lambda e, cc=cc, bi=bi, t0=t0, n=n: e.tensor_tensor(
                out=ntmp[bi][:, 0:n], in0=y[:, cc, t0:t0 + n], in1=rb[:, t0:t0 + n], op=ALU.mult)),
                reads=[("y", cc), ("rb", t0)], writes=[("ntmp", bi)])
            pg.op("act", (lambda e, cc=cc, bi=bi, t0=t0, n=n: e.activation(
                out=mixc[:, cc, t0:t0 + n], in_=ntmp[bi][:, 0:n], func=AF.Silu,
                bias=v.convp[:, cc, 2:3], scale=v.convp[:, cc, 1:2])),
                reads=[("ntmp", bi), "convp"], writes=["mixc"])
    if debug:
        dd = v.dbg_out("mixc", [1024, NL], BF16)
        pg.dma("sp", dd.rearrange("(i p) t -> p i t", p=128), mixc, reads=["mixc"], key=("dbg", 8))

    mixg2 = HM[:, 0:8192].rearrange("p (i t) -> p i t", t=1024)
    transfer(Y, ["mixg2"])
    pg.op("act", lambda e: e.copy(out=mixg2[:, :, :], in_=mixg[:, :, :]), reads=["mixg"], writes=["mixg2"])
    transfer(OACC + ["uext", "mixg"], v.HALL)
    pg.dma("sp", hT[:, :, 0:NL], v.h_spill.rearrange("p (k t) -> p k t", t=NL), reads=["hspill"], writes=v.HALL,
           key="unspill")

    wov = v.w_out.rearrange("(i p) n -> p i n", p=128)
    oc = 0
    for ds_ in range(D // 512):
        s = load_slab([(lambda t: t[:, 0:8192].rearrange("p (i n) -> p i n", n=512),
                        wov[:, :, ds_ * 512:(ds_ + 1) * 512])])
        wv = wr[s][:, 0:8192].rearrange("p (i n) -> p i n", n=512)
        for dc in range(4):
            dg = ds_ * 4 + dc
            for (t0, n) in TILES2:
                pb = 4 + (oc % 2)
                oc += 1
                for i in range(16):
                    src = mixg2 if i < 8 else mixc
                    rn = "mixg2" if i < 8 else "mixc"
                    pg.op("pe", (lambda e, wv=wv, i=i, dc=dc, pb=pb, t0=t0, n=n, src=src: e.matmul(
                        ps[pb][:, 0:n], lhsT=wv[:, i, dc * 128:(dc + 1) * 128], rhs=src[:, i % 8, t0:t0 + n],
                        start=(i == 0), stop=(i == 15))),
                        reads=[("wr", s), rn], writes=[("ps", pb)], signal=(i == 15))
                pg.op("dve", (lambda e, pb=pb, dg=dg, t0=t0, n=n: e.scalar_tensor_tensor(
                    out=hT[:, dg, t0:t0 + n], in0=ps[pb][:, 0:n], scalar=v.coef[:, 0, 1, 2, dg:dg + 1],
                    in1=hT[:, dg, t0:t0 + n], op0=ALU.mult, op1=ALU.add)),
                    reads=[("ps", pb), "coef", ("h", dg)], writes=[("h", dg)])
    transfer(["mixg2"], v.HMALL)
    transfer(["mixc"], [("act", f) for f in range(12)])
    if debug:
        d2 = v.dbg_out("h2", [D, NL])
        pg.dma("sp", d2.rearrange("(k p) t -> p k t", p=128), hT[:, :, 0:NL], reads=v.HALL, key=("dbg", 9))


def _emit(pg, es):
    nc = pg.nc
    sems = {}
    for e in pg.ENGS:
        sems[e] = es.enter_context(nc.semaphore("s_" + e))
    for i, key in enumerate(sorted(pg.dma_cnt, key=str)):
        sems[("dma", key)] = es.enter_context(nc.semaphore("d%d" % i))
    for e in pg.ENGS:
        assert pg.max_wait.get(e, 0) <= pg.count[e], (e, pg.max_wait.get(e), pg.count[e])
    block = es.enter_context(nc.Block())
    deco = {"pe": block.tensor, "act": block.scalar, "dve": block.vector, "pool": block.gpsimd, "sp": block.sync}

    def make(e):
        def body(eng):
            for item in pg.ops[e]:
                kind = item[0]
                if kind == "wait":
                    _, src, val = item
                    if isinstance(src, tuple) and src[1] in pg.total_keys:
                        val = 16 * pg.dma_cnt[src[1]]
                    eng.wait_ge(sems[src], val)
                elif kind == "op":
                    _, fn, signal = item
                    ins = fn(eng)
                    if signal:
                        ins.then_inc(sems[e], 1)
                elif kind == "dma":
                    _, out, in_, key, kw = item
                    eng.dma_start(out=out, in_=in_, **kw).then_inc(sems[("dma", key)], 16)
                elif kind == "custom":
                    _, fn, key = item
                    fn(eng).then_inc(sems[("dma", key)], 1)
                elif kind == "custom16":
                    _, fn, key = item
                    fn(eng).then_inc(sems[("dma", key)], 16)
        return body

    for e in pg.ENGS:
        deco[e](make(e))


def _fm(v, kd):
    return np.ascontiguousarray(np.asarray(v, np.float32).reshape(kd, 128).T)


def prepare_inputs(inputs, cfg):
    KD, D, CPC = cfg.KD, cfg.D, cfg.CPC
    f32 = lambda a: np.asarray(a, np.float32)
    x, ctx, c, c_ctx = f32(inputs["x"]), f32(inputs["ctx"]), f32(inputs["c"]), f32(inputs["c_ctx"])
    w_mod, b_mod = f32(inputs["w_mod"])[0], f32(inputs["b_mod"])[0]
    cTs = [np.ascontiguousarray(np.stack([_fm(c[b_], KD), _fm(c_ctx, KD)], axis=-1)) for b_ in range(2)]
    gains = np.stack([_fm(inputs["norm_ffn1"][0], KD), _fm(inputs["norm_mix"][0], KD),
                      _fm(inputs["norm_ffn2"][0], KD), _fm(inputs["norm_final"], KD)], axis=1)
    w_gk2, b_gk2 = f32(inputs["w_gk2"])[0], f32(inputs["b_gk2"])[0]
    w2aug = np.concatenate([w_gk2.transpose(1, 0, 2), b_gk2[None]], axis=0)
    conv_w = f32(inputs["conv_w"])[0]
    convw = np.ascontiguousarray(conv_w.T.reshape(8, 128, 31).transpose(1, 0, 2))
    convp = np.stack([f32(inputs["conv_b"])[0].reshape(8, 128).T, f32(inputs["conv_ln_g"])[0].reshape(8, 128).T,
                      f32(inputs["conv_ln_b"])[0].reshape(8, 128).T], axis=-1)
    shared = {
        "gains": np.ascontiguousarray(gains),
        "w_ffn1_in": f32(inputs["w_ffn1_in"])[0], "w_ffn1_out": f32(inputs["w_ffn1_out"])[0],
        "w_ffn2_in": f32(inputs["w_ffn2_in"])[0], "w_ffn2_out": f32(inputs["w_ffn2_out"])[0],
        "w_in": f32(inputs["w_in"])[0], "w_out": f32(inputs["w_out"])[0],
        "gconst": gla_consts(), "w2aug": np.ascontiguousarray(w2aug),
        "gnb": np.ascontiguousarray(np.broadcast_to(f32(inputs["gla_norm"])[0][None, :], (128, 256))),
        "convw": convw, "convp": np.ascontiguousarray(convp),
    }
    in_maps = []
    for core in range(8):
        b, j = core // 4, core % 4
        xt = np.concatenate([x[b, j * NL:(j + 1) * NL], ctx[b, j * NC_:(j + 1) * NC_]], axis=0)
        m = dict(shared)
        m["xT"] = np.ascontiguousarray(xt.T)
        m["cT"] = cTs[b]
        m["wmod"] = np.ascontiguousarray(w_mod[:, j * CPC * 128:(j + 1) * CPC * 128])
        m["bmod"] = np.ascontiguousarray(b_mod[j * CPC * 128:(j + 1) * CPC * 128].reshape(CPC, 128).T)
        sm = np.zeros((128, 16), np.float32)
        for r in range(4):
            sm[:, r] = 1.0 if r < j else 0.0
            sm[:, 8 + r] = 1.0 if r > j else 0.0
        sm[:, 4:8] = 1.0 - sm[:, 0:4]
        sm[:, 12:16] = 1.0 - sm[:, 8:12]
        m["segmask"] = sm
        in_maps.append(m)
    return in_maps


def build(cfg=None, stage=99, debug=False):
    cfg = cfg or Cfg()
    nc, pg, es = build_program(cfg, stage=stage, debug=debug)
    with es:
        _emit(pg, es)
    return nc, pg


def run(inputs, cfg=None, stage=99, debug=False, trace=False):
    cfg = cfg or Cfg()
    nc, pg = build(cfg, stage=stage, debug=debug)
    in_maps = prepare_inputs(inputs, cfg)
    if stage < 2:
        for m in in_maps:
            pass
    return run_bass_kernel_spmd(nc, in_maps, core_ids=list(range(8)), trace=trace)


def kernel(**inputs):
    cfg = Cfg()
    res = run(inputs, cfg)
    out = np.empty((2, 4096, cfg.D), np.float32)
    for core in range(8):
        b, j = core // 4, core % 4
        out[b, j * NL:(j + 1) * NL, :] = res.results[core]["outT"].T
    return out
```

```python
import numpy as np
import concourse.bass as bass
import concourse.mybir as mybir
from concourse.bass_utils import run_bass_kernel_spmd
from contextlib import ExitStack

F32 = mybir.dt.float32
BF16 = mybir.dt.bfloat16
AF = mybir.ActivationFunctionType
ALU = mybir.AluOpType

D = 2048
KD = 16
NL = 1024
NC_ = 64
NT = NL + NC_
DFF = 5632
NFC = 44
RMS_EPS = 1e-6

import os
SAME_ENGINE_SYNC = os.environ.get("KERNEL_SES", "1") == "1"


class Prog:
    ENGS = ("pe", "act", "dve", "pool", "sp")

    def __init__(self, nc):
        self.nc = nc
        self.ops = {e: [] for e in self.ENGS}
        self.count = {e: 0 for e in self.ENGS}
        self.waited = {e: {} for e in self.ENGS}
        self.res_w = {}
        self.res_r = {}
        self.dma_cnt = {}
        self.total_keys = set()
        self.max_wait = {}

    def _deps(self, eng, reads, writes):
        deps = {}

        def add(src, val):
            if deps.get(src, 0) < val:
                deps[src] = val

        for r in reads:
            w = self.res_w.get(r)
            if w is not None:
                add(*w)
        for w_ in writes:
            w = self.res_w.get(w_)
            if w is not None:
                add(*w)
            for src, val in self.res_r.get(w_, {}).items():
                add(src, val)
        out = []
        for src, val in deps.items():
            if src == eng:
                if not SAME_ENGINE_SYNC or eng == "pe":
                    continue
                if val > self.count[eng]:
                    continue
            if self.waited[eng].get(src, 0) >= val:
                continue
            self.waited[eng][src] = val
            out.append((src, val))
        return out

    def op(self, eng, fn, reads=(), writes=(), signal=True):
        for src, val in self._deps(eng, reads, writes):
            self.ops[eng].append(("wait", src, val))
            if self.max_wait.get(src, 0) < val:
                self.max_wait[src] = val
        if signal:
            self.count[eng] += 1
            seq = self.count[eng]
        else:
            seq = self.count[eng] + 1
        self.ops[eng].append(("op", fn, signal))
        for r in reads:
            self.res_r.setdefault(r, {})[eng] = seq
        for w in writes:
            self.res_w[w] = (eng, seq)
            self.res_r[w] = {}
        return seq

    def dma(self, q, out, in_, reads=(), writes=(), key=None, total=False, **kw):
        assert key is not None
        for src, val in self._deps(q, reads, writes):
            self.ops[q].append(("wait", src, val))
            if self.max_wait.get(src, 0) < val:
                self.max_wait[src] = val
        self.dma_cnt[key] = self.dma_cnt.get(key, 0) + 1
        val = 16 * self.dma_cnt[key]
        if total:
            self.total_keys.add(key)
        src = ("dma", key)
        self.ops[q].append(("dma", out, in_, key, kw))
        for r in reads:
            self.res_r.setdefault(r, {})[src] = val
        for w in writes:
            self.res_w[w] = (src, val)
            self.res_r[w] = {}

    def custom(self, q, fn, reads=(), writes=(), key=None):
        for src, val in self._deps(q, reads, writes):
            self.ops[q].append(("wait", src, val))
        self.dma_cnt[key] = self.dma_cnt.get(key, 0) + 1
        val = self.dma_cnt[key]
        src = ("dma", key)
        self.ops[q].append(("custom", fn, key))
        for r in reads:
            self.res_r.setdefault(r, {})[src] = val
        for w in writes:
            self.res_w[w] = (src, val)
            self.res_r[w] = {}

    def wait_all(self, eng, resources):
        for src, val in self._deps(eng, resources, ()):
            self.ops[eng].append(("wait", src, val))


class Cfg:
    def __init__(self, D=2048, DFF=5632, mstop=99):
        self.mstop = mstop
        self.D = D
        self.KD = D // 128
        self.DFF = DFF
        self.NFC = DFF // 128
        self.CPC = 9 * self.KD // 4
        assert 9 * self.KD % 4 == 0 and self.NFC % 2 == 0 and D % 512 == 0


OFF_K, OFF_V, OFF_GKF, OFF_GKB, OFF_Q, OFF_G, OFF_GA, OFF_GB = 0, 512, 1536, 1552, 1568, 2080, 3104, 4128
D_IN = 5152
EW = 1028
GC_MTB = 0
GC_CE = 256
GC_CH = 768
GC_MASK = 776
GC_ID = 1032
GC_ONE = 1160
GC_N = 1288


def gla_consts():
    g = np.zeros((128, GC_N), np.float32)
    s = np.arange(128)[:, None]
    t = np.arange(128)[None, :]
    same = (s // 64) == (t // 64)
    sc = -1.0 / 16.0
    g[:, GC_MTB:GC_MTB + 128] = sc * (same & (s > t))
    g[:, GC_MTB + 128:GC_MTB + 256] = sc * (same & (s < t))
    cum_f = sc * (same & (s <= t))
    cum_b = sc * (same & (s >= t))
    mid_f = sc * (same & ((s % 64) <= 31))
    mid_b = sc * (same & ((s % 64) >= 32))
    g[:, GC_CE:GC_CE + 128] = cum_f
    g[:, GC_CE + 128:GC_CE + 256] = cum_f - mid_f
    g[:, GC_CE + 256:GC_CE + 384] = cum_b
    g[:, GC_CE + 384:GC_CE + 512] = cum_b - mid_b
    g[:, GC_CH] = sc * (np.arange(128) < 64)
    g[:, GC_CH + 1] = sc * (np.arange(128) >= 64)
    g[:, GC_MASK:GC_MASK + 128] = (same & (t >= s))
    g[:, GC_MASK + 128:GC_MASK + 256] = (same & (t <= s))
    g[:, GC_ID:GC_ID + 128] = np.eye(128)
    g[:, GC_ONE:GC_ONE + 128] = 1.0 / 1024.0
    return g


def build_program(cfg=None, stage=99, debug=False):
    cfg = cfg or Cfg()
    D, KD, DFF, NFC, CPC = cfg.D, cfg.KD, cfg.DFF, cfg.NFC, cfg.CPC
    nc = bass.Bass("TRN2", target_bir_lowering=False)
    es = ExitStack()
    pg = Prog(nc)

    def dram_in(name, shape, dt=F32):
        return nc.dram_tensor(name, list(shape), dt, kind="ExternalInput").ap()

    def dram_tmp(name, shape, dt=F32):
        return nc.dram_tensor(name, list(shape), dt, kind="Internal").ap()

    xT = dram_in("xT", [D, NT])
    cT = dram_in("cT", [128, KD, 2])
    wmod = dram_in("wmod", [D, CPC * 128])
    bmod = dram_in("bmod", [128, CPC])
    gains = dram_in("gains", [128, 4, KD])
    w1i = dram_in("w_ffn1_in", [D, 2 * DFF])
    w1o = dram_in("w_ffn1_out", [DFF, D])
    w2i = dram_in("w_ffn2_in", [D, 2 * DFF])
    w2o = dram_in("w_ffn2_out", [DFF, D])
    w_in = dram_in("w_in", [D, D_IN])
    w_out = dram_in("w_out", [2048, D])
    gconst_d = dram_in("gconst", [128, GC_N])
    w2aug_d = dram_in("w2aug", [17, 2, 512])
    gnb_d = dram_in("gnb", [128, 256])
    convw_d = dram_in("convw", [128, 8, 31])
    convp_d = dram_in("convp", [128, 8, 3])
    segmask_d = dram_in("segmask", [128, 16])
    outT = nc.dram_tensor("outT", [D, NL], F32, kind="ExternalOutput").ap()
    mod_in = dram_tmp("mod_in", [128, 2 * CPC])
    mod_out = dram_tmp("mod_out", [4 * 128, 2 * CPC])
    h_spill = dram_tmp("h_spill", [128, KD * NL])
    sg_spill = dram_tmp("sg_spill", [NL, 1024], BF16)
    u_row = dram_tmp("u_row", [512, NL])
    ucol_in = [dram_tmp("ucol_in%d" % i, [256, NL]) for i in range(2)]
    ucol_g = [dram_tmp("ucol_g%d" % i, [4 * 256, NL]) for i in range(2)]
    st_in = [dram_tmp("st_in%d" % i, [128, EW]) for i in range(4)]
    st_out = [dram_tmp("st_out%d" % i, [4 * 128, EW]) for i in range(4)]
    dbg = {}

    def dbg_out(name, shape, dt=F32):
        dbg[name] = nc.dram_tensor("dbg_" + name, list(shape), dt, kind="ExternalOutput").ap()
        return dbg[name]

    def sb(name, shape, dt):
        return es.enter_context(nc.sbuf_tensor(name, list(shape), dt))

    HN = max(KD * NT, 17408)
    H = sb("H", [128, HN], F32)
    hT = H[:, 0:KD * NT].rearrange("p (k t) -> p k t", t=NT)
    HMN = max(KD * NT, 17408)
    HM = sb("HM", [128, HMN], BF16)
    hm = HM[:, 0:KD * NT].rearrange("p (k t) -> p k t", t=NT)
    wr = [sb("wr%d" % i, [128, 8192], BF16) for i in range(2)]
    NRING = 2
    ACTA = sb("ACTA", [128, 12 * NT], BF16)
    act = ACTA[:, :].rearrange("p (f t) -> p f t", t=NT)
    TMPA = sb("TMPA", [128, 3072], F32)
    ones_bf = sb("ones_bf", [128, 128], BF16)
    cs = sb("cs", [128, KD, 2], F32)
    cs_bf = sb("cs_bf", [128, KD, 2], BF16)
    bmod_sb = sb("bmod_sb", [128, CPC], F32)
    modloc = sb("modloc", [128, 2, CPC], F32)
    modL = sb("modL", [128, 9 * KD], F32)
    modC = sb("modC", [128, 9 * KD], F32)
    gains_sb = sb("gains_sb", [128, 4, KD], F32)
    coef = sb("coef", [128, 2, 3, 3, KD], F32)
    rb = sb("rb", [128, NT], F32)
    epsc = sb("epsc", [128, 4], F32)
    gconst = sb("gconst_sb", [128, GC_N], F32)
    w2aug = sb("w2aug_sb", [17, 2, 512], F32)
    gnb = sb("gnb_sb", [128, 256], F32)
    convw = sb("convw_sb", [128, 8, 31], F32)
    convp = sb("convp_sb", [128, 8, 3], F32)
    segmask = sb("segmask_sb", [128, 16], F32)
    pgk = sb("pgk", [32, 2, NT], F32)
    ps = [es.enter_context(nc.psum_tensor("ps%d" % i, [128, 512], F32)) for i in range(8)]

    sq = [TMPA[:, 0:256].bitcast(BF16), TMPA[:, 256:512].bitcast(BF16)]
    rtmp = TMPA[:, 512:1024]
    ntmp = [TMPA[:, 1024:1536], TMPA[:, 1536:2048]]
    sg = [TMPA[:, 2048:2560], TMPA[:, 2560:3072]]

    TILES3 = [(0, 512), (512, 512), (1024, 64)]
    TILES2 = [(0, 512), (512, 512)]
    HALL = [("h", k) for k in range(KD)]
    HMALL = [("hm", k) for k in range(KD)]

    def transfer(old, new):
        u = {}
        for o in old:
            w = pg.res_w.get(o)
            if w is not None:
                u[w[0]] = max(u.get(w[0], 0), w[1])
            for src, val in pg.res_r.get(o, {}).items():
                u[src] = max(u.get(src, 0), val)
        for n in new:
            pg.res_w.pop(n, None)
            pg.res_r[n] = dict(u)

    def dma_custom(q, fn, reads, writes, key):
        for src, val in pg._deps(q, reads, writes):
            pg.ops[q].append(("wait", src, val))
        pg.dma_cnt[key] = pg.dma_cnt.get(key, 0) + 1
        val = 16 * pg.dma_cnt[key]
        src = ("dma", key)
        pg.ops[q].append(("custom16", fn, key))
        for r in reads:
            pg.res_r.setdefault(r, {})[src] = val
        for w in writes:
            pg.res_w[w] = (src, val)
            pg.res_r[w] = {}

    xTv = xT.rearrange("(k p) t -> p k t", p=128)
    kstep = max(1, KD // 4)
    for k0 in range(0, KD, kstep):
        pg.dma("sp", hT[:, k0:k0 + kstep, :], xTv[:, k0:k0 + kstep, :],
               writes=[("h", k) for k in range(k0, k0 + kstep)], key="ld_x", total=True)
    for dst, src, name in ((cs[:], cT[:, :, :], "cs"), (bmod_sb[:], bmod[:, :], "bmod"),
                           (gains_sb[:], gains[:, :, :], "gains"), (gconst[:], gconst_d[:, :], "gconst"),
                           (w2aug[:], w2aug_d[:, :, :], "w2aug"), (gnb[:], gnb_d[:, :], "gnb"),
                           (convw[:], convw_d[:, :, :], "convw"), (convp[:], convp_d[:, :, :], "convp"),
                           (segmask[:], segmask_d[:, :], "segmask")):
        pg.dma("sp", dst, src, writes=[name], key="const", total=True)
    pg.op("dve", lambda e: e.memset(ones_bf[:], 1.0 / D), writes=["ones"])
    pg.op("dve", lambda e: e.memset(epsc[:, 0:1], RMS_EPS), writes=["epsc"])
    pg.op("dve", lambda e: e.memset(epsc[:, 1:2], 1e-5), writes=["epsc"])
    pg.op("dve", lambda e: e.memset(epsc[:, 2:3], 1.0), writes=["epsc"])
    pg.op("dve", lambda e: e.memset(pgk[:], 1.0), writes=["pgk"])

    ring = {"n": 0}

    def load_slab(pieces):
        s = ring["n"] % NRING
        ring["n"] += 1
        for vf, src in pieces:
            pg.dma("pool", vf(wr[s]), src, writes=[("wr", s)], key=("wr", s))
        return s

    def rms_stats(tiles):
        cnt = 0
        for (t0, n) in tiles:
            for k in range(KD):
                si = cnt % 2
                sqb = sq[si]
                cnt += 1
                pg.op("act", (lambda e, sqb=sqb, k=k, t0=t0, n=n: e.activation(
                    out=sqb[:, 0:n], in_=hT[:, k, t0:t0 + n], func=AF.Square)),
                    reads=[("h", k)], writes=[("sq", si)])
                pg.op("pe", (lambda e, sqb=sqb, k=k, n=n: e.matmul(
                    ps[6][:, 0:n], lhsT=ones_bf[:, :], rhs=sqb[:, 0:n], start=(k == 0), stop=(k == KD - 1))),
                    reads=[("sq", si), "ones"], writes=[("ps", 6)], signal=True)
            pg.op("act", (lambda e, n=n: e.activation(out=rtmp[:, 0:n], in_=ps[6][:, 0:n], func=AF.Sqrt,
                                                      bias=epsc[:, 0:1], scale=1.0)),
                  reads=[("ps", 6), "epsc"], writes=["rtmp"])
            pg.op("dve", (lambda e, t0=t0, n=n: e.reciprocal(out=rb[:, t0:t0 + n], in_=rtmp[:, 0:n])),
                  reads=["rtmp"], writes=[("rb", t0)])

    pg.op("act", lambda e: e.activation(out=cs_bf[:], in_=cs[:], func=AF.Silu), reads=["cs"], writes=["cs_bf"])
    wmv = wmod.rearrange("(k p) n -> p k n", p=128)
    rms_stats(TILES3)
    MC = 4 if CPC % 4 == 0 else 2
    for sl in range(CPC // MC):
        s = load_slab([(lambda t: t[:, 0:KD * 128 * MC].rearrange("p (k n) -> p k n", n=128 * MC),
                        wmv[:, :, sl * 128 * MC:(sl + 1) * 128 * MC])])
        wv = wr[s][:, 0:KD * 128 * MC].rearrange("p (k n) -> p k n", n=128 * MC)
        for gi in range(MC):
            gl = sl * MC + gi
            for k in range(KD):
                pg.op("pe", (lambda e, wv=wv, k=k, gl=gl, gi=gi: e.matmul(
                    ps[7][:, gl * 2:gl * 2 + 2], lhsT=wv[:, k, gi * 128:(gi + 1) * 128], rhs=cs_bf[:, k, :],
                    start=(k == 0), stop=(k == KD - 1))),
                    reads=[("wr", s), "cs_bf"], writes=[("ps", 7)], signal=(k == KD - 1))
    pg.op("dve", lambda e: e.tensor_tensor(
        out=modloc[:].rearrange("p v g -> p g v"),
        in0=ps[7][:, 0:2 * CPC].rearrange("p (g v) -> p g v", v=2),
        in1=bmod_sb[:].unsqueeze(2).to_broadcast([128, CPC, 2]), op=ALU.add),
        reads=[("ps", 7), "bmod"], writes=["modloc"])
    pg.dma("sp", mod_in[:, :], modloc[:].rearrange("p v g -> p (v g)"), reads=["modloc"], writes=["mod_in"],
           key="mod_io")
    pg.custom("pool", lambda e: e.collective_compute(
        "AllGather", ALU.bypass, replica_groups=[[0, 1, 2, 3], [4, 5, 6, 7]],
        ins=[mod_in[:, :]], outs=[mod_out[:, :]]), reads=["mod_in"], writes=["mod_out"], key="cc_mod")
    mov = mod_out.rearrange("(r p) n -> p r n", p=128)

    pidc = {}

    def get_pid(eng):
        if "pid" not in pidc:
            pidc["pid"] = eng.partition_id()
        return pidc["pid"]

    pg.dma("sp", modL[:].rearrange("p (r g) -> p r g", g=CPC), mov[:, :, 0:CPC], reads=["mod_out"],
           writes=["modL"], key="ld_mod", total=True)
    pg.dma("sp", modC[:].rearrange("p (r g) -> p r g", g=CPC), mov[:, :, CPC:2 * CPC], reads=["mod_out"],
           writes=["modC"], key="ld_mod", total=True)
    if debug:
        dm = dbg_out("mod", [128, 2, 9 * KD])
        pg.dma("sp", dm[:, 0, :], modL[:], reads=["modL"], key=("dbg", 1))
        pg.dma("sp", dm[:, 1, :], modC[:], reads=["modC"], key=("dbg", 2))

    for wi, m in enumerate((modL, modC)):
        mv = m[:].rearrange("p (m k) -> p m k", k=KD)
        for s_ in range(3):
            fac = 1.0 if s_ == 1 else 0.5
            pg.op("dve", (lambda e, wi=wi, mv=mv, s_=s_: e.scalar_tensor_tensor(
                out=coef[:, wi, s_, 0, :], in0=mv[:, 3 * s_ + 1, :], scalar=1.0, in1=gains_sb[:, s_, :],
                op0=ALU.add, op1=ALU.mult)), reads=["modL", "modC", "gains"], writes=["coef"])
            pg.op("dve", (lambda e, wi=wi, mv=mv, s_=s_: e.tensor_copy(
                out=coef[:, wi, s_, 1, :], in_=mv[:, 3 * s_, :])), reads=["modL", "modC"], writes=["coef"])
            pg.op("dve", (lambda e, wi=wi, mv=mv, s_=s_, fac=fac: e.tensor_scalar(
                out=coef[:, wi, s_, 2, :], in0=mv[:, 3 * s_ + 2, :], scalar1=fac, scalar2=None, op0=ALU.mult)),
                reads=["modL", "modC"], writes=["coef"])

    def norm_mod(s_, tiles, stats_done=False):
        if not stats_done:
            rms_stats(tiles)
        cnt = 0
        for (t0, n) in tiles:
            wi = 1 if t0 >= NL else 0
            for k in range(KD):
                ti = cnt % 2
                tb = ntmp[ti]
                cnt += 1
                pg.op("dve", (lambda e, tb=tb, k=k, t0=t0, n=n: e.tensor_tensor(
                    out=tb[:, 0:n], in0=hT[:, k, t0:t0 + n], in1=rb[:, t0:t0 + n], op=ALU.mult)),
                    reads=[("h", k), ("rb", t0)], writes=[("ntmp", ti)])
                pg.op("act", (lambda e, tb=tb, k=k, t0=t0, n=n, wi=wi: e.activation(
                    out=hm[:, k, t0:t0 + n], in_=tb[:, 0:n], func=AF.Identity,
                    bias=coef[:, wi, s_, 1, k:k + 1], scale=coef[:, wi, s_, 0, k:k + 1])),
                    reads=[("ntmp", ti), "coef"], writes=[("hm", k)])

    def ffn(w_i, w_o, s_, tiles):
        wiv = w_i.rearrange("(k p) n -> p k n", p=128)
        wov = w_o.rearrange("(f p) n -> p f n", p=128)
        nsl_tot = NFC // 2
        ngr = min(4, nsl_tot)
        groups = []
        a = 0
        for gi in range(ngr):
            n_ = nsl_tot // ngr + (1 if gi < nsl_tot % ngr else 0)
            groups.append((a, n_))
            a += n_
        assert max(n_ for _, n_ in groups) * 2 <= 12
        pair = 0
        oc = 0
        for (sl0, nsl) in groups:
            nf = 2 * nsl
            for sl in range(sl0, sl0 + nsl):
                s = load_slab([
                    (lambda t: t[:, 0:KD * 512].rearrange("p (k g n) -> p k g n", g=2, n=256)[:, :, 0, :],
                     wiv[:, :, sl * 256:(sl + 1) * 256]),
                    (lambda t: t[:, 0:KD * 512].rearrange("p (k g n) -> p k g n", g=2, n=256)[:, :, 1, :],
                     wiv[:, :, DFF + sl * 256:DFF + (sl + 1) * 256]),
                ])
                wv = wr[s][:, 0:KD * 512].rearrange("p (k g n) -> p k g n", g=2, n=256)
                for fi in range(2):
                    fl = (sl - sl0) * 2 + fi
                    for (t0, n) in tiles:
                        pG = 2 * (pair % 2)
                        pU = pG + 1
                        sgi = pair % 2
                        pair += 1
                        for g_, pb in ((0, pG), (1, pU)):
                            for k in range(KD):
                                pg.op("pe", (lambda e, wv=wv, k=k, g_=g_, fi=fi, pb=pb, t0=t0, n=n: e.matmul(
                                    ps[pb][:, 0:n], lhsT=wv[:, k, g_, fi * 128:(fi + 1) * 128],
                                    rhs=hm[:, k, t0:t0 + n], start=(k == 0), stop=(k == KD - 1))),
                                    reads=[("wr", s), ("hm", k)], writes=[("ps", pb)], signal=(k == KD - 1))
                        pg.op("act", (lambda e, pG=pG, sgi=sgi, n=n: e.activation(
                            out=sg[sgi][:, 0:n], in_=ps[pG][:, 0:n], func=AF.Silu)),
                            reads=[("ps", pG)], writes=[("sg", sgi)])
                        pg.op("dve", (lambda e, pU=pU, sgi=sgi, fl=fl, t0=t0, n=n: e.tensor_tensor(
                            out=act[:, fl, t0:t0 + n], in0=sg[sgi][:, 0:n], in1=ps[pU][:, 0:n], op=ALU.mult)),
                            reads=[("sg", sgi), ("ps", pU)], writes=[("act", fl)])
            f0 = 2 * sl0
            for ds_ in range(D // 512):
                s = load_slab([(lambda t, nf=nf: t[:, 0:nf * 512].rearrange("p (f n) -> p f n", n=512),
                                wov[:, f0:f0 + nf, ds_ * 512:(ds_ + 1) * 512])])
                wv = wr[s][:, 0:nf * 512].rearrange("p (f n) -> p f n", n=512)
                for dc in range(4):
                    dg = ds_ * 4 + dc
                    for (t0, n) in tiles:
                        pb = 4 + (oc % 2)
                        oc += 1
                        for fl in range(nf):
                            pg.op("pe", (lambda e, wv=wv, fl=fl, dc=dc, pb=pb, t0=t0, n=n: e.matmul(
                                ps[pb][:, 0:n], lhsT=wv[:, fl, dc * 128:(dc + 1) * 128],
                                rhs=act[:, fl, t0:t0 + n], start=(fl == 0), stop=(fl == nf - 1))),
                                reads=[("wr", s), ("act", fl)], writes=[("ps", pb)], signal=(fl == nf - 1))
                        subs = []
                        if t0 < NL:
                            subs.append((t0, min(t0 + n, NL) - t0, 0))
                        if t0 + n > NL:
                            a0 = max(t0, NL)
                            subs.append((a0, t0 + n - a0, 1))
                        for (a0, an, wi) in subs:
                            pg.op("dve", (lambda e, pb=pb, dg=dg, t0=t0, a0=a0, an=an, wi=wi: e.scalar_tensor_tensor(
                                out=hT[:, dg, a0:a0 + an], in0=ps[pb][:, a0 - t0:a0 - t0 + an],
                                scalar=coef[:, wi, s_, 2, dg:dg + 1], in1=hT[:, dg, a0:a0 + an],
                                op0=ALU.mult, op1=ALU.add)),
                                reads=[("ps", pb), "coef", ("h", dg)], writes=[("h", dg)])

    norm_mod(0, TILES3, stats_done=True)
    ffn(w1i, w1o, 0, [(0, 384), (384, 384), (768, 320)])
    if debug:
        d1 = dbg_out("h1", [D, NT])
        pg.dma("sp", d1.rearrange("(k p) t -> p k t", p=128), hT, reads=HALL, key=("dbg", 3))

    if stage >= 2:
        mixer_args = dict(locals())
        _mixer(mixer_args)

    if stage >= 3:
        norm_mod(2, TILES2)
        ffn(w2i, w2o, 2, TILES2)

    rms_stats(TILES2)
    for k in range(KD):
        for (t0, n) in TILES2:
            pg.op("dve", (lambda e, k=k, t0=t0, n=n: e.scalar_tensor_tensor(
                out=hT[:, k, t0:t0 + n], in0=hT[:, k, t0:t0 + n], scalar=gains_sb[:, 3, k:k + 1],
                in1=rb[:, t0:t0 + n], op0=ALU.mult, op1=ALU.mult)),
                reads=[("h", k), ("rb", t0), "gains"], writes=[("h", k)])
    outv = outT.rearrange("(k p) t -> p k t", p=128)
    for k0 in range(0, KD, kstep):
        pg.dma("sp", outv[:, k0:k0 + kstep, :], hT[:, k0:k0 + kstep, 0:NL],
               reads=[("h", k) for k in range(k0, k0 + kstep)], writes=[("out", k0)], key="st_out", total=True)
    pg.wait_all("sp", [("out", k0) for k0 in range(0, KD, kstep)])
    for key in list(pg.dma_cnt):
        if isinstance(key, tuple) and key[0] == "dbg":
            pg.ops["sp"].append(("wait", ("dma", key), 16 * pg.dma_cnt[key]))
    return nc, pg, es


def _mixer(a):
    from types import SimpleNamespace
    v = SimpleNamespace(**a)
    pg, nc, KD, D = v.pg, v.nc, v.KD, v.D
    H, HM, ACTA, TMPA, ps, wr = v.H, v.HM, v.ACTA, v.TMPA, v.ps, v.wr
    hT, hm, gconst, pgk = v.hT, v.hm, v.gconst, v.pgk
    TILES2, TILES3 = v.TILES2, v.TILES3
    load_slab, transfer, dma_custom = v.load_slab, v.transfer, v.dma_custom
    sq, ntmp, sg, rtmp, rb, epsc = v.sq, v.ntmp, v.sg, v.rtmp, v.rb, v.epsc
    debug = v.debug
    w_in_v = v.w_in.rearrange("(k p) n -> p k n", p=128)

    def bail():
        names = set(pg.res_w) | set(pg.res_r)
        transfer(list(names), v.HALL + v.HMALL + [("act", f) for f in range(12)]
                 + [("sq", 0), ("sq", 1), "rtmp", ("ntmp", 0), ("ntmp", 1), ("sg", 0), ("sg", 1)])
        pg.dma("sp", hT[:, :, 0:NL], v.h_spill.rearrange("p (k t) -> p k t", t=NL), reads=["hspill"], writes=v.HALL,
               key="unspill")

    mstop = v.cfg.mstop

    def hmb(off, nbytes, dt=F32):
        x = HM[:, off // 2:(off + nbytes) // 2]
        return x.bitcast(F32) if dt == F32 else x

    v.norm_mod(1, TILES3)
    pg.dma("sp", v.h_spill.rearrange("p (k t) -> p k t", t=NL), hT[:, :, 0:NL], reads=v.HALL, writes=["hspill"],
           key="spill")
    o_acc = H[:, 0:8192].rearrange("p (i c) -> p i c", c=1024)
    v_tm = H[:, 8192:12800].bitcast(BF16).rearrange("p (i c) -> p i c", c=1024)
    mixg = H[:, 12800:16896].bitcast(BF16).rearrange("p (i t) -> p i t", t=1024)
    uext = H[:, 8192:8192 + 46 * 64]
    OACC = [("oacc", i) for i in range(8)]
    VTM = [("vtm", i) for i in range(9)]
    transfer(v.HALL, OACC + VTM + ["mixg", ("qks", 2), ("qks", 3), ("attT", 2), ("attT", 3)])
    kT = ACTA[:, 0:4352].rearrange("p (h t) -> p h t", t=NT)
    qT = ACTA[:, 4352:8448].rearrange("p (h t) -> p h t", t=NL)
    k_tm = ACTA[:, 8448:13056].rearrange("p (i c) -> p i c", c=512)
    mixc = ACTA[:, 0:8192].rearrange("p (i t) -> p i t", t=1024)
    KT = [("kT", h) for h in range(4)]
    QT = [("qT", h) for h in range(4)]
    KTM = [("ktm", i) for i in range(9)]
    transfer([("act", f) for f in range(12)], KT + QT + KTM)

    PADS = []

    def slab512(col0, ncols=512):
        s = load_slab([(lambda t: t[:, 0:KD * ncols].rearrange("p (k n) -> p k n", n=ncols),
                        w_in_v[:, :, col0:col0 + ncols])])
        return s, wr[s][:, 0:KD * ncols].rearrange("p (k n) -> p k n", n=ncols)

    pair = 0
    for sl in range(4):
        s = load_slab([
            (lambda t: t[:, 0:KD * 512].rearrange("p (k g n) -> p k g n", g=2, n=256)[:, :, 0, :],
             w_in_v[:, :, OFF_GA + sl * 256:OFF_GA + (sl + 1) * 256]),
            (lambda t: t[:, 0:KD * 512].rearrange("p (k g n) -> p k g n", g=2, n=256)[:, :, 1, :],
             w_in_v[:, :, OFF_GB + sl * 256:OFF_GB + (sl + 1) * 256]),
        ])
        wv = wr[s][:, 0:KD * 512].rearrange("p (k g n) -> p k g n", g=2, n=256)
        for fi in range(2):
            cc = sl * 2 + fi
            for (t0, n) in TILES2:
                pA = 2 * (pair % 2)
                pB = pA + 1
                bi = pair % 2
                pair += 1
                for g_, pb in ((0, pA), (1, pB)):
                    for k in range(KD):
                        pg.op("pe", (lambda e, wv=wv, k=k, g_=g_, fi=fi, pb=pb, t0=t0, n=n: e.matmul(
                            ps[pb][:, 0:n], lhsT=wv[:, k, g_, fi * 128:(fi + 1) * 128], rhs=hm[:, k, t0:t0 + n],
                            start=(k == 0), stop=(k == KD - 1))),
                            reads=[("wr", s), ("hm", k)], writes=[("ps", pb)], signal=(k == KD - 1))
                pg.op("act", (lambda e, pB=pB, bi=bi, n=n: e.activation(
                    out=sg[bi][:, 0:n], in_=ps[pB][:, 0:n], func=AF.Sigmoid)),
                    reads=[("ps", pB)], writes=[("sg", bi)])
                pg.op("dve", (lambda e, pA=pA, bi=bi, n=n: e.tensor_tensor(
                    out=ntmp[bi][:, 0:n], in0=sg[bi][:, 0:n], in1=ps[pA][:, 0:n], op=ALU.mult)),
                    reads=[("sg", bi), ("ps", pA)], writes=[("ntmp", bi)])
                c4 = cc % 4
                if cc < 4:
                    dst = v.u_row[c4 * 128:(c4 + 1) * 128, t0:t0 + n]
                else:
                    dst = v.ucol_in[c4 // 2][(c4 % 2) * 128:(c4 % 2 + 1) * 128, t0:t0 + n]
                pg.dma("sp", dst, ntmp[bi][:, 0:n], reads=[("ntmp", bi)],
                       writes=[("udram", cc, t0)], key=("ust", bi))
    UCOL = [("udram", cc, t0) for cc in range(4, 8) for (t0, n) in TILES2]
    UROW = [("udram", cc, t0) for cc in range(4) for (t0, n) in TILES2]
    if mstop <= 1:
        return bail()
    cnt = 0
    for sl in range(2):
        s, wv = slab512(OFF_G + sl * 512)
        for ti in range(8):
            pb = 4 + (cnt % 2)
            bi = cnt % 2
            cnt += 1
            for k in range(KD):
                pg.op("pe", (lambda e, wv=wv, k=k, pb=pb, ti=ti: e.matmul(
                    ps[pb][:, :], lhsT=hm[:, k, ti * 128:(ti + 1) * 128], rhs=wv[:, k, :],
                    start=(k == 0), stop=(k == KD - 1))),
                    reads=[("wr", s), ("hm", k)], writes=[("ps", pb)], signal=(k == KD - 1))
            pg.op("act", (lambda e, pb=pb, bi=bi: e.activation(out=sq[bi][:, :], in_=ps[pb][:, :], func=AF.Silu)),
                  reads=[("ps", pb)], writes=[("sq", bi)])
            pg.dma("sp", v.sg_spill[ti * 128:(ti + 1) * 128, sl * 512:(sl + 1) * 512], sq[bi][:, :],
                   reads=[("sq", bi)], writes=[("sgsp", ti, sl)], key=("sgst", bi))

    for sl in range(2):
        s, wv = slab512(OFF_V + sl * 512)
        for ti in range(9):
            np_ = 128 if ti < 8 else 64
            pb = 4 + (cnt % 2)
            cnt += 1
            for k in range(KD):
                pg.op("pe", (lambda e, wv=wv, k=k, pb=pb, ti=ti, np_=np_: e.matmul(
                    ps[pb][0:np_, :], lhsT=hm[:, k, ti * 128:ti * 128 + np_], rhs=wv[:, k, :],
                    start=(k == 0), stop=(k == KD - 1))),
                    reads=[("wr", s), ("hm", k)], writes=[("ps", pb)], signal=(k == KD - 1))
            pg.op("act", (lambda e, pb=pb, ti=ti, sl=sl, np_=np_: e.copy(
                out=v_tm[0:np_, ti, sl * 512:(sl + 1) * 512], in_=ps[pb][0:np_, :])),
                reads=[("ps", pb)], writes=[("vtm", ti)])

    s, wv = slab512(OFF_K)
    for ti in range(9):
        np_ = 128 if ti < 8 else 64
        pb = 4 + (cnt % 2)
        cnt += 1
        for k in range(KD):
            pg.op("pe", (lambda e, wv=wv, k=k, pb=pb, ti=ti, np_=np_: e.matmul(
                ps[pb][0:np_, :], lhsT=hm[:, k, ti * 128:ti * 128 + np_], rhs=wv[:, k, :],
                start=(k == 0), stop=(k == KD - 1))),
                reads=[("wr", s), ("hm", k)], writes=[("ps", pb)], signal=(k == KD - 1))
        pg.op("dve", (lambda e, pb=pb, ti=ti, np_=np_: e.tensor_copy(out=k_tm[0:np_, ti, :], in_=ps[pb][0:np_, :])),
              reads=[("ps", pb)], writes=[("ktm", ti)])
    for h in range(4):
        for (t0, n) in TILES3:
            pb = 4 + (cnt % 2)
            cnt += 1
            for k in range(KD):
                pg.op("pe", (lambda e, wv=wv, k=k, pb=pb, h=h, t0=t0, n=n: e.matmul(
                    ps[pb][:, 0:n], lhsT=wv[:, k, h * 128:(h + 1) * 128], rhs=hm[:, k, t0:t0 + n],
                    start=(k == 0), stop=(k == KD - 1))),
                    reads=[("wr", s), ("hm", k)], writes=[("ps", pb)], signal=(k == KD - 1))
            pg.op("act", (lambda e, pb=pb, h=h, t0=t0, n=n: e.copy(out=kT[:, h, t0:t0 + n], in_=ps[pb][:, 0:n])),
                  reads=[("ps", pb)], writes=[("kT", h)])
    s, wv = slab512(OFF_Q)
    for h in range(4):
        for (t0, n) in TILES2:
            pb = 4 + (cnt % 2)
            cnt += 1
            for k in range(KD):
                pg.op("pe", (lambda e, wv=wv, k=k, pb=pb, h=h, t0=t0, n=n: e.matmul(
                    ps[pb][:, 0:n], lhsT=wv[:, k, h * 128:(h + 1) * 128], rhs=hm[:, k, t0:t0 + n],
                    start=(k == 0), stop=(k == KD - 1))),
                    reads=[("wr", s), ("hm", k)], writes=[("ps", pb)], signal=(k == KD - 1))
            pg.op("act", (lambda e, pb=pb, h=h, t0=t0, n=n: e.mul(out=qT[:, h, t0:t0 + n], in_=ps[pb][:, 0:n],
                                                                 mul=float(128 ** -0.5))),
                  reads=[("ps", pb)], writes=[("qT", h)])
    s, wv = slab512(OFF_GKF, 32)
    for d in range(2):
        for (t0, n) in TILES3:
            pb = 4 + (cnt % 2)
            cnt += 1
            for k in range(KD):
                pg.op("pe", (lambda e, wv=wv, k=k, pb=pb, d=d, t0=t0, n=n: e.matmul(
                    ps[pb][0:16, 0:n], lhsT=wv[:, k, d * 16:(d + 1) * 16], rhs=hm[:, k, t0:t0 + n],
                    start=(k == 0), stop=(k == KD - 1))),
                    reads=[("wr", s), ("hm", k)], writes=[("ps", pb)], signal=(k == KD - 1))
            pg.op("dve", (lambda e, pb=pb, d=d, t0=t0, n=n: e.tensor_copy(out=pgk[0:16, d, t0:t0 + n],
                                                                        in_=ps[pb][0:16, 0:n])),
                  reads=[("ps", pb)], writes=["pgk"])

    for i_ in range(2):
        pg.custom("pool", (lambda e, i_=i_: e.collective_compute(
            "AllGather", ALU.bypass, replica_groups=[[0, 1, 2, 3], [4, 5, 6, 7]],
            ins=[v.ucol_in[i_][:, :]], outs=[v.ucol_g[i_][:, :]])), reads=UCOL, writes=[("ucolg", i_)],
            key=("cc_u", i_))


    if mstop <= 2:
        return bail()
    st_loc = hmb(0, 4 * EW * 4).rearrange("p (e w) -> p e w", w=EW)
    Sbf = HM[:, 8224:8224 + 2048].rearrange("p (d h e) -> p d h e", d=2, h=4)
    off = 20544
    spb = hmb(off, 2048); off += 2048
    edb = hmb(off, 2048); off += 2048
    kdb = HM[:, off // 2:off // 2 + 512]; off += 1024
    decb = hmb(off, 32); off += 32
    ebe = [hmb(off + i * 1024, 1024) for i in range(2)]; off += 2048
    e1n = [hmb(off + i * 512, 512) for i in range(2)]; off += 1024
    qks = [[HM[:, (off + (i * 3 + j) * 256) // 2:(off + (i * 3 + j) * 256) // 2 + 128] for j in range(3)]
           for i in range(2)]; off += 1536
    attT = [HM[:, (off + i * 256) // 2:(off + i * 256) // 2 + 128] for i in range(2)]; off += 512
    assert off <= 34816
    ebe += [rb[:, 0:256], rb[:, 256:512]]
    e1n += [rb[:, 512:640], rb[:, 640:768]]
    HT_ = H[:, 16896:17408].bitcast(BF16)
    qks += [[HT_[:, (i * 3 + j) * 128:(i * 3 + j + 1) * 128] for j in range(3)] for i in range(2)]
    attT += [HT_[:, 768 + i * 128:768 + (i + 1) * 128] for i in range(2)]
    transfer([("rb", 0), ("rb", 512), ("rb", 1024)], [("ebe", 2), ("ebe", 3), ("e1n", 2), ("e1n", 3)])
    og = TMPA[:, 0:1024]
    sgt = TMPA[:, 1024:1536].bitcast(BF16)
    ebuf = TMPA[:, 1536:1536 + EW]
    t1 = TMPA[:, 2564:2820]
    ssb = TMPA[:, 2820:2828]
    GT = [("stloc", 0), ("stloc", 1)] + [(n_, d_, h_) for n_ in ("S", "Sbf") for d_ in range(2) for h_ in range(4)] + [ "spb", "edb", "kdb", "decb", ("ebe", 0), ("ebe", 1), ("e1n", 0), ("e1n", 1),
          ("qks", 0), ("qks", 1), ("attT", 0), ("attT", 1)]
    transfer(v.HMALL, GT)
    TM = ["og", "sgt", "ebuf", "t1", "ssb"]
    transfer([("sq", 0), ("sq", 1), "rtmp", ("ntmp", 0), ("ntmp", 1), ("sg", 0), ("sg", 1)], TM)

    def Sv(d):
        return st_loc[:, 2 * d + 1, 0:1024].rearrange("p (h e) -> p h e", e=256)

    def Sx(d):
        return st_loc[:, 2 * d, 0:1024].rearrange("p (h e) -> p h e", e=256)

    pg.op("dve", lambda e: e.memset(st_loc[:, :, 0:1024], 0.0), writes=[("stloc", 0), ("stloc", 1)])
    pg.op("dve", lambda e: e.memset(st_loc[:, :, 1024:1028], 1.0), writes=[("stloc", 0), ("stloc", 1)])

    Mtb = lambda d: gconst[:, GC_MTB + d * 128:GC_MTB + (d + 1) * 128]
    CE = lambda d: gconst[:, GC_CE + d * 256:GC_CE + (d + 1) * 256]
    CH = gconst[:, GC_CH:GC_CH + 2]
    MASK = lambda d: gconst[:, GC_MASK + d * 128:GC_MASK + (d + 1) * 128]
    IDM = gconst[:, GC_ID:GC_ID + 128]
    ONEF = gconst[:, GC_ONE:GC_ONE + 128]
    hcnt = {"n": 0}

    def gla_tile(ti, d, phase):
        np_ = 128 if ti < 8 else 64
        tok0 = ti * 128
        pg.op("pe", lambda e: e.matmul(ps[0][0:np_, :], lhsT=pgk[0:17, d, tok0:tok0 + np_], rhs=v.w2aug[0:17, d, :],
                                       start=True, stop=True), reads=["pgk", "w2aug"], writes=[("ps", 0)])
        pg.op("act", lambda e: e.activation(out=spb[0:np_, :], in_=ps[0][0:np_, :], func=AF.Exp, scale=-1.0),
              reads=[("ps", 0)], writes=["spb"])
        pg.op("act", lambda e: e.activation(out=spb[0:np_, :], in_=spb[0:np_, :], func=AF.Ln, bias=epsc[0:np_, 2:3],
                                            scale=1.0), reads=["spb", "epsc"], writes=["spb"])
        pg.op("pe", lambda e: e.matmul(ps[1][0:np_, :], lhsT=Mtb(d)[0:np_, 0:np_], rhs=spb[0:np_, :],
                                       start=True, stop=True), reads=["spb", "gconst"], writes=[("ps", 1)])
        pg.op("act", lambda e: e.activation(out=edb[0:np_, :], in_=ps[1][0:np_, :], func=AF.Exp),
              reads=[("ps", 1)], writes=["edb"])
        pg.op("dve", lambda e: e.tensor_tensor(out=kdb[0:np_, :], in0=k_tm[0:np_, ti, :], in1=edb[0:np_, :],
                                               op=ALU.mult), reads=[("ktm", ti), "edb"], writes=["kdb"])
        chunks = [0, 1] if d == 0 else [1, 0]
        if ti == 8:
            chunks = [0]
        if phase == "A":
            for h in range(4):
                pg.op("pe", (lambda e, h=h: e.matmul(ps[3][:, 2 * h:2 * h + 2], lhsT=spb[0:np_, h * 128:(h + 1) * 128],
                                                     rhs=CH[0:np_, :], start=True, stop=True)),
                      reads=["spb", "gconst"], writes=[("ps", 3)], signal=(h == 3))
            pg.op("act", lambda e: e.activation(out=decb[:, 0:8], in_=ps[3][:, 0:8], func=AF.Exp),
                  reads=[("ps", 3)], writes=["decb"])
            dview = decb[:, 0:8].rearrange("p (h x) -> p h x", x=2)
            for X in chunks:
                r0 = X * 64
                for h in range(4):
                    pb = 6 + (h // 2)
                    c0 = (h % 2) * 256
                    pg.op("pe", (lambda e, h=h, pb=pb, c0=c0, r0=r0: e.matmul(
                        ps[pb][:, c0:c0 + 256], lhsT=kdb[r0:r0 + 64, h * 128:(h + 1) * 128],
                        rhs=v_tm[r0:r0 + 64, ti, h * 256:(h + 1) * 256], start=True, stop=True)),
                        reads=["kdb", ("vtm", ti)], writes=[("ps", pb)])
                    if ti < 8:
                        pg.op("dve", (lambda e, h=h, pb=pb, c0=c0, X=X: e.scalar_tensor_tensor(
                            out=Sv(d)[:, h, :], in0=Sv(d)[:, h, :], scalar=decb[:, 2 * h + X:2 * h + X + 1],
                            in1=ps[pb][:, c0:c0 + 256], op0=ALU.mult, op1=ALU.add)),
                            reads=[("ps", pb), "decb", ("stloc", d)], writes=[("stloc", d)])
                    else:
                        pg.op("dve", (lambda e, h=h, pb=pb, c0=c0: e.tensor_copy(
                            out=Sx(d)[:, h, :], in_=ps[pb][:, c0:c0 + 256])),
                            reads=[("ps", pb)], writes=[("stloc", d)])
                if ti < 8:
                    pg.op("dve", (lambda e, X=X: e.tensor_tensor(
                        out=st_loc[:, 2 * d + 1, 1024:1028], in0=st_loc[:, 2 * d + 1, 1024:1028],
                        in1=dview[:, :, X], op=ALU.mult)), reads=["decb", ("stloc", d)], writes=[("stloc", d)])
                else:
                    pg.op("dve", (lambda e, X=X: e.tensor_copy(out=st_loc[:, 2 * d, 1024:1028], in_=dview[:, :, X])),
                          reads=["decb"], writes=[("stloc", d)])
            return
        S = Sv(d)
        for h in (0, 2, 1, 3):
            pbe = 3 + (h // 2)
            cb = (h % 2) * 256
            pg.op("pe", (lambda e, h=h, pbe=pbe, cb=cb: e.matmul(
                ps[pbe][:, cb:cb + 256], lhsT=spb[:, h * 128:(h + 1) * 128], rhs=CE(d), start=True, stop=True)),
                reads=["spb", "gconst"], writes=[("ps", pbe)])
            pg.op("act", (lambda e, h=h, pbe=pbe, cb=cb: e.activation(
                out=ebe[h][:, :], in_=ps[pbe][:, cb:cb + 256], func=AF.Exp)),
                reads=[("ps", pbe)], writes=[("ebe", h)])
            pg.op("act", (lambda e, h=h, pbe=pbe, cb=cb: e.activation(
                out=e1n[h][:, :], in_=ps[pbe][:, cb + 128:cb + 256], func=AF.Exp, scale=-1.0)),
                reads=[("ps", pbe)], writes=[("e1n", h)])
        for h in range(4):
            qb, qs, ks = qks[h]
            pg.op("dve", (lambda e, h=h, qb=qb: e.tensor_tensor(
                out=qb[:, :], in0=qT[:, h, tok0:tok0 + 128], in1=ebe[h][:, 0:128], op=ALU.mult)),
                reads=[("qT", h), ("ebe", h)], writes=[("qks", h)])
            pg.op("dve", (lambda e, h=h, qs=qs: e.tensor_tensor(
                out=qs[:, :], in0=qT[:, h, tok0:tok0 + 128], in1=ebe[h][:, 128:256], op=ALU.mult)),
                reads=[("qT", h), ("ebe", h)], writes=[("qks", h)])
            pg.op("dve", (lambda e, h=h, ks=ks: e.tensor_tensor(
                out=ks[:, :], in0=kT[:, h, tok0:tok0 + 128], in1=e1n[h][:, :], op=ALU.mult)),
                reads=[("kT", h), ("e1n", h)], writes=[("qks", h)])
            pg.op("pe", (lambda e, h=h, ks=ks, qs=qs: e.matmul(
                ps[5][:, h * 128:(h + 1) * 128], lhsT=ks[:, :], rhs=qs[:, :], start=True, stop=True)),
                reads=[("qks", h)], writes=[("ps", 5)])
            pg.op("dve", (lambda e, h=h: e.tensor_tensor(
                out=attT[h][:, :], in0=ps[5][:, h * 128:(h + 1) * 128], in1=MASK(d), op=ALU.mult)),
                reads=[("ps", 5), "gconst"], writes=[("attT", h)])
        for xi, X in enumerate(chunks):
            r0 = X * 64
            for h in range(4):
                qb = qks[h][0]
                po = 6 + (h // 2)
                co = (h % 2) * 256
                kc = (h % 2) * 256
                if xi == 0:
                    pg.op("pe", (lambda e, h=h, po=po, co=co: e.matmul(
                        ps[po][:, co:co + 256], lhsT=attT[h][:, :], rhs=v_tm[:, ti, h * 256:(h + 1) * 256],
                        start=(h % 2 == 0), stop=False, skip_group_check=True)),
                        reads=[("attT", h), ("vtm", ti)], writes=[("ps", po)])
                pg.op("pe", (lambda e, h=h, qb=qb, po=po, co=co, r0=r0, xi=xi: e.matmul(
                    ps[po][r0:r0 + 64, co:co + 256], lhsT=qb[:, r0:r0 + 64], rhs=Sbf[:, d, h, :],
                    start=False, stop=(xi == 1), skip_group_check=True)),
                    reads=[("qks", h), ("Sbf", d, h)], writes=[("ps", po)])
                pg.op("pe", (lambda e, h=h, r0=r0, kc=kc: e.matmul(
                    ps[2][:, kc:kc + 256], lhsT=kdb[r0:r0 + 64, h * 128:(h + 1) * 128],
                    rhs=v_tm[r0:r0 + 64, ti, h * 256:(h + 1) * 256], start=True, stop=True)),
                    reads=["kdb", ("vtm", ti)], writes=[("ps", 2)])
                col = (r0 + 63) if d == 0 else r0
                pg.op("dve", (lambda e, h=h, kc=kc, col=col: e.scalar_tensor_tensor(
                    out=S[:, h, :], in0=S[:, h, :], scalar=ebe[h][:, col:col + 1], in1=ps[2][:, kc:kc + 256],
                    op0=ALU.mult, op1=ALU.add)), reads=[("ps", 2), ("ebe", h), ("S", d, h)], writes=[("S", d, h)])
                pg.op("act", (lambda e, h=h: e.copy(out=Sbf[:, d, h, :], in_=S[:, h, :])),
                      reads=[("S", d, h)], writes=[("Sbf", d, h)])

    def phase_a(d):
        order = list(range(8)) if d == 0 else list(range(7, -1, -1))
        for ti in order + [8]:
            gla_tile(ti, d, "A")

    def exchange(d):
        for e_ in (2 * d, 2 * d + 1):
            pg.dma("sp", v.st_in[e_][:, :], st_loc[:, e_, :], reads=[("stloc", d)], writes=[("st_in", e_)],
                   key=("st_io", e_))
        for e_ in (2 * d, 2 * d + 1):
            pg.custom("pool", (lambda e, e_=e_: e.collective_compute(
                "AllGather", ALU.bypass, replica_groups=[[0, 1, 2, 3], [4, 5, 6, 7]],
                ins=[v.st_in[e_][:, :]], outs=[v.st_out[e_][:, :]])), reads=[("st_in", e_)],
                writes=[("st_out", e_)], key=("cc_st", e_))

    def combine(d):
        S = Sv(d)
        pg.op("dve", (lambda e, S=S: e.memset(S[:, :, :], 0.0)), reads=[("stloc", d)], writes=[("stloc", d)])
        ctx_order = [0, 1, 2, 3] if d == 0 else [3, 2, 1, 0]
        seg_order = [0, 1, 2] if d == 0 else [3, 2, 1]
        for kind, order in ((0, ctx_order), (1, seg_order)):
            for r in order:
                e_ = 2 * d + kind
                pg.dma("sp", ebuf[:, :], v.st_out[e_][r * 128:(r + 1) * 128, :], reads=[("st_out", e_)],
                       writes=["ebuf"], key="ld_st")
                if kind == 0:
                    for h in range(4):
                        pg.op("dve", (lambda e, h=h, S=S: e.scalar_tensor_tensor(
                            out=S[:, h, :], in0=S[:, h, :], scalar=ebuf[:, 1024 + h:1025 + h],
                            in1=ebuf[:, h * 256:(h + 1) * 256], op0=ALU.mult, op1=ALU.add)),
                            reads=["ebuf", ("stloc", d)], writes=[("stloc", d)])
                else:
                    mcol = (0 if d == 0 else 8) + r
                    pg.op("dve", (lambda e, mcol=mcol: e.tensor_scalar(
                        out=t1[:, 0:4], in0=ebuf[:, 1024:1028], scalar1=v.segmask[:, mcol:mcol + 1],
                        scalar2=v.segmask[:, mcol + 4:mcol + 5], op0=ALU.mult, op1=ALU.add)),
                        reads=["ebuf", "segmask"], writes=["t1"])
                    pg.op("dve", (lambda e, mcol=mcol: e.tensor_scalar(
                        out=og[:, :], in0=ebuf[:, 0:1024], scalar1=v.segmask[:, mcol:mcol + 1], scalar2=None,
                        op0=ALU.mult)), reads=["ebuf", "segmask"], writes=["og"])
                    for h in range(4):
                        pg.op("dve", (lambda e, h=h, S=S: e.scalar_tensor_tensor(
                            out=S[:, h, :], in0=S[:, h, :], scalar=t1[:, h:h + 1], in1=og[:, h * 256:(h + 1) * 256],
                            op0=ALU.mult, op1=ALU.add)), reads=["t1", "og", ("stloc", d)], writes=[("stloc", d)])
        transfer([("stloc", d)], [("S", d, h) for h in range(4)])
        for h in range(4):
            pg.res_w[("S", d, h)] = pg.res_w[("stloc", d)]
        for h in range(4):
            pg.op("act", (lambda e, S=S, d=d, h=h: e.copy(out=Sbf[:, d, h, :], in_=S[:, h, :])),
                  reads=[("S", d, h)], writes=[("Sbf", d, h)])

    phase_a(0)
    exchange(0)
    phase_a(1)
    exchange(1)
    combine(0)
    if mstop <= 3:
        return bail()
    for ti in range(8):
        gla_tile(ti, 0, "C")
        for half in range(2):
            pg.op("act", (lambda e, ti=ti, half=half: e.copy(out=o_acc[:, ti, half * 512:(half + 1) * 512],
                                                             in_=ps[6 + half][:, :])),
                  reads=[("ps", 6 + half)], writes=[("oacc", ti)])
    combine(1)
    for ti in range(7, -1, -1):
        gla_tile(ti, 1, "C")
        for half in range(2):
            pg.op("dve", (lambda e, ti=ti, half=half: e.tensor_tensor(
                out=o_acc[:, ti, half * 512:(half + 1) * 512], in0=o_acc[:, ti, half * 512:(half + 1) * 512],
                in1=ps[6 + half][:, :], op=ALU.add)), reads=[("ps", 6 + half), ("oacc", ti)], writes=[("oacc", ti)])
        pg.dma("sp", sgt[:, :], v.sg_spill[ti * 128:(ti + 1) * 128, :],
               reads=[("sgsp", ti, 0), ("sgsp", ti, 1)], writes=["sgt"], key="ld_sg")
        for h in range(4):
            pg.op("act", (lambda e, ti=ti, h=h: e.activation(
                out=t1[:, :], in_=o_acc[:, ti, h * 256:(h + 1) * 256], func=AF.Square,
                accum_out=ssb[:, h:h + 1])), reads=[("oacc", ti)], writes=["t1", "ssb"])
        pg.op("act", lambda e: e.activation(out=ssb[:, 4:8], in_=ssb[:, 0:4], func=AF.Sqrt, bias=epsc[:, 1:2],
                                            scale=1.0 / 256.0), reads=["ssb", "epsc"], writes=["ssb"])
        pg.op("dve", lambda e: e.reciprocal(out=ssb[:, 4:8], in_=ssb[:, 4:8]), reads=["ssb"], writes=["ssb"])
        for h in range(4):
            pg.op("dve", (lambda e, ti=ti, h=h: e.scalar_tensor_tensor(
                out=t1[:, :], in0=o_acc[:, ti, h * 256:(h + 1) * 256], scalar=ssb[:, 4 + h:5 + h], in1=v.gnb[:, :],
                op0=ALU.mult, op1=ALU.mult)), reads=[("oacc", ti), "ssb", "gnb"], writes=["t1"])
            pg.op("dve", (lambda e, h=h: e.tensor_tensor(
                out=og[:, h * 256:(h + 1) * 256], in0=t1[:, :], in1=sgt[:, h * 256:(h + 1) * 256], op=ALU.mult)),
                reads=["t1", "sgt"], writes=["og"])
        for half in range(2):
            pt = 5 if half == 0 else 2
            for b4 in range(4):
                blk = half * 4 + b4
                pg.op("pe", (lambda e, pt=pt, b4=b4, blk=blk: e.transpose(
                    out=ps[pt][:, b4 * 128:(b4 + 1) * 128], in_=og[:, blk * 128:(blk + 1) * 128], identity=IDM)),
                    reads=["og", "gconst"], writes=[("ps", pt)], signal=(b4 == 3))
            pg.op("act", (lambda e, pt=pt, half=half, ti=ti: e.copy(
                out=mixg[:, half * 4:(half + 1) * 4, ti * 128:(ti + 1) * 128],
                in_=ps[pt][:, :].rearrange("p (b t) -> p b t", t=128))), reads=[("ps", pt)], writes=["mixg"])
    if debug:
        dd = v.dbg_out("mixg", [1024, NL], BF16)
        pg.dma("sp", dd.rearrange("(i p) t -> p i t", p=128), mixg, reads=["mixg"], key=("dbg", 6))

    if mstop <= 4:
        return bail()
    y = hmb(0, 32768).rearrange("p (c t) -> p c t", t=1024)
    Y = [("y", c) for c in range(8)]
    transfer(GT, Y)
    transfer(VTM, ["uext"])
    transfer(KT + QT + KTM, ["mixc"])
    for cc in range(8):
        yv = y[:, cc, :]
        if cc < 4:
            pg.dma("sp", uext[:, 0:1024], v.u_row[cc * 128:(cc + 1) * 128, :], reads=UROW, writes=["uext"],
                   key="ld_u")
            ctr = uext[:, 0:1024]
        else:
            c4 = cc - 4

            def mk(which, c4=c4):
                UG = v.ucol_g[c4 // 2]
                ro = (c4 % 2) * 128

                def f(eng):
                    pid = v.get_pid(eng)
                    myr = pid % 4
                    if which == 0:
                        return eng.dma_start(out=uext[:, 0:960],
                                             in_=UG[bass.ds(((myr + 3) % 4) * 256 + ro, 128), 64:1024])
                    if which == 1:
                        return eng.dma_start(out=uext[:, 960:1984],
                                             in_=UG[bass.ds(myr * 256 + ro, 128), 0:1024])
                    return eng.dma_start(out=uext[:, 1984:2944],
                                         in_=UG[bass.ds(((myr + 1) % 4) * 256 + ro, 128), 0:960])
                return f
            for which in range(3):
                dma_custom("sp", mk(which), [("ucolg", c4 // 2)], ["uext"], "ld_uc")
            pg.op("dve", lambda e: e.tensor_scalar(out=uext[:, 0:960], in0=uext[:, 0:960], scalar1=v.segmask[:, 0:1],
                                                   scalar2=None, op0=ALU.mult), reads=["uext", "segmask"],
                  writes=["uext"])
            pg.op("dve", lambda e: e.tensor_scalar(out=uext[:, 1984:2944], in0=uext[:, 1984:2944],
                                                   scalar1=v.segmask[:, 11:12], scalar2=None, op0=ALU.mult),
                  reads=["uext", "segmask"], writes=["uext"])
            ctr = uext[:, 960:1984]
        pg.op("dve", (lambda e, cc=cc, ctr=ctr, yv=yv: e.tensor_scalar(
            out=yv, in0=ctr, scalar1=v.convw[:, cc, 15:16], scalar2=v.convp[:, cc, 0:1], op0=ALU.mult, op1=ALU.add)),
            reads=["uext", "convw", "convp"], writes=[("y", cc)])
        for j in range(31):
            if j == 15:
                continue
            s_ = j - 15
            if cc < 4:
                c0, c1 = max(0, -s_), min(64, 64 - s_)
                y3 = yv.rearrange("p (r c) -> p r c", c=64)[:, :, c0:c1]
                u3 = uext[:, 0:1024].rearrange("p (r c) -> p r c", c=64)[:, :, c0 + s_:c1 + s_]
            else:
                y3 = yv
                u3 = uext[:, j * 64:j * 64 + 1024]
            pg.op("dve", (lambda e, cc=cc, j=j, y3=y3, u3=u3: e.scalar_tensor_tensor(
                out=y3, in0=u3, scalar=v.convw[:, cc, j:j + 1], in1=y3, op0=ALU.mult, op1=ALU.add)),
                reads=["uext", "convw", ("y", cc)], writes=[("y", cc)])
    if debug:
        dd = v.dbg_out("y", [1024, NL])
        pg.dma("sp", dd.rearrange("(i p) t -> p i t", p=128), y, reads=Y, key=("dbg", 7))
    if mstop <= 5:
        return bail()
    transfer(TM, [("sq", 0), ("sq", 1), "rtmp", ("ntmp", 0), ("ntmp", 1), ("sg", 0), ("sg", 1)])
    transfer([("ebe", 2), ("ebe", 3), ("e1n", 2), ("e1n", 3)], [("rb", 0), ("rb", 512), ("rb", 1024)])
    cnt = 0
    for (t0, n) in TILES2:
        for cc in range(8):
            pg.op("pe", (lambda e, cc=cc, t0=t0, n=n: e.matmul(ps[0][:, 0:n], lhsT=ONEF, rhs=y[:, cc, t0:t0 + n],
                                                             start=(cc == 0), stop=(cc == 7))),
                  reads=[("y", cc), "gconst"], writes=[("ps", 0)], signal=(cc == 7))
        for cc in range(8):
            pg.op("dve", (lambda e, cc=cc, t0=t0, n=n: e.tensor_tensor(
                out=y[:, cc, t0:t0 + n], in0=y[:, cc, t0:t0 + n], in1=ps[0][:, 0:n], op=ALU.subtract)),
                reads=[("ps", 0), ("y", cc)], writes=[("y", cc)])
        for cc in range(8):
            bi = cnt % 2
            cnt += 1
            pg.op("act", (lambda e, cc=cc, bi=bi, t0=t0, n=n: e.activation(
                out=ntmp[bi][:, 0:n], in_=y[:, cc, t0:t0 + n], func=AF.Square)),
                reads=[("y", cc)], writes=[("ntmp", bi)])
            pg.op("pe", (lambda e, cc=cc, bi=bi, n=n: e.matmul(ps[1][:, 0:n], lhsT=ONEF, rhs=ntmp[bi][:, 0:n],
                                                             start=(cc == 0), stop=(cc == 7))),
                  reads=[("ntmp", bi), "gconst"], writes=[("ps", 1)], signal=True)
        pg.op("act", (lambda e, n=n: e.activation(out=rtmp[:, 0:n], in_=ps[1][:, 0:n], func=AF.Sqrt,
                                                  bias=epsc[:, 1:2], scale=1.0)),
              reads=[("ps", 1), "epsc"], writes=["rtmp"])
        pg.op("dve", (lambda e, t0=t0, n=n: e.reciprocal(out=rb[:, t0:t0 + n], in_=rtmp[:, 0:n])),
              reads=["rtmp"], writes=[("rb", t0)])
        for cc in range(8):
            bi = cnt % 2
            cnt += 1
            pg.op("dve", (lambda e, cc=cc, bi=bi, t0=t0, n=n: e.tensor_tensor(
                out=ntmp[bi][:, 0:n], in0=y[:, cc, t0:t0 + n], in1=rb[:, t0:t0 + n], op=ALU.mult)),
                reads=[("y", cc), ("rb", t0)], writes=[("ntmp", bi)])
            pg.op("act", (lambda e, cc=cc, bi=bi, t0=t0, n=n: e.activation(
                out=mixc[:, cc, t0:t0 + n], in_=ntmp[bi][:, 0:n], func=AF.Silu,
                bias=v.convp[:, cc, 2:3], scale=v.convp[:, cc, 1:2])),
                reads=[("ntmp", bi), "convp"], writes=["mixc"])
    if debug:
        dd = v.dbg_out("mixc", [1024, NL], BF16)
        pg.dma("sp", dd.rearrange("(i p) t -> p i t", p=128), mixc, reads=["mixc"], key=("dbg", 8))

    mixg2 = HM[:, 0:8192].rearrange("p (i t) -> p i t", t=1024)
    transfer(Y, ["mixg2"])
    pg.op("act", lambda e: e.copy(out=mixg2[:, :, :], in_=mixg[:, :, :]), reads=["mixg"], writes=["mixg2"])
    transfer(OACC + ["uext", "mixg", ("qks", 2), ("qks", 3), ("attT", 2), ("attT", 3)], v.HALL)
    pg.dma("sp", hT[:, :, 0:NL], v.h_spill.rearrange("p (k t) -> p k t", t=NL), reads=["hspill"], writes=v.HALL,
           key="unspill")

    wov = v.w_out.rearrange("(i p) n -> p i n", p=128)
    oc = 0
    for ds_ in range(D // 512):
        s = load_slab([(lambda t: t[:, 0:8192].rearrange("p (i n) -> p i n", n=512),
                        wov[:, :, ds_ * 512:(ds_ + 1) * 512])])
        wv = wr[s][:, 0:8192].rearrange("p (i n) -> p i n", n=512)
        for dc in range(4):
            dg = ds_ * 4 + dc
            for (t0, n) in TILES2:
                pb = 4 + (oc % 2)
                oc += 1
                for i in range(16):
                    src = mixg2 if i < 8 else mixc
                    rn = "mixg2" if i < 8 else "mixc"
                    pg.op("pe", (lambda e, wv=wv, i=i, dc=dc, pb=pb, t0=t0, n=n, src=src: e.matmul(
                        ps[pb][:, 0:n], lhsT=wv[:, i, dc * 128:(dc + 1) * 128], rhs=src[:, i % 8, t0:t0 + n],
                        start=(i == 0), stop=(i == 15))),
                        reads=[("wr", s), rn], writes=[("ps", pb)], signal=(i == 15))
                pg.op("dve", (lambda e, pb=pb, dg=dg, t0=t0, n=n: e.scalar_tensor_tensor(
                    out=hT[:, dg, t0:t0 + n], in0=ps[pb][:, 0:n], scalar=v.coef[:, 0, 1, 2, dg:dg + 1],
                    in1=hT[:, dg, t0:t0 + n], op0=ALU.mult, op1=ALU.add)),
                    reads=[("ps", pb), "coef", ("h", dg)], writes=[("h", dg)])
    transfer(["mixg2"], v.HMALL)
    transfer(["mixc"], [("act", f) for f in range(12)])
    if debug:
        d2 = v.dbg_out("h2", [D, NL])
        pg.dma("sp", d2.rearrange("(k p) t -> p k t", p=128), hT[:, :, 0:NL], reads=v.HALL, key=("dbg", 9))


def _emit(pg, es):
    nc = pg.nc
    sems = {}
    for e in pg.ENGS:
        sems[e] = es.enter_context(nc.semaphore("s_" + e))
    for i, key in enumerate(sorted(pg.dma_cnt, key=str)):
        sems[("dma", key)] = es.enter_context(nc.semaphore("d%d" % i))
    for e in pg.ENGS:
        assert pg.max_wait.get(e, 0) <= pg.count[e], (e, pg.max_wait.get(e), pg.count[e])
    block = es.enter_context(nc.Block())
    deco = {"pe": block.tensor, "act": block.scalar, "dve": block.vector, "pool": block.gpsimd, "sp": block.sync}

    def make(e):
        def body(eng):
            for item in pg.ops[e]:
                kind = item[0]
                if kind == "wait":
                    _, src, val = item
                    if isinstance(src, tuple) and src[1] in pg.total_keys:
                        val = 16 * pg.dma_cnt[src[1]]
                    eng.wait_ge(sems[src], val)
                elif kind == "op":
                    _, fn, signal = item
                    ins = fn(eng)
                    if signal:
                        ins.then_inc(sems[e], 1)
                elif kind == "dma":
                    _, out, in_, key, kw = item
                    eng.dma_start(out=out, in_=in_, **kw).then_inc(sems[("dma", key)], 16)
                elif kind == "custom":
                    _, fn, key = item
                    fn(eng).then_inc(sems[("dma", key)], 1)
                elif kind == "custom16":
                    _, fn, key = item
                    fn(eng).then_inc(sems[("dma", key)], 16)
        return body

    for e in pg.ENGS:
        deco[e](make(e))


def _fm(v, kd):
    return np.ascontiguousarray(np.asarray(v, np.float32).reshape(kd, 128).T)


def prepare_inputs(inputs, cfg):
    KD, D, CPC = cfg.KD, cfg.D, cfg.CPC
    f32 = lambda a: np.asarray(a, np.float32)
    x, ctx, c, c_ctx = f32(inputs["x"]), f32(inputs["ctx"]), f32(inputs["c"]), f32(inputs["c_ctx"])
    w_mod, b_mod = f32(inputs["w_mod"])[0], f32(inputs["b_mod"])[0]
    cTs = [np.ascontiguousarray(np.stack([_fm(c[b_], KD), _fm(c_ctx, KD)], axis=-1)) for b_ in range(2)]
    gains = np.stack([_fm(inputs["norm_ffn1"][0], KD), _fm(inputs["norm_mix"][0], KD),
                      _fm(inputs["norm_ffn2"][0], KD), _fm(inputs["norm_final"], KD)], axis=1)
    w_gk2, b_gk2 = f32(inputs["w_gk2"])[0], f32(inputs["b_gk2"])[0]
    w2aug = np.concatenate([w_gk2.transpose(1, 0, 2), b_gk2[None]], axis=0)
    conv_w = f32(inputs["conv_w"])[0]
    convw = np.ascontiguousarray(conv_w.T.reshape(8, 128, 31).transpose(1, 0, 2))
    convp = np.stack([f32(inputs["conv_b"])[0].reshape(8, 128).T, f32(inputs["conv_ln_g"])[0].reshape(8, 128).T,
                      f32(inputs["conv_ln_b"])[0].reshape(8, 128).T], axis=-1)
    shared = {
        "gains": np.ascontiguousarray(gains),
        "w_ffn1_in": f32(inputs["w_ffn1_in"])[0], "w_ffn1_out": f32(inputs["w_ffn1_out"])[0],
        "w_ffn2_in": f32(inputs["w_ffn2_in"])[0], "w_ffn2_out": f32(inputs["w_ffn2_out"])[0],
        "w_in": f32(inputs["w_in"])[0], "w_out": f32(inputs["w_out"])[0],
        "gconst": gla_consts(), "w2aug": np.ascontiguousarray(w2aug),
        "gnb": np.ascontiguousarray(np.broadcast_to(f32(inputs["gla_norm"])[0][None, :], (128, 256))),
        "convw": convw, "convp": np.ascontiguousarray(convp),
    }
    in_maps = []
    for core in range(8):
        b, j = core // 4, core % 4
        xt = np.concatenate([x[b, j * NL:(j + 1) * NL], ctx[b, j * NC_:(j + 1) * NC_]], axis=0)
        m = dict(shared)
        m["xT"] = np.ascontiguousarray(xt.T)
        m["cT"] = cTs[b]
        m["wmod"] = np.ascontiguousarray(w_mod[:, j * CPC * 128:(j + 1) * CPC * 128])
        m["bmod"] = np.ascontiguousarray(b_mod[j * CPC * 128:(j + 1) * CPC * 128].reshape(CPC, 128).T)
        sm = np.zeros((128, 16), np.float32)
        for r in range(4):
            sm[:, r] = 1.0 if r < j else 0.0
            sm[:, 8 + r] = 1.0 if r > j else 0.0
        sm[:, 4:8] = 1.0 - sm[:, 0:4]
        sm[:, 12:16] = 1.0 - sm[:, 8:12]
        m["segmask"] = sm
        in_maps.append(m)
    return in_maps


def build(cfg=None, stage=99, debug=False):
    cfg = cfg or Cfg()
    nc, pg, es = build_program(cfg, stage=stage, debug=debug)
    with es:
        _emit(pg, es)
    return nc, pg


def run(inputs, cfg=None, stage=99, debug=False, trace=False):
    cfg = cfg or Cfg()
    nc, pg = build(cfg, stage=stage, debug=debug)
    in_maps = prepare_inputs(inputs, cfg)
    if stage < 2:
        for m in in_maps:
            pass
    return run_bass_kernel_spmd(nc, in_maps, core_ids=list(range(8)), trace=trace)


def kernel(**inputs):
    cfg = Cfg()
    res = run(inputs, cfg)
    out = np.empty((2, 4096, cfg.D), np.float32)
    for core in range(8):
        b, j = core // 4, core % 4
        out[b, j * NL:(j + 1) * NL, :] = res.results[core]["outT"].T
    return out
```

```python
import numpy as np
import concourse.bass as bass
import concourse.mybir as mybir
from concourse.bass_utils import run_bass_kernel_spmd
from contextlib import ExitStack

F32 = mybir.dt.float32
BF16 = mybir.dt.bfloat16
AF = mybir.ActivationFunctionType
ALU = mybir.AluOpType

D = 2048
KD = 16
NL = 1024
NC_ = 64
NT = NL + NC_
DFF = 5632
NFC = 44
RMS_EPS = 1e-6

import os
SAME_ENGINE_SYNC = os.environ.get("KERNEL_SES", "1") == "1"


class Prog:
    ENGS = ("pe", "act", "dve", "pool", "sp")

    def __init__(self, nc):
        self.nc = nc
        self.ops = {e: [] for e in self.ENGS}
        self.count = {e: 0 for e in self.ENGS}
        self.waited = {e: {} for e in self.ENGS}
        self.res_w = {}
        self.res_r = {}
        self.dma_cnt = {}
        self.total_keys = set()
        self.max_wait = {}

    def _deps(self, eng, reads, writes):
        deps = {}

        def add(src, val):
            if deps.get(src, 0) < val:
                deps[src] = val

        for r in reads:
            w = self.res_w.get(r)
            if w is not None:
                add(*w)
        for w_ in writes:
            w = self.res_w.get(w_)
            if w is not None:
                add(*w)
            for src, val in self.res_r.get(w_, {}).items():
                add(src, val)
        out = []
        for src, val in deps.items():
            if src == eng:
                if not SAME_ENGINE_SYNC or eng == "pe":
                    continue
                if val > self.count[eng]:
                    continue
            if self.waited[eng].get(src, 0) >= val:
                continue
            self.waited[eng][src] = val
            out.append((src, val))
        return out

    def op(self, eng, fn, reads=(), writes=(), signal=True):
        for src, val in self._deps(eng, reads, writes):
            self.ops[eng].append(("wait", src, val))
            if self.max_wait.get(src, 0) < val:
                self.max_wait[src] = val
        if signal:
            self.count[eng] += 1
            seq = self.count[eng]
        else:
            seq = self.count[eng] + 1
        self.ops[eng].append(("op", fn, signal))
        for r in reads:
            self.res_r.setdefault(r, {})[eng] = seq
        for w in writes:
            self.res_w[w] = (eng, seq)
            self.res_r[w] = {}
        return seq

    def dma(self, q, out, in_, reads=(), writes=(), key=None, total=False, **kw):
        assert key is not None
        for src, val in self._deps(q, reads, writes):
            self.ops[q].append(("wait", src, val))
            if self.max_wait.get(src, 0) < val:
                self.max_wait[src] = val
        self.dma_cnt[key] = self.dma_cnt.get(key, 0) + 1
        val = 16 * self.dma_cnt[key]
        if total:
            self.total_keys.add(key)
        src = ("dma", key)
        self.ops[q].append(("dma", out, in_, key, kw))
        for r in reads:
            self.res_r.setdefault(r, {})[src] = val
        for w in writes:
            self.res_w[w] = (src, val)
            self.res_r[w] = {}

    def custom(self, q, fn, reads=(), writes=(), key=None):
        for src, val in self._deps(q, reads, writes):
            self.ops[q].append(("wait", src, val))
        self.dma_cnt[key] = self.dma_cnt.get(key, 0) + 1
        val = self.dma_cnt[key]
        src = ("dma", key)
        self.ops[q].append(("custom", fn, key))
        for r in reads:
            self.res_r.setdefault(r, {})[src] = val
        for w in writes:
            self.res_w[w] = (src, val)
            self.res_r[w] = {}

    def wait_all(self, eng, resources):
        for src, val in self._deps(eng, resources, ()):
            self.ops[eng].append(("wait", src, val))


class Cfg:
    def __init__(self, D=2048, DFF=5632, mstop=99):
        self.mstop = mstop
        self.D = D
        self.KD = D // 128
        self.DFF = DFF
        self.NFC = DFF // 128
        self.CPC = 9 * self.KD // 4
        assert 9 * self.KD % 4 == 0 and self.NFC % 2 == 0 and D % 512 == 0


OFF_K, OFF_V, OFF_GKF, OFF_GKB, OFF_Q, OFF_G, OFF_GA, OFF_GB = 0, 512, 1536, 1552, 1568, 2080, 3104, 4128
D_IN = 5152
EW = 1028
GC_MTB = 0
GC_CE = 256
GC_CH = 768
GC_MASK = 776
GC_ID = 1032
GC_ONE = 1160
GC_N = 1288


def gla_consts():
    g = np.zeros((128, GC_N), np.float32)
    s = np.arange(128)[:, None]
    t = np.arange(128)[None, :]
    same = (s // 64) == (t // 64)
    sc = -1.0 / 16.0
    g[:, GC_MTB:GC_MTB + 128] = sc * (same & (s > t))
    g[:, GC_MTB + 128:GC_MTB + 256] = sc * (same & (s < t))
    cum_f = sc * (same & (s <= t))
    cum_b = sc * (same & (s >= t))
    mid_f = sc * (same & ((s % 64) <= 31))
    mid_b = sc * (same & ((s % 64) >= 32))
    g[:, GC_CE:GC_CE + 128] = cum_f
    g[:, GC_CE + 128:GC_CE + 256] = cum_f - mid_f
    g[:, GC_CE + 256:GC_CE + 384] = cum_b
    g[:, GC_CE + 384:GC_CE + 512] = cum_b - mid_b
    g[:, GC_CH] = sc * (np.arange(128) < 64)
    g[:, GC_CH + 1] = sc * (np.arange(128) >= 64)
    g[:, GC_MASK:GC_MASK + 128] = (same & (t >= s))
    g[:, GC_MASK + 128:GC_MASK + 256] = (same & (t <= s))
    g[:, GC_ID:GC_ID + 128] = np.eye(128)
    g[:, GC_ONE:GC_ONE + 128] = 1.0 / 1024.0
    return g


def build_program(cfg=None, stage=99, debug=False):
    cfg = cfg or Cfg()
    D, KD, DFF, NFC, CPC = cfg.D, cfg.KD, cfg.DFF, cfg.NFC, cfg.CPC
    nc = bass.Bass("TRN2", target_bir_lowering=False)
    es = ExitStack()
    pg = Prog(nc)

    def dram_in(name, shape, dt=F32):
        return nc.dram_tensor(name, list(shape), dt, kind="ExternalInput").ap()

    def dram_tmp(name, shape, dt=F32):
        return nc.dram_tensor(name, list(shape), dt, kind="Internal").ap()

    xT = dram_in("xT", [D, NT])
    cT = dram_in("cT", [128, KD, 2])
    wmod = dram_in("wmod", [D, CPC * 128])
    bmod = dram_in("bmod", [128, CPC])
    gains = dram_in("gains", [128, 4, KD])
    w1i = dram_in("w_ffn1_in", [D, 2 * DFF])
    w1o = dram_in("w_ffn1_out", [DFF, D])
    w2i = dram_in("w_ffn2_in", [D, 2 * DFF])
    w2o = dram_in("w_ffn2_out", [DFF, D])
    w_in = dram_in("w_in", [D, D_IN])
    w_out = dram_in("w_out", [2048, D])
    gconst_d = dram_in("gconst", [128, GC_N])
    w2aug_d = dram_in("w2aug", [17, 2, 512])
    gnb_d = dram_in("gnb", [128, 256])
    convw_d = dram_in("convw", [128, 8, 31])
    convp_d = dram_in("convp", [128, 8, 3])
    segmask_d = dram_in("segmask", [128, 16])
    outT = nc.dram_tensor("outT", [D, NL], F32, kind="ExternalOutput").ap()
    mod_in = dram_tmp("mod_in", [128, 2 * CPC])
    mod_out = dram_tmp("mod_out", [4 * 128, 2 * CPC])
    h_spill = dram_tmp("h_spill", [128, KD * NL])
    sg_spill = dram_tmp("sg_spill", [NL, 1024], BF16)
    u_row = dram_tmp("u_row", [512, NL])
    yrow_spill = dram_tmp("yrow_spill", [512, NL])
    ycol_spill = dram_tmp("ycol_spill", [512, NL])
    ucol_in = [dram_tmp("ucol_in%d" % i, [256, NL]) for i in range(2)]
    ucol_g = [dram_tmp("ucol_g%d" % i, [4 * 256, NL]) for i in range(2)]
    st_in = [dram_tmp("st_in%d" % i, [128, EW]) for i in range(4)]
    st_out = [dram_tmp("st_out%d" % i, [4 * 128, EW]) for i in range(4)]
    dbg = {}

    def dbg_out(name, shape, dt=F32):
        dbg[name] = nc.dram_tensor("dbg_" + name, list(shape), dt, kind="ExternalOutput").ap()
        return dbg[name]

    def sb(name, shape, dt):
        return es.enter_context(nc.sbuf_tensor(name, list(shape), dt))

    HN = max(KD * NT, 17408)
    H = sb("H", [128, HN], F32)
    hT = H[:, 0:KD * NT].rearrange("p (k t) -> p k t", t=NT)
    HMN = max(KD * NT, 17408)
    HM = sb("HM", [128, HMN], BF16)
    hm = HM[:, 0:KD * NT].rearrange("p (k t) -> p k t", t=NT)
    wr = [sb("wr%d" % i, [128, 8192], BF16) for i in range(2)]
    NRING = 2
    ACTA = sb("ACTA", [128, 12 * NT], BF16)
    act = ACTA[:, :].rearrange("p (f t) -> p f t", t=NT)
    TMPA = sb("TMPA", [128, 3072], F32)
    ones_bf = sb("ones_bf", [128, 128], BF16)
    cs = sb("cs", [128, KD, 2], F32)
    cs_bf = sb("cs_bf", [128, KD, 2], BF16)
    bmod_sb = sb("bmod_sb", [128, CPC], F32)
    modloc = sb("modloc", [128, 2, CPC], F32)
    modL = sb("modL", [128, 9 * KD], F32)
    modC = sb("modC", [128, 9 * KD], F32)
    gains_sb = sb("gains_sb", [128, 4, KD], F32)
    coef = sb("coef", [128, 2, 3, 3, KD], F32)
    rb = sb("rb", [128, NT], F32)
    epsc = sb("epsc", [128, 4], F32)
    gconst = sb("gconst_sb", [128, GC_N], F32)
    w2aug = sb("w2aug_sb", [17, 2, 512], F32)
    gnb = sb("gnb_sb", [128, 256], F32)
    convw = sb("convw_sb", [128, 8, 31], F32)
    convp = sb("convp_sb", [128, 8, 3], F32)
    segmask = sb("segmask_sb", [128, 16], F32)
    pgk = sb("pgk", [32, 2, NT], F32)
    ps = [es.enter_context(nc.psum_tensor("ps%d" % i, [128, 512], F32)) for i in range(8)]

    sq = [TMPA[:, 0:256].bitcast(BF16), TMPA[:, 256:512].bitcast(BF16)]
    rtmp = TMPA[:, 512:1024]
    ntmp = [TMPA[:, 1024:1536], TMPA[:, 1536:2048]]
    sg = [TMPA[:, 2048:2560], TMPA[:, 2560:3072]]

    TILES3 = [(0, 512), (512, 512), (1024, 64)]
    TILES2 = [(0, 512), (512, 512)]
    HALL = [("h", k) for k in range(KD)]
    HMALL = [("hm", k) for k in range(KD)]

    def transfer(old, new):
        u = {}
        for o in old:
            w = pg.res_w.get(o)
            if w is not None:
                u[w[0]] = max(u.get(w[0], 0), w[1])
            for src, val in pg.res_r.get(o, {}).items():
                u[src] = max(u.get(src, 0), val)
        for n in new:
            pg.res_w.pop(n, None)
            pg.res_r[n] = dict(u)

    def dma_custom(q, fn, reads, writes, key):
        for src, val in pg._deps(q, reads, writes):
            pg.ops[q].append(("wait", src, val))
        pg.dma_cnt[key] = pg.dma_cnt.get(key, 0) + 1
        val = 16 * pg.dma_cnt[key]
        src = ("dma", key)
        pg.ops[q].append(("custom16", fn, key))
        for r in reads:
            pg.res_r.setdefault(r, {})[src] = val
        for w in writes:
            pg.res_w[w] = (src, val)
            pg.res_r[w] = {}

    xTv = xT.rearrange("(k p) t -> p k t", p=128)
    kstep = max(1, KD // 4)
    for k0 in range(0, KD, kstep):
        pg.dma("sp", hT[:, k0:k0 + kstep, :], xTv[:, k0:k0 + kstep, :],
               writes=[("h", k) for k in range(k0, k0 + kstep)], key="ld_x", total=True)
    for dst, src, name in ((cs[:], cT[:, :, :], "cs"), (bmod_sb[:], bmod[:, :], "bmod"),
                           (gains_sb[:], gains[:, :, :], "gains"), (gconst[:], gconst_d[:, :], "gconst"),
                           (w2aug[:], w2aug_d[:, :, :], "w2aug"), (gnb[:], gnb_d[:, :], "gnb"),
                           (convw[:], convw_d[:, :, :], "convw"), (convp[:], convp_d[:, :, :], "convp"),
                           (segmask[:], segmask_d[:, :], "segmask")):
        pg.dma("sp", dst, src, writes=[name], key="const", total=True)
    pg.op("dve", lambda e: e.memset(ones_bf[:], 1.0 / D), writes=["ones"])
    pg.op("dve", lambda e: e.memset(epsc[:, 0:1], RMS_EPS), writes=["epsc"])
    pg.op("dve", lambda e: e.memset(epsc[:, 1:2], 1e-5), writes=["epsc"])
    pg.op("dve", lambda e: e.memset(epsc[:, 2:3], 1.0), writes=["epsc"])
    pg.op("dve", lambda e: e.memset(pgk[:], 1.0), writes=["pgk"])

    ring = {"n": 0}

    def load_slab(pieces):
        s = ring["n"] % NRING
        ring["n"] += 1
        for vf, src in pieces:
            pg.dma("pool", vf(wr[s]), src, writes=[("wr", s)], key=("wr", s))
        return s

    def rms_stats(tiles):
        cnt = 0
        for (t0, n) in tiles:
            for k in range(KD):
                si = cnt % 2
                sqb = sq[si]
                cnt += 1
                pg.op("act", (lambda e, sqb=sqb, k=k, t0=t0, n=n: e.activation(
                    out=sqb[:, 0:n], in_=hT[:, k, t0:t0 + n], func=AF.Square)),
                    reads=[("h", k)], writes=[("sq", si)])
                pg.op("pe", (lambda e, sqb=sqb, k=k, n=n: e.matmul(
                    ps[6][:, 0:n], lhsT=ones_bf[:, :], rhs=sqb[:, 0:n], start=(k == 0), stop=(k == KD - 1))),
                    reads=[("sq", si), "ones"], writes=[("ps", 6)], signal=True)
            pg.op("act", (lambda e, n=n: e.activation(out=rtmp[:, 0:n], in_=ps[6][:, 0:n], func=AF.Sqrt,
                                                      bias=epsc[:, 0:1], scale=1.0)),
                  reads=[("ps", 6), "epsc"], writes=["rtmp"])
            pg.op("dve", (lambda e, t0=t0, n=n: e.reciprocal(out=rb[:, t0:t0 + n], in_=rtmp[:, 0:n])),
                  reads=["rtmp"], writes=[("rb", t0)])

    pg.op("act", lambda e: e.activation(out=cs_bf[:], in_=cs[:], func=AF.Silu), reads=["cs"], writes=["cs_bf"])
    wmv = wmod.rearrange("(k p) n -> p k n", p=128)
    rms_stats(TILES3)
    MC = 4 if CPC % 4 == 0 else 2
    for sl in range(CPC // MC):
        s = load_slab([(lambda t: t[:, 0:KD * 128 * MC].rearrange("p (k n) -> p k n", n=128 * MC),
                        wmv[:, :, sl * 128 * MC:(sl + 1) * 128 * MC])])
        wv = wr[s][:, 0:KD * 128 * MC].rearrange("p (k n) -> p k n", n=128 * MC)
        for gi in range(MC):
            gl = sl * MC + gi
            for k in range(KD):
                pg.op("pe", (lambda e, wv=wv, k=k, gl=gl, gi=gi: e.matmul(
                    ps[7][:, gl * 2:gl * 2 + 2], lhsT=wv[:, k, gi * 128:(gi + 1) * 128], rhs=cs_bf[:, k, :],
                    start=(k == 0), stop=(k == KD - 1))),
                    reads=[("wr", s), "cs_bf"], writes=[("ps", 7)], signal=(k == KD - 1))
    pg.op("dve", lambda e: e.tensor_tensor(
        out=modloc[:].rearrange("p v g -> p g v"),
        in0=ps[7][:, 0:2 * CPC].rearrange("p (g v) -> p g v", v=2),
        in1=bmod_sb[:].unsqueeze(2).to_broadcast([128, CPC, 2]), op=ALU.add),
        reads=[("ps", 7), "bmod"], writes=["modloc"])
    pg.dma("sp", mod_in[:, :], modloc[:].rearrange("p v g -> p (v g)"), reads=["modloc"], writes=["mod_in"],
           key="mod_io")
    pg.custom("pool", lambda e: e.collective_compute(
        "AllGather", ALU.bypass, replica_groups=[[0, 1, 2, 3], [4, 5, 6, 7]],
        ins=[mod_in[:, :]], outs=[mod_out[:, :]]), reads=["mod_in"], writes=["mod_out"], key="cc_mod")
    mov = mod_out.rearrange("(r p) n -> p r n", p=128)

    pidc = {}

    def get_pid(eng):
        if "pid" not in pidc:
            pidc["pid"] = eng.partition_id()
        return pidc["pid"]

    pg.dma("sp", modL[:].rearrange("p (r g) -> p r g", g=CPC), mov[:, :, 0:CPC], reads=["mod_out"],
           writes=["modL"], key="ld_mod", total=True)
    pg.dma("sp", modC[:].rearrange("p (r g) -> p r g", g=CPC), mov[:, :, CPC:2 * CPC], reads=["mod_out"],
           writes=["modC"], key="ld_mod", total=True)
    if debug:
        dm = dbg_out("mod", [128, 2, 9 * KD])
        pg.dma("sp", dm[:, 0, :], modL[:], reads=["modL"], key=("dbg", 1))
        pg.dma("sp", dm[:, 1, :], modC[:], reads=["modC"], key=("dbg", 2))

    for wi, m in enumerate((modL, modC)):
        mv = m[:].rearrange("p (m k) -> p m k", k=KD)
        for s_ in range(3):
            fac = 1.0 if s_ == 1 else 0.5
            pg.op("dve", (lambda e, wi=wi, mv=mv, s_=s_: e.scalar_tensor_tensor(
                out=coef[:, wi, s_, 0, :], in0=mv[:, 3 * s_ + 1, :], scalar=1.0, in1=gains_sb[:, s_, :],
                op0=ALU.add, op1=ALU.mult)), reads=["modL", "modC", "gains"], writes=["coef"])
            pg.op("dve", (lambda e, wi=wi, mv=mv, s_=s_: e.tensor_copy(
                out=coef[:, wi, s_, 1, :], in_=mv[:, 3 * s_, :])), reads=["modL", "modC"], writes=["coef"])
            pg.op("dve", (lambda e, wi=wi, mv=mv, s_=s_, fac=fac: e.tensor_scalar(
                out=coef[:, wi, s_, 2, :], in0=mv[:, 3 * s_ + 2, :], scalar1=fac, scalar2=None, op0=ALU.mult)),
                reads=["modL", "modC"], writes=["coef"])

    def norm_mod(s_, tiles, stats_done=False):
        if not stats_done:
            rms_stats(tiles)
        cnt = 0
        for (t0, n) in tiles:
            wi = 1 if t0 >= NL else 0
            for k in range(KD):
                ti = cnt % 2
                tb = ntmp[ti]
                cnt += 1
                pg.op("dve", (lambda e, tb=tb, k=k, t0=t0, n=n: e.tensor_tensor(
                    out=tb[:, 0:n], in0=hT[:, k, t0:t0 + n], in1=rb[:, t0:t0 + n], op=ALU.mult)),
                    reads=[("h", k), ("rb", t0)], writes=[("ntmp", ti)])
                pg.op("act", (lambda e, tb=tb, k=k, t0=t0, n=n, wi=wi: e.activation(
                    out=hm[:, k, t0:t0 + n], in_=tb[:, 0:n], func=AF.Identity,
                    bias=coef[:, wi, s_, 1, k:k + 1], scale=coef[:, wi, s_, 0, k:k + 1])),
                    reads=[("ntmp", ti), "coef"], writes=[("hm", k)])

    def ffn(w_i, w_o, s_, tiles):
        wiv = w_i.rearrange("(k p) n -> p k n", p=128)
        wov = w_o.rearrange("(f p) n -> p f n", p=128)
        nsl_tot = NFC // 2
        ngr = min(4, nsl_tot)
        groups = []
        a = 0
        for gi in range(ngr):
            n_ = nsl_tot // ngr + (1 if gi < nsl_tot % ngr else 0)
            groups.append((a, n_))
            a += n_
        assert max(n_ for _, n_ in groups) * 2 <= 12
        pair = 0
        oc = 0
        for (sl0, nsl) in groups:
            nf = 2 * nsl
            for sl in range(sl0, sl0 + nsl):
                s = load_slab([
                    (lambda t: t[:, 0:KD * 512].rearrange("p (k g n) -> p k g n", g=2, n=256)[:, :, 0, :],
                     wiv[:, :, sl * 256:(sl + 1) * 256]),
                    (lambda t: t[:, 0:KD * 512].rearrange("p (k g n) -> p k g n", g=2, n=256)[:, :, 1, :],
                     wiv[:, :, DFF + sl * 256:DFF + (sl + 1) * 256]),
                ])
                wv = wr[s][:, 0:KD * 512].rearrange("p (k g n) -> p k g n", g=2, n=256)
                for fi in range(2):
                    fl = (sl - sl0) * 2 + fi
                    for (t0, n) in tiles:
                        pG = 2 * (pair % 2)
                        pU = pG + 1
                        sgi = pair % 2
                        pair += 1
                        for g_, pb in ((0, pG), (1, pU)):
                            for k in range(KD):
                                pg.op("pe", (lambda e, wv=wv, k=k, g_=g_, fi=fi, pb=pb, t0=t0, n=n: e.matmul(
                                    ps[pb][:, 0:n], lhsT=wv[:, k, g_, fi * 128:(fi + 1) * 128],
                                    rhs=hm[:, k, t0:t0 + n], start=(k == 0), stop=(k == KD - 1))),
                                    reads=[("wr", s), ("hm", k)], writes=[("ps", pb)], signal=(k == KD - 1))
                        pg.op("act", (lambda e, pG=pG, sgi=sgi, n=n: e.activation(
                            out=sg[sgi][:, 0:n], in_=ps[pG][:, 0:n], func=AF.Silu)),
                            reads=[("ps", pG)], writes=[("sg", sgi)])
                        pg.op("dve", (lambda e, pU=pU, sgi=sgi, fl=fl, t0=t0, n=n: e.tensor_tensor(
                            out=act[:, fl, t0:t0 + n], in0=sg[sgi][:, 0:n], in1=ps[pU][:, 0:n], op=ALU.mult)),
                            reads=[("sg", sgi), ("ps", pU)], writes=[("act", fl)])
            f0 = 2 * sl0
            for ds_ in range(D // 512):
                s = load_slab([(lambda t, nf=nf: t[:, 0:nf * 512].rearrange("p (f n) -> p f n", n=512),
                                wov[:, f0:f0 + nf, ds_ * 512:(ds_ + 1) * 512])])
                wv = wr[s][:, 0:nf * 512].rearrange("p (f n) -> p f n", n=512)
                for dc in range(4):
                    dg = ds_ * 4 + dc
                    for (t0, n) in tiles:
                        pb = 4 + (oc % 2)
                        oc += 1
                        for fl in range(nf):
                            pg.op("pe", (lambda e, wv=wv, fl=fl, dc=dc, pb=pb, t0=t0, n=n: e.matmul(
                                ps[pb][:, 0:n], lhsT=wv[:, fl, dc * 128:(dc + 1) * 128],
                                rhs=act[:, fl, t0:t0 + n], start=(fl == 0), stop=(fl == nf - 1))),
                                reads=[("wr", s), ("act", fl)], writes=[("ps", pb)], signal=(fl == nf - 1))
                        subs = []
                        if t0 < NL:
                            subs.append((t0, min(t0 + n, NL) - t0, 0))
                        if t0 + n > NL:
                            a0 = max(t0, NL)
                            subs.append((a0, t0 + n - a0, 1))
                        for (a0, an, wi) in subs:
                            pg.op("dve", (lambda e, pb=pb, dg=dg, t0=t0, a0=a0, an=an, wi=wi: e.scalar_tensor_tensor(
                                out=hT[:, dg, a0:a0 + an], in0=ps[pb][:, a0 - t0:a0 - t0 + an],
                                scalar=coef[:, wi, s_, 2, dg:dg + 1], in1=hT[:, dg, a0:a0 + an],
                                op0=ALU.mult, op1=ALU.add)),
                                reads=[("ps", pb), "coef", ("h", dg)], writes=[("h", dg)])

    norm_mod(0, TILES3, stats_done=True)
    ffn(w1i, w1o, 0, [(0, 384), (384, 384), (768, 320)])
    if debug:
        d1 = dbg_out("h1", [D, NT])
        pg.dma("sp", d1.rearrange("(k p) t -> p k t", p=128), hT, reads=HALL, key=("dbg", 3))

    if stage >= 2:
        mixer_args = dict(locals())
        _mixer(mixer_args)

    if stage >= 3:
        norm_mod(2, TILES2)
        ffn(w2i, w2o, 2, TILES2)

    rms_stats(TILES2)
    for k in range(KD):
        for (t0, n) in TILES2:
            pg.op("dve", (lambda e, k=k, t0=t0, n=n: e.scalar_tensor_tensor(
                out=hT[:, k, t0:t0 + n], in0=hT[:, k, t0:t0 + n], scalar=gains_sb[:, 3, k:k + 1],
                in1=rb[:, t0:t0 + n], op0=ALU.mult, op1=ALU.mult)),
                reads=[("h", k), ("rb", t0), "gains"], writes=[("h", k)])
    outv = outT.rearrange("(k p) t -> p k t", p=128)
    for k0 in range(0, KD, kstep):
        pg.dma("sp", outv[:, k0:k0 + kstep, :], hT[:, k0:k0 + kstep, 0:NL],
               reads=[("h", k) for k in range(k0, k0 + kstep)], writes=[("out", k0)], key="st_out", total=True)
    pg.wait_all("sp", [("out", k0) for k0 in range(0, KD, kstep)])
    for key in list(pg.dma_cnt):
        if isinstance(key, tuple) and key[0] == "dbg":
            pg.ops["sp"].append(("wait", ("dma", key), 16 * pg.dma_cnt[key]))
    return nc, pg, es


def _mixer(a):
    from types import SimpleNamespace
    v = SimpleNamespace(**a)
    pg, nc, KD, D = v.pg, v.nc, v.KD, v.D
    H, HM, ACTA, TMPA, ps, wr = v.H, v.HM, v.ACTA, v.TMPA, v.ps, v.wr
    hT, hm, gconst, pgk = v.hT, v.hm, v.gconst, v.pgk
    TILES2, TILES3 = v.TILES2, v.TILES3
    load_slab, transfer, dma_custom = v.load_slab, v.transfer, v.dma_custom
    sq, ntmp, sg, rtmp, rb, epsc = v.sq, v.ntmp, v.sg, v.rtmp, v.rb, v.epsc
    debug = v.debug
    w_in_v = v.w_in.rearrange("(k p) n -> p k n", p=128)

    def bail():
        names = set(pg.res_w) | set(pg.res_r)
        transfer(list(names), v.HALL + v.HMALL + [("act", f) for f in range(12)]
                 + [("sq", 0), ("sq", 1), "rtmp", ("ntmp", 0), ("ntmp", 1), ("sg", 0), ("sg", 1)])
        pg.dma("sp", hT[:, :, 0:NL], v.h_spill.rearrange("p (k t) -> p k t", t=NL), reads=["hspill"], writes=v.HALL,
               key="unspill")

    mstop = v.cfg.mstop

    def hmb(off, nbytes, dt=F32):
        x = HM[:, off // 2:(off + nbytes) // 2]
        return x.bitcast(F32) if dt == F32 else x

    v.norm_mod(1, TILES3)
    pg.dma("sp", v.h_spill.rearrange("p (k t) -> p k t", t=NL), hT[:, :, 0:NL], reads=v.HALL, writes=["hspill"],
           key="spill")
    o_acc = H[:, 0:8192].rearrange("p (i c) -> p i c", c=1024)
    v_tm = H[:, 8192:12800].bitcast(BF16).rearrange("p (i c) -> p i c", c=1024)
    mixg = H[:, 12800:16896].bitcast(BF16).rearrange("p (i t) -> p i t", t=1024)
    uext = H[:, 8192:8192 + 46 * 64]
    OACC = [("oacc", i) for i in range(8)]
    VTM = [("vtm", i) for i in range(9)]
    transfer(v.HALL, OACC + VTM + ["mixg", ("qks", 2), ("qks", 3), ("attT", 2), ("attT", 3)])
    kT = ACTA[:, 0:4352].rearrange("p (h t) -> p h t", t=NT)
    qT = ACTA[:, 4352:8448].rearrange("p (h t) -> p h t", t=NL)
    k_tm = ACTA[:, 8448:13056].rearrange("p (i c) -> p i c", c=512)
    mixc = ACTA[:, 0:8192].rearrange("p (i t) -> p i t", t=1024)
    KT = [("kT", h) for h in range(4)]
    QT = [("qT", h) for h in range(4)]
    KTM = [("ktm", i) for i in range(9)]
    transfer([("act", f) for f in range(12)], KT + QT + KTM)

    PADS = []

    def slab512(col0, ncols=512):
        s = load_slab([(lambda t: t[:, 0:KD * ncols].rearrange("p (k n) -> p k n", n=ncols),
                        w_in_v[:, :, col0:col0 + ncols])])
        return s, wr[s][:, 0:KD * ncols].rearrange("p (k n) -> p k n", n=ncols)

    pair = 0
    for sl in range(4):
        s = load_slab([
            (lambda t: t[:, 0:KD * 512].rearrange("p (k g n) -> p k g n", g=2, n=256)[:, :, 0, :],
             w_in_v[:, :, OFF_GA + sl * 256:OFF_GA + (sl + 1) * 256]),
            (lambda t: t[:, 0:KD * 512].rearrange("p (k g n) -> p k g n", g=2, n=256)[:, :, 1, :],
             w_in_v[:, :, OFF_GB + sl * 256:OFF_GB + (sl + 1) * 256]),
        ])
        wv = wr[s][:, 0:KD * 512].rearrange("p (k g n) -> p k g n", g=2, n=256)
        for fi in range(2):
            cc = sl * 2 + fi
            for (t0, n) in TILES2:
                pA = 2 * (pair % 2)
                pB = pA + 1
                bi = pair % 2
                pair += 1
                for g_, pb in ((0, pA), (1, pB)):
                    for k in range(KD):
                        pg.op("pe", (lambda e, wv=wv, k=k, g_=g_, fi=fi, pb=pb, t0=t0, n=n: e.matmul(
                            ps[pb][:, 0:n], lhsT=wv[:, k, g_, fi * 128:(fi + 1) * 128], rhs=hm[:, k, t0:t0 + n],
                            start=(k == 0), stop=(k == KD - 1))),
                            reads=[("wr", s), ("hm", k)], writes=[("ps", pb)], signal=(k == KD - 1))
                pg.op("act", (lambda e, pB=pB, bi=bi, n=n: e.activation(
                    out=sg[bi][:, 0:n], in_=ps[pB][:, 0:n], func=AF.Sigmoid)),
                    reads=[("ps", pB)], writes=[("sg", bi)])
                pg.op("dve", (lambda e, pA=pA, bi=bi, n=n: e.tensor_tensor(
                    out=ntmp[bi][:, 0:n], in0=sg[bi][:, 0:n], in1=ps[pA][:, 0:n], op=ALU.mult)),
                    reads=[("sg", bi), ("ps", pA)], writes=[("ntmp", bi)])
                c4 = cc % 4
                if cc < 4:
                    dst = v.u_row[c4 * 128:(c4 + 1) * 128, t0:t0 + n]
                else:
                    dst = v.ucol_in[c4 // 2][(c4 % 2) * 128:(c4 % 2 + 1) * 128, t0:t0 + n]
                pg.dma("sp", dst, ntmp[bi][:, 0:n], reads=[("ntmp", bi)],
                       writes=[("udram", cc, t0)], key=("ust", bi))
    UCOL = [("udram", cc, t0) for cc in range(4, 8) for (t0, n) in TILES2]
    UROW = [("udram", cc, t0) for cc in range(4) for (t0, n) in TILES2]
    for cc in range(4):
        bsel = cc % 2
        RB = H[:, (2 * bsel) * 1024:(2 * bsel + 1) * 1024]
        RA = H[:, (2 * bsel + 1) * 1024:(2 * bsel + 2) * 1024]
        nb, na = ("oacc", 2 * bsel), ("oacc", 2 * bsel + 1)
        pg.dma("sp", RB, v.u_row[cc * 128:(cc + 1) * 128, :], reads=UROW, writes=[nb], key=("ld_ur", bsel))
        pg.op("dve", (lambda e, cc=cc, RB=RB, RA=RA: e.tensor_scalar(
            out=RA, in0=RB, scalar1=v.convw[:, cc, 15:16], scalar2=v.convp[:, cc, 0:1], op0=ALU.mult, op1=ALU.add)),
            reads=[nb, "convw", "convp"], writes=[na])
        for j in range(31):
            if j == 15:
                continue
            s_ = j - 15
            c0, c1 = max(0, -s_), min(64, 64 - s_)
            y3 = RA.rearrange("p (r c) -> p r c", c=64)[:, :, c0:c1]
            u3 = RB.rearrange("p (r c) -> p r c", c=64)[:, :, c0 + s_:c1 + s_]
            pg.op("dve", (lambda e, cc=cc, j=j, y3=y3, u3=u3: e.scalar_tensor_tensor(
                out=y3, in0=u3, scalar=v.convw[:, cc, j:j + 1], in1=y3, op0=ALU.mult, op1=ALU.add)),
                reads=[nb, "convw", na], writes=[na])
        pg.dma("sp", v.yrow_spill[cc * 128:(cc + 1) * 128, :], RA, reads=[na], writes=[("yrow", cc)],
               key=("st_yr", bsel))
    if mstop <= 1:
        return bail()
    cnt = 0
    for sl in range(2):
        s, wv = slab512(OFF_G + sl * 512)
        for ti in range(8):
            pb = 4 + (cnt % 2)
            bi = cnt % 2
            cnt += 1
            for k in range(KD):
                pg.op("pe", (lambda e, wv=wv, k=k, pb=pb, ti=ti: e.matmul(
                    ps[pb][:, :], lhsT=hm[:, k, ti * 128:(ti + 1) * 128], rhs=wv[:, k, :],
                    start=(k == 0), stop=(k == KD - 1))),
                    reads=[("wr", s), ("hm", k)], writes=[("ps", pb)], signal=(k == KD - 1))
            pg.op("act", (lambda e, pb=pb, bi=bi: e.activation(out=sq[bi][:, :], in_=ps[pb][:, :], func=AF.Silu)),
                  reads=[("ps", pb)], writes=[("sq", bi)])
            pg.dma("sp", v.sg_spill[ti * 128:(ti + 1) * 128, sl * 512:(sl + 1) * 512], sq[bi][:, :],
                   reads=[("sq", bi)], writes=[("sgsp", ti, sl)], key=("sgst", bi))

    for sl in range(2):
        s, wv = slab512(OFF_V + sl * 512)
        for ti in range(9):
            np_ = 128 if ti < 8 else 64
            pb = 4 + (cnt % 2)
            cnt += 1
            for k in range(KD):
                pg.op("pe", (lambda e, wv=wv, k=k, pb=pb, ti=ti, np_=np_: e.matmul(
                    ps[pb][0:np_, :], lhsT=hm[:, k, ti * 128:ti * 128 + np_], rhs=wv[:, k, :],
                    start=(k == 0), stop=(k == KD - 1))),
                    reads=[("wr", s), ("hm", k)], writes=[("ps", pb)], signal=(k == KD - 1))
            pg.op("act", (lambda e, pb=pb, ti=ti, sl=sl, np_=np_: e.copy(
                out=v_tm[0:np_, ti, sl * 512:(sl + 1) * 512], in_=ps[pb][0:np_, :])),
                reads=[("ps", pb)], writes=[("vtm", ti)])

    s, wv = slab512(OFF_K)
    for ti in range(9):
        np_ = 128 if ti < 8 else 64
        pb = 4 + (cnt % 2)
        cnt += 1
        for k in range(KD):
            pg.op("pe", (lambda e, wv=wv, k=k, pb=pb, ti=ti, np_=np_: e.matmul(
                ps[pb][0:np_, :], lhsT=hm[:, k, ti * 128:ti * 128 + np_], rhs=wv[:, k, :],
                start=(k == 0), stop=(k == KD - 1))),
                reads=[("wr", s), ("hm", k)], writes=[("ps", pb)], signal=(k == KD - 1))
        pg.op("act", (lambda e, pb=pb, ti=ti, np_=np_: e.copy(out=k_tm[0:np_, ti, :], in_=ps[pb][0:np_, :])),
              reads=[("ps", pb)], writes=[("ktm", ti)])
    for h in range(4):
        for (t0, n) in TILES3:
            pb = 4 + (cnt % 2)
            cnt += 1
            for k in range(KD):
                pg.op("pe", (lambda e, wv=wv, k=k, pb=pb, h=h, t0=t0, n=n: e.matmul(
                    ps[pb][:, 0:n], lhsT=wv[:, k, h * 128:(h + 1) * 128], rhs=hm[:, k, t0:t0 + n],
                    start=(k == 0), stop=(k == KD - 1))),
                    reads=[("wr", s), ("hm", k)], writes=[("ps", pb)], signal=(k == KD - 1))
            pg.op("act", (lambda e, pb=pb, h=h, t0=t0, n=n: e.copy(out=kT[:, h, t0:t0 + n], in_=ps[pb][:, 0:n])),
                  reads=[("ps", pb)], writes=[("kT", h)])
    s, wv = slab512(OFF_Q)
    for h in range(4):
        for (t0, n) in TILES2:
            pb = 4 + (cnt % 2)
            cnt += 1
            for k in range(KD):
                pg.op("pe", (lambda e, wv=wv, k=k, pb=pb, h=h, t0=t0, n=n: e.matmul(
                    ps[pb][:, 0:n], lhsT=wv[:, k, h * 128:(h + 1) * 128], rhs=hm[:, k, t0:t0 + n],
                    start=(k == 0), stop=(k == KD - 1))),
                    reads=[("wr", s), ("hm", k)], writes=[("ps", pb)], signal=(k == KD - 1))
            pg.op("act", (lambda e, pb=pb, h=h, t0=t0, n=n: e.mul(out=qT[:, h, t0:t0 + n], in_=ps[pb][:, 0:n],
                                                                 mul=float(128 ** -0.5))),
                  reads=[("ps", pb)], writes=[("qT", h)])
    s, wv = slab512(OFF_GKF, 32)
    for d in range(2):
        for (t0, n) in TILES3:
            pb = 4 + (cnt % 2)
            cnt += 1
            for k in range(KD):
                pg.op("pe", (lambda e, wv=wv, k=k, pb=pb, d=d, t0=t0, n=n: e.matmul(
                    ps[pb][0:16, 0:n], lhsT=wv[:, k, d * 16:(d + 1) * 16], rhs=hm[:, k, t0:t0 + n],
                    start=(k == 0), stop=(k == KD - 1))),
                    reads=[("wr", s), ("hm", k)], writes=[("ps", pb)], signal=(k == KD - 1))
            pg.op("act", (lambda e, pb=pb, d=d, t0=t0, n=n: e.copy(out=pgk[0:16, d, t0:t0 + n],
                                                                 in_=ps[pb][0:16, 0:n])),
                  reads=[("ps", pb)], writes=["pgk"])

    for i_ in range(2):
        pg.custom("pool", (lambda e, i_=i_: e.collective_compute(
            "AllGather", ALU.bypass, replica_groups=[[0, 1, 2, 3], [4, 5, 6, 7]],
            ins=[v.ucol_in[i_][:, :]], outs=[v.ucol_g[i_][:, :]])), reads=UCOL, writes=[("ucolg", i_)],
            key=("cc_u", i_))


    if mstop <= 2:
        return bail()
    st_loc = hmb(0, 4 * EW * 4).rearrange("p (e w) -> p e w", w=EW)
    Sbf = HM[:, 8224:8224 + 2048].rearrange("p (d h e) -> p d h e", d=2, h=4)
    off = 20544
    spb = hmb(off, 2048); off += 2048
    edb = hmb(off, 2048); off += 2048
    kdb = HM[:, off // 2:off // 2 + 512]; off += 1024
    decb = hmb(off, 32); off += 32
    ebe = [hmb(off + i * 1024, 1024) for i in range(2)]; off += 2048
    e1n = [hmb(off + i * 512, 512) for i in range(2)]; off += 1024
    qks = [[HM[:, (off + (i * 3 + j) * 256) // 2:(off + (i * 3 + j) * 256) // 2 + 128] for j in range(3)]
           for i in range(2)]; off += 1536
    attT = [HM[:, (off + i * 256) // 2:(off + i * 256) // 2 + 128] for i in range(2)]; off += 512
    assert off <= 34816
    ebe += [rb[:, 0:256], rb[:, 256:512]]
    e1n += [rb[:, 512:640], rb[:, 640:768]]
    HT_ = H[:, 16896:17408].bitcast(BF16)
    qks += [[HT_[:, (i * 3 + j) * 128:(i * 3 + j + 1) * 128] for j in range(3)] for i in range(2)]
    attT += [HT_[:, 768 + i * 128:768 + (i + 1) * 128] for i in range(2)]
    transfer([("rb", 0), ("rb", 512), ("rb", 1024)], [("ebe", 2), ("ebe", 3), ("e1n", 2), ("e1n", 3)])
    og = TMPA[:, 0:1024]
    sgt = TMPA[:, 1024:1536].bitcast(BF16)
    ebuf = TMPA[:, 1536:1536 + EW]
    t1 = TMPA[:, 2564:2820]
    ssb = TMPA[:, 2820:2828]
    GT = [("stloc", 0), ("stloc", 1)] + [(n_, d_, h_) for n_ in ("S", "Sbf") for d_ in range(2) for h_ in range(4)] + [ "spb", "edb", "kdb", "decb", ("ebe", 0), ("ebe", 1), ("e1n", 0), ("e1n", 1),
          ("qks", 0), ("qks", 1), ("attT", 0), ("attT", 1)]
    transfer(v.HMALL, GT)
    TM = ["og", "sgt", "ebuf", "t1", "ssb"]
    transfer([("sq", 0), ("sq", 1), "rtmp", ("ntmp", 0), ("ntmp", 1), ("sg", 0), ("sg", 1)], TM)

    def Sv(d):
        return st_loc[:, 2 * d + 1, 0:1024].rearrange("p (h e) -> p h e", e=256)

    def Sx(d):
        return st_loc[:, 2 * d, 0:1024].rearrange("p (h e) -> p h e", e=256)

    pg.op("dve", lambda e: e.memset(st_loc[:, :, 0:1024], 0.0), writes=[("stloc", 0), ("stloc", 1)])
    pg.op("dve", lambda e: e.memset(st_loc[:, :, 1024:1028], 1.0), writes=[("stloc", 0), ("stloc", 1)])

    Mtb = lambda d: gconst[:, GC_MTB + d * 128:GC_MTB + (d + 1) * 128]
    CE = lambda d: gconst[:, GC_CE + d * 256:GC_CE + (d + 1) * 256]
    CH = gconst[:, GC_CH:GC_CH + 2]
    MASK = lambda d: gconst[:, GC_MASK + d * 128:GC_MASK + (d + 1) * 128]
    IDM = gconst[:, GC_ID:GC_ID + 128]
    ONEF = gconst[:, GC_ONE:GC_ONE + 128]
    hcnt = {"n": 0}

    def gla_tile(ti, d, phase):
        np_ = 128 if ti < 8 else 64
        tok0 = ti * 128
        pg.op("pe", lambda e: e.matmul(ps[0][0:np_, :], lhsT=pgk[0:17, d, tok0:tok0 + np_], rhs=v.w2aug[0:17, d, :],
                                       start=True, stop=True), reads=["pgk", "w2aug"], writes=[("ps", 0)])
        pg.op("act", lambda e: e.activation(out=spb[0:np_, :], in_=ps[0][0:np_, :], func=AF.Exp, scale=-1.0),
              reads=[("ps", 0)], writes=["spb"])
        pg.op("act", lambda e: e.activation(out=spb[0:np_, :], in_=spb[0:np_, :], func=AF.Ln, bias=epsc[0:np_, 2:3],
                                            scale=1.0), reads=["spb", "epsc"], writes=["spb"])
        pg.op("pe", lambda e: e.matmul(ps[1][0:np_, :], lhsT=Mtb(d)[0:np_, 0:np_], rhs=spb[0:np_, :],
                                       start=True, stop=True), reads=["spb", "gconst"], writes=[("ps", 1)])
        pg.op("act", lambda e: e.activation(out=edb[0:np_, :], in_=ps[1][0:np_, :], func=AF.Exp),
              reads=[("ps", 1)], writes=["edb"])
        pg.op("dve", lambda e: e.tensor_tensor(out=kdb[0:np_, :], in0=k_tm[0:np_, ti, :], in1=edb[0:np_, :],
                                               op=ALU.mult), reads=[("ktm", ti), "edb"], writes=["kdb"])
        chunks = [0, 1] if d == 0 else [1, 0]
        if ti == 8:
            chunks = [0]
        if phase == "A":
            for h in range(4):
                pg.op("pe", (lambda e, h=h: e.matmul(ps[3][:, 2 * h:2 * h + 2], lhsT=spb[0:np_, h * 128:(h + 1) * 128],
                                                     rhs=CH[0:np_, :], start=True, stop=True)),
                      reads=["spb", "gconst"], writes=[("ps", 3)], signal=(h == 3))
            pg.op("act", lambda e: e.activation(out=decb[:, 0:8], in_=ps[3][:, 0:8], func=AF.Exp),
                  reads=[("ps", 3)], writes=["decb"])
            dview = decb[:, 0:8].rearrange("p (h x) -> p h x", x=2)
            for X in chunks:
                r0 = X * 64
                for h in range(4):
                    pb = 6 + (h // 2)
                    c0 = (h % 2) * 256
                    pg.op("pe", (lambda e, h=h, pb=pb, c0=c0, r0=r0: e.matmul(
                        ps[pb][:, c0:c0 + 256], lhsT=kdb[r0:r0 + 64, h * 128:(h + 1) * 128],
                        rhs=v_tm[r0:r0 + 64, ti, h * 256:(h + 1) * 256], start=True, stop=True)),
                        reads=["kdb", ("vtm", ti)], writes=[("ps", pb)])
                    if ti < 8:
                        pg.op("dve", (lambda e, h=h, pb=pb, c0=c0, X=X: e.scalar_tensor_tensor(
                            out=Sv(d)[:, h, :], in0=Sv(d)[:, h, :], scalar=decb[:, 2 * h + X:2 * h + X + 1],
                            in1=ps[pb][:, c0:c0 + 256], op0=ALU.mult, op1=ALU.add)),
                            reads=[("ps", pb), "decb", ("stloc", d)], writes=[("stloc", d)])
                    else:
                        pg.op("dve", (lambda e, h=h, pb=pb, c0=c0: e.tensor_copy(
                            out=Sx(d)[:, h, :], in_=ps[pb][:, c0:c0 + 256])),
                            reads=[("ps", pb)], writes=[("stloc", d)])
                if ti < 8:
                    pg.op("dve", (lambda e, X=X: e.tensor_tensor(
                        out=st_loc[:, 2 * d + 1, 1024:1028], in0=st_loc[:, 2 * d + 1, 1024:1028],
                        in1=dview[:, :, X], op=ALU.mult)), reads=["decb", ("stloc", d)], writes=[("stloc", d)])
                else:
                    pg.op("dve", (lambda e, X=X: e.tensor_copy(out=st_loc[:, 2 * d, 1024:1028], in_=dview[:, :, X])),
                          reads=["decb"], writes=[("stloc", d)])
            return
        S = Sv(d)
        for h in (0, 2, 1, 3):
            pbe = 3 + (h // 2)
            cb = (h % 2) * 256
            pg.op("pe", (lambda e, h=h, pbe=pbe, cb=cb: e.matmul(
                ps[pbe][:, cb:cb + 256], lhsT=spb[:, h * 128:(h + 1) * 128], rhs=CE(d), start=True, stop=True)),
                reads=["spb", "gconst"], writes=[("ps", pbe)])
            pg.op("act", (lambda e, h=h, pbe=pbe, cb=cb: e.activation(
                out=ebe[h][:, :], in_=ps[pbe][:, cb:cb + 256], func=AF.Exp)),
                reads=[("ps", pbe)], writes=[("ebe", h)])
            pg.op("act", (lambda e, h=h, pbe=pbe, cb=cb: e.activation(
                out=e1n[h][:, :], in_=ps[pbe][:, cb + 128:cb + 256], func=AF.Exp, scale=-1.0)),
                reads=[("ps", pbe)], writes=[("e1n", h)])
        for h in range(4):
            qb, qs, ks = qks[h]
            pg.op("dve", (lambda e, h=h, qb=qb: e.tensor_tensor(
                out=qb[:, :], in0=qT[:, h, tok0:tok0 + 128], in1=ebe[h][:, 0:128], op=ALU.mult)),
                reads=[("qT", h), ("ebe", h)], writes=[("qks", h)])
            pg.op("dve", (lambda e, h=h, qs=qs: e.tensor_tensor(
                out=qs[:, :], in0=qT[:, h, tok0:tok0 + 128], in1=ebe[h][:, 128:256], op=ALU.mult)),
                reads=[("qT", h), ("ebe", h)], writes=[("qks", h)])
            pg.op("dve", (lambda e, h=h, ks=ks: e.tensor_tensor(
                out=ks[:, :], in0=kT[:, h, tok0:tok0 + 128], in1=e1n[h][:, :], op=ALU.mult)),
                reads=[("kT", h), ("e1n", h)], writes=[("qks", h)])
            pg.op("pe", (lambda e, h=h, ks=ks, qs=qs: e.matmul(
                ps[5][:, h * 128:(h + 1) * 128], lhsT=ks[:, :], rhs=qs[:, :], start=True, stop=True)),
                reads=[("qks", h)], writes=[("ps", 5)])
            pg.op("dve", (lambda e, h=h: e.tensor_tensor(
                out=attT[h][:, :], in0=ps[5][:, h * 128:(h + 1) * 128], in1=MASK(d), op=ALU.mult)),
                reads=[("ps", 5), "gconst"], writes=[("attT", h)])
        for xi, X in enumerate(chunks):
            r0 = X * 64
            for h in range(4):
                qb = qks[h][0]
                po = 6 + (h // 2)
                co = (h % 2) * 256
                kc = (h % 2) * 256
                if xi == 0:
                    pg.op("pe", (lambda e, h=h, po=po, co=co: e.matmul(
                        ps[po][:, co:co + 256], lhsT=attT[h][:, :], rhs=v_tm[:, ti, h * 256:(h + 1) * 256],
                        start=(h % 2 == 0), stop=False, skip_group_check=True)),
                        reads=[("attT", h), ("vtm", ti)], writes=[("ps", po)])
                pg.op("pe", (lambda e, h=h, qb=qb, po=po, co=co, r0=r0, xi=xi: e.matmul(
                    ps[po][r0:r0 + 64, co:co + 256], lhsT=qb[:, r0:r0 + 64], rhs=Sbf[:, d, h, :],
                    start=False, stop=(xi == 1), skip_group_check=True)),
                    reads=[("qks", h), ("Sbf", d, h)], writes=[("ps", po)])
                pg.op("pe", (lambda e, h=h, r0=r0, kc=kc: e.matmul(
                    ps[2][:, kc:kc + 256], lhsT=kdb[r0:r0 + 64, h * 128:(h + 1) * 128],
                    rhs=v_tm[r0:r0 + 64, ti, h * 256:(h + 1) * 256], start=True, stop=True)),
                    reads=["kdb", ("vtm", ti)], writes=[("ps", 2)])
                col = (r0 + 63) if d == 0 else r0
                pg.op("dve", (lambda e, h=h, kc=kc, col=col: e.scalar_tensor_tensor(
                    out=S[:, h, :], in0=S[:, h, :], scalar=ebe[h][:, col:col + 1], in1=ps[2][:, kc:kc + 256],
                    op0=ALU.mult, op1=ALU.add)), reads=[("ps", 2), ("ebe", h), ("S", d, h)], writes=[("S", d, h)])
                pg.op("act", (lambda e, h=h: e.copy(out=Sbf[:, d, h, :], in_=S[:, h, :])),
                      reads=[("S", d, h)], writes=[("Sbf", d, h)])

    def phase_a(d):
        order = list(range(8)) if d == 0 else list(range(7, -1, -1))
        for ti in order + [8]:
            gla_tile(ti, d, "A")

    def exchange(d):
        for e_ in (2 * d, 2 * d + 1):
            pg.dma("sp", v.st_in[e_][:, :], st_loc[:, e_, :], reads=[("stloc", d)], writes=[("st_in", e_)],
                   key=("st_io", e_))
        for e_ in (2 * d, 2 * d + 1):
            pg.custom("pool", (lambda e, e_=e_: e.collective_compute(
                "AllGather", ALU.bypass, replica_groups=[[0, 1, 2, 3], [4, 5, 6, 7]],
                ins=[v.st_in[e_][:, :]], outs=[v.st_out[e_][:, :]])), reads=[("st_in", e_)],
                writes=[("st_out", e_)], key=("cc_st", e_))

    def combine(d):
        S = Sv(d)
        pg.op("dve", (lambda e, S=S: e.memset(S[:, :, :], 0.0)), reads=[("stloc", d)], writes=[("stloc", d)])
        ctx_order = [0, 1, 2, 3] if d == 0 else [3, 2, 1, 0]
        seg_order = [0, 1, 2] if d == 0 else [3, 2, 1]
        for kind, order in ((0, ctx_order), (1, seg_order)):
            for r in order:
                e_ = 2 * d + kind
                pg.dma("sp", ebuf[:, :], v.st_out[e_][r * 128:(r + 1) * 128, :], reads=[("st_out", e_)],
                       writes=["ebuf"], key="ld_st")
                if kind == 0:
                    for h in range(4):
                        pg.op("dve", (lambda e, h=h, S=S: e.scalar_tensor_tensor(
                            out=S[:, h, :], in0=S[:, h, :], scalar=ebuf[:, 1024 + h:1025 + h],
                            in1=ebuf[:, h * 256:(h + 1) * 256], op0=ALU.mult, op1=ALU.add)),
                            reads=["ebuf", ("stloc", d)], writes=[("stloc", d)])
                else:
                    mcol = (0 if d == 0 else 8) + r
                    pg.op("dve", (lambda e, mcol=mcol: e.tensor_scalar(
                        out=t1[:, 0:4], in0=ebuf[:, 1024:1028], scalar1=v.segmask[:, mcol:mcol + 1],
                        scalar2=v.segmask[:, mcol + 4:mcol + 5], op0=ALU.mult, op1=ALU.add)),
                        reads=["ebuf", "segmask"], writes=["t1"])
                    pg.op("dve", (lambda e, mcol=mcol: e.tensor_scalar(
                        out=og[:, :], in0=ebuf[:, 0:1024], scalar1=v.segmask[:, mcol:mcol + 1], scalar2=None,
                        op0=ALU.mult)), reads=["ebuf", "segmask"], writes=["og"])
                    for h in range(4):
                        pg.op("dve", (lambda e, h=h, S=S: e.scalar_tensor_tensor(
                            out=S[:, h, :], in0=S[:, h, :], scalar=t1[:, h:h + 1], in1=og[:, h * 256:(h + 1) * 256],
                            op0=ALU.mult, op1=ALU.add)), reads=["t1", "og", ("stloc", d)], writes=[("stloc", d)])
        transfer([("stloc", d)], [("S", d, h) for h in range(4)])
        for h in range(4):
            pg.res_w[("S", d, h)] = pg.res_w[("stloc", d)]
        for h in range(4):
            pg.op("act", (lambda e, S=S, d=d, h=h: e.copy(out=Sbf[:, d, h, :], in_=S[:, h, :])),
                  reads=[("S", d, h)], writes=[("Sbf", d, h)])


    def col_conv_early():
        ue = H[:, 4096:4096 + 2944]
        ac = H[:, 7168:8192]
        UE = [("oacc", 4), ("oacc", 5), ("oacc", 6)]
        AC = ("oacc", 7)
        for c4 in range(4):
            cc = 4 + c4

            def mk(which, c4=c4):
                UG = v.ucol_g[c4 // 2]
                ro = (c4 % 2) * 128

                def f(eng):
                    pid = v.get_pid(eng)
                    myr = pid % 4
                    if which == 0:
                        return eng.dma_start(out=ue[:, 0:960],
                                             in_=UG[bass.ds(((myr + 3) % 4) * 256 + ro, 128), 64:1024])
                    if which == 1:
                        return eng.dma_start(out=ue[:, 960:1984], in_=UG[bass.ds(myr * 256 + ro, 128), 0:1024])
                    return eng.dma_start(out=ue[:, 1984:2944],
                                         in_=UG[bass.ds(((myr + 1) % 4) * 256 + ro, 128), 0:960])
                return f
            for which in range(3):
                dma_custom("sp", mk(which), [("ucolg", c4 // 2)], UE, "ld_uc")
            pg.op("dve", lambda e: e.tensor_scalar(out=ue[:, 0:960], in0=ue[:, 0:960], scalar1=v.segmask[:, 0:1],
                                                   scalar2=None, op0=ALU.mult), reads=UE + ["segmask"], writes=UE)
            pg.op("dve", lambda e: e.tensor_scalar(out=ue[:, 1984:2944], in0=ue[:, 1984:2944],
                                                   scalar1=v.segmask[:, 11:12], scalar2=None, op0=ALU.mult),
                  reads=UE + ["segmask"], writes=UE)
            pg.op("dve", (lambda e, cc=cc: e.tensor_scalar(
                out=ac, in0=ue[:, 960:1984], scalar1=v.convw[:, cc, 15:16], scalar2=v.convp[:, cc, 0:1],
                op0=ALU.mult, op1=ALU.add)), reads=UE + ["convw", "convp"], writes=[AC])
            for j in range(31):
                if j == 15:
                    continue
                pg.op("dve", (lambda e, cc=cc, j=j: e.scalar_tensor_tensor(
                    out=ac, in0=ue[:, j * 64:j * 64 + 1024], scalar=v.convw[:, cc, j:j + 1], in1=ac,
                    op0=ALU.mult, op1=ALU.add)), reads=UE + ["convw", AC], writes=[AC])
            pg.dma("sp", v.ycol_spill[c4 * 128:(c4 + 1) * 128, :], ac, reads=[AC], writes=[("ycol", c4)],
                   key="st_yc")

    phase_a(0)
    exchange(0)
    phase_a(1)
    exchange(1)
    col_conv_early()
    combine(0)
    if mstop <= 3:
        return bail()
    for ti in range(8):
        gla_tile(ti, 0, "C")
        for half in range(2):
            pg.op("act", (lambda e, ti=ti, half=half: e.copy(out=o_acc[:, ti, half * 512:(half + 1) * 512],
                                                             in_=ps[6 + half][:, :])),
                  reads=[("ps", 6 + half)], writes=[("oacc", ti)])
    combine(1)
    for ti in range(7, -1, -1):
        gla_tile(ti, 1, "C")
        for half in range(2):
            pg.op("dve", (lambda e, ti=ti, half=half: e.tensor_tensor(
                out=o_acc[:, ti, half * 512:(half + 1) * 512], in0=o_acc[:, ti, half * 512:(half + 1) * 512],
                in1=ps[6 + half][:, :], op=ALU.add)), reads=[("ps", 6 + half), ("oacc", ti)], writes=[("oacc", ti)])
        pg.dma("sp", sgt[:, :], v.sg_spill[ti * 128:(ti + 1) * 128, :],
               reads=[("sgsp", ti, 0), ("sgsp", ti, 1)], writes=["sgt"], key="ld_sg")
        for h in range(4):
            pg.op("act", (lambda e, ti=ti, h=h: e.activation(
                out=t1[:, :], in_=o_acc[:, ti, h * 256:(h + 1) * 256], func=AF.Square,
                accum_out=ssb[:, h:h + 1])), reads=[("oacc", ti)], writes=["t1", "ssb"])
        pg.op("act", lambda e: e.activation(out=ssb[:, 4:8], in_=ssb[:, 0:4], func=AF.Sqrt, bias=epsc[:, 1:2],
                                            scale=1.0 / 256.0), reads=["ssb", "epsc"], writes=["ssb"])
        pg.op("dve", lambda e: e.reciprocal(out=ssb[:, 4:8], in_=ssb[:, 4:8]), reads=["ssb"], writes=["ssb"])
        for h in range(4):
            pg.op("dve", (lambda e, ti=ti, h=h: e.scalar_tensor_tensor(
                out=t1[:, :], in0=o_acc[:, ti, h * 256:(h + 1) * 256], scalar=ssb[:, 4 + h:5 + h], in1=v.gnb[:, :],
                op0=ALU.mult, op1=ALU.mult)), reads=[("oacc", ti), "ssb", "gnb"], writes=["t1"])
            pg.op("dve", (lambda e, h=h: e.tensor_tensor(
                out=og[:, h * 256:(h + 1) * 256], in0=t1[:, :], in1=sgt[:, h * 256:(h + 1) * 256], op=ALU.mult)),
                reads=["t1", "sgt"], writes=["og"])
        for half in range(2):
            pt = 5 if half == 0 else 2
            for b4 in range(4):
                blk = half * 4 + b4
                pg.op("pe", (lambda e, pt=pt, b4=b4, blk=blk: e.transpose(
                    out=ps[pt][:, b4 * 128:(b4 + 1) * 128], in_=og[:, blk * 128:(blk + 1) * 128], identity=IDM)),
                    reads=["og", "gconst"], writes=[("ps", pt)], signal=(b4 == 3))
            pg.op("act", (lambda e, pt=pt, half=half, ti=ti: e.copy(
                out=mixg[:, half * 4:(half + 1) * 4, ti * 128:(ti + 1) * 128],
                in_=ps[pt][:, :].rearrange("p (b t) -> p b t", t=128))), reads=[("ps", pt)], writes=["mixg"])
    if debug:
        dd = v.dbg_out("mixg", [1024, NL], BF16)
        pg.dma("sp", dd.rearrange("(i p) t -> p i t", p=128), mixg, reads=["mixg"], key=("dbg", 6))

    if mstop <= 4:
        return bail()
    y = hmb(0, 32768).rearrange("p (c t) -> p c t", t=1024)
    Y = [("y", c) for c in range(8)]
    transfer(GT, Y)
    transfer(VTM, ["uext"])
    transfer(KT + QT + KTM, ["mixc"])
    for cc in range(8):
        src = v.yrow_spill if cc < 4 else v.ycol_spill
        rn = ("yrow", cc) if cc < 4 else ("ycol", cc - 4)
        pg.dma("sp", y[:, cc, :], src[(cc % 4) * 128:(cc % 4 + 1) * 128, :], reads=[rn], writes=[("y", cc)],
               key=("ld_y", cc))
    if debug:
        dd = v.dbg_out("y", [1024, NL])
        pg.dma("sp", dd.rearrange("(i p) t -> p i t", p=128), y, reads=Y, key=("dbg", 7))
    if mstop <= 5:
        return bail()
    transfer(TM, [("sq", 0), ("sq", 1), "rtmp", ("ntmp", 0), ("ntmp", 1), ("sg", 0), ("sg", 1)])
    transfer([("ebe", 2), ("ebe", 3), ("e1n", 2), ("e1n", 3)], [("rb", 0), ("rb", 512), ("rb", 1024)])
    cnt = 0
    for (t0, n) in TILES2:
        for cc in range(8):
            pg.op("pe", (lambda e, cc=cc, t0=t0, n=n: e.matmul(ps[0][:, 0:n], lhsT=ONEF, rhs=y[:, cc, t0:t0 + n],
                                                             start=(cc == 0), stop=(cc == 7))),
                  reads=[("y", cc), "gconst"], writes=[("ps", 0)], signal=(cc == 7))
        for cc in range(8):
            pg.op("dve", (lambda e, cc=cc, t0=t0, n=n: e.tensor_tensor(
                out=y[:, cc, t0:t0 + n], in0=y[:, cc, t0:t0 + n], in1=ps[0][:, 0:n], op=ALU.subtract)),
                reads=[("ps", 0), ("y", cc)], writes=[("y", cc)])
        for cc in range(8):
            bi = cnt % 2
            cnt += 1
            pg.op("act", (lambda e, cc=cc, bi=bi, t0=t0, n=n: e.activation(
                out=ntmp[bi][:, 0:n], in_=y[:, cc, t0:t0 + n], func=AF.Square)),
                reads=[("y", cc)], writes=[("ntmp", bi)])
            pg.op("pe", (lambda e, cc=cc, bi=bi, n=n: e.matmul(ps[1][:, 0:n], lhsT=ONEF, rhs=ntmp[bi][:, 0:n],
                                                             start=(cc == 0), stop=(cc == 7))),
                  reads=[("ntmp", bi), "gconst"], writes=[("ps", 1)], signal=True)
        pg.op("act", (lambda e, n=n: e.activation(out=rtmp[:, 0:n], in_=ps[1][:, 0:n], func=AF.Sqrt,
                                                  bias=epsc[:, 1:2], scale=1.0)),
              reads=[("ps", 1), "epsc"], writes=["rtmp"])
        pg.op("dve", (lambda e, t0=t0, n=n: e.reciprocal(out=rb[:, t0:t0 + n], in_=rtmp[:, 0:n])),
              reads=["rtmp"], writes=[("rb", t0)])
        for cc in range(8):
            bi = cnt % 2
            cnt += 1
            pg.op("dve", (lambda e, cc=cc, bi=bi, t0=t0, n=n: e.tensor_tensor(
                out=ntmp[bi][:, 0:n], in0=y[:, cc, t0:t0 + n], in1=rb[:, t0:t0 + n], op=ALU.mult)),
                reads=[("y", cc), ("rb", t0)], writes=[("ntmp", bi)])
            pg.op("act", (lambda e, cc=cc, bi=bi, t0=t0, n=n: e.activation(
                out=mixc[:, cc, t0:t0 + n], in_=ntmp[bi][:, 0:n], func=AF.Silu,
                bias=v.convp[:, cc, 2:3], scale=v.convp[:, cc, 1:2])),
                reads=[("ntmp", bi), "convp"], writes=["mixc"])
    if debug:
        dd = v.dbg_out("mixc", [1024, NL], BF16)
        pg.dma("sp", dd.rearrange("(i p) t -> p i t", p=128), mixc, reads=["mixc"], key=("dbg", 8))

    mixg2 = HM[:, 0:8192].rearrange("p (i t) -> p i t", t=1024)
    transfer(Y, ["mixg2"])
    pg.op("act", lambda e: e.copy(out=mixg2[:, :, :], in_=mixg[:, :, :]), reads=["mixg"], writes=["mixg2"])
    transfer(OACC + ["uext", "mixg", ("qks", 2), ("qks", 3), ("attT", 2), ("attT", 3)], v.HALL)
    pg.dma("sp", hT[:, :, 0:NL], v.h_spill.rearrange("p (k t) -> p k t", t=NL), reads=["hspill"], writes=v.HALL,
           key="unspill")

    wov = v.w_out.rearrange("(i p) n -> p i n", p=128)
    oc = 0
    for ds_ in range(D // 512):
        s = load_slab([(lambda t: t[:, 0:8192].rearrange("p (i n) -> p i n", n=512),
                        wov[:, :, ds_ * 512:(ds_ + 1) * 512])])
        wv = wr[s][:, 0:8192].rearrange("p (i n) -> p i n", n=512)
        for dc in range(4):
            dg = ds_ * 4 + dc
            for (t0, n) in TILES2:
                pb = 4 + (oc % 2)
                oc += 1
                for i in range(16):
                    src = mixg2 if i < 8 else mixc
                    rn = "mixg2" if i < 8 else "mixc"
                    pg.op("pe", (lambda e, wv=wv, i=i, dc=dc, pb=pb, t0=t0, n=n, src=src: e.matmul(
                        ps[pb][:, 0:n], lhsT=wv[:, i, dc * 128:(dc + 1) * 128], rhs=src[:, i % 8, t0:t0 + n],
                        start=(i == 0), stop=(i == 15))),
                        reads=[("wr", s), rn], writes=[("ps", pb)], signal=(i == 15))
                pg.op("dve", (lambda e, pb=pb, dg=dg, t0=t0, n=n: e.scalar_tensor_tensor(
                    out=hT[:, dg, t0:t0 + n], in0=ps[pb][:, 0:n], scalar=v.coef[:, 0, 1, 2, dg:dg + 1],
                    in1=hT[:, dg, t0:t0 + n], op0=ALU.mult, op1=ALU.add)),
                    reads=[("ps", pb), "coef", ("h", dg)], writes=[("h", dg)])
    transfer(["mixg2"], v.HMALL)
    transfer(["mixc"], [("act", f) for f in range(12)])
    if debug:
        d2 = v.dbg_out("h2", [D, NL])
        pg.dma("sp", d2.rearrange("(k p) t -> p k t", p=128), hT[:, :, 0:NL], reads=v.HALL, key=("dbg", 9))


def _emit(pg, es):
    nc = pg.nc
    sems = {}
    for e in pg.ENGS:
        sems[e] = es.enter_context(nc.semaphore("s_" + e))
    for i, key in enumerate(sorted(pg.dma_cnt, key=str)):
        sems[("dma", key)] = es.enter_context(nc.semaphore("d%d" % i))
    for e in pg.ENGS:
        assert pg.max_wait.get(e, 0) <= pg.count[e], (e, pg.max_wait.get(e), pg.count[e])
    block = es.enter_context(nc.Block())
    deco = {"pe": block.tensor, "act": block.scalar, "dve": block.vector, "pool": block.gpsimd, "sp": block.sync}

    def make(e):
        def body(eng):
            for item in pg.ops[e]:
                kind = item[0]
                if kind == "wait":
                    _, src, val = item
                    if isinstance(src, tuple) and src[1] in pg.total_keys:
                        val = 16 * pg.dma_cnt[src[1]]
                    eng.wait_ge(sems[src], val)
                elif kind == "op":
                    _, fn, signal = item
                    ins = fn(eng)
                    if signal:
                        ins.then_inc(sems[e], 1)
                elif kind == "dma":
                    _, out, in_, key, kw = item
                    eng.dma_start(out=out, in_=in_, **kw).then_inc(sems[("dma", key)], 16)
                elif kind == "custom":
                    _, fn, key = item
                    fn(eng).then_inc(sems[("dma", key)], 1)
                elif kind == "custom16":
                    _, fn, key = item
                    fn(eng).then_inc(sems[("dma", key)], 16)
        return body

    for e in pg.ENGS:
        deco[e](make(e))


def _fm(v, kd):
    return np.ascontiguousarray(np.asarray(v, np.float32).reshape(kd, 128).T)


def prepare_inputs(inputs, cfg):
    KD, D, CPC = cfg.KD, cfg.D, cfg.CPC
    f32 = lambda a: np.asarray(a, np.float32)
    x, ctx, c, c_ctx = f32(inputs["x"]), f32(inputs["ctx"]), f32(inputs["c"]), f32(inputs["c_ctx"])
    w_mod, b_mod = f32(inputs["w_mod"])[0], f32(inputs["b_mod"])[0]
    cTs = [np.ascontiguousarray(np.stack([_fm(c[b_], KD), _fm(c_ctx, KD)], axis=-1)) for b_ in range(2)]
    gains = np.stack([_fm(inputs["norm_ffn1"][0], KD), _fm(inputs["norm_mix"][0], KD),
                      _fm(inputs["norm_ffn2"][0], KD), _fm(inputs["norm_final"], KD)], axis=1)
    w_gk2, b_gk2 = f32(inputs["w_gk2"])[0], f32(inputs["b_gk2"])[0]
    w2aug = np.concatenate([w_gk2.transpose(1, 0, 2), b_gk2[None]], axis=0)
    conv_w = f32(inputs["conv_w"])[0]
    convw = np.ascontiguousarray(conv_w.T.reshape(8, 128, 31).transpose(1, 0, 2))
    convp = np.stack([f32(inputs["conv_b"])[0].reshape(8, 128).T, f32(inputs["conv_ln_g"])[0].reshape(8, 128).T,
                      f32(inputs["conv_ln_b"])[0].reshape(8, 128).T], axis=-1)
    shared = {
        "gains": np.ascontiguousarray(gains),
        "w_ffn1_in": f32(inputs["w_ffn1_in"])[0], "w_ffn1_out": f32(inputs["w_ffn1_out"])[0],
        "w_ffn2_in": f32(inputs["w_ffn2_in"])[0], "w_ffn2_out": f32(inputs["w_ffn2_out"])[0],
        "w_in": f32(inputs["w_in"])[0], "w_out": f32(inputs["w_out"])[0],
        "gconst": gla_consts(), "w2aug": np.ascontiguousarray(w2aug),
        "gnb": np.ascontiguousarray(np.broadcast_to(f32(inputs["gla_norm"])[0][None, :], (128, 256))),
        "convw": convw, "convp": np.ascontiguousarray(convp),
    }
    in_maps = []
    for core in range(8):
        b, j = core // 4, core % 4
        xt = np.concatenate([x[b, j * NL:(j + 1) * NL], ctx[b, j * NC_:(j + 1) * NC_]], axis=0)
        m = dict(shared)
        m["xT"] = np.ascontiguousarray(xt.T)
        m["cT"] = cTs[b]
        m["wmod"] = np.ascontiguousarray(w_mod[:, j * CPC * 128:(j + 1) * CPC * 128])
        m["bmod"] = np.ascontiguousarray(b_mod[j * CPC * 128:(j + 1) * CPC * 128].reshape(CPC, 128).T)
        sm = np.zeros((128, 16), np.float32)
        for r in range(4):
            sm[:, r] = 1.0 if r < j else 0.0
            sm[:, 8 + r] = 1.0 if r > j else 0.0
        sm[:, 4:8] = 1.0 - sm[:, 0:4]
        sm[:, 12:16] = 1.0 - sm[:, 8:12]
        m["segmask"] = sm
        in_maps.append(m)
    return in_maps


def build(cfg=None, stage=99, debug=False):
    cfg = cfg or Cfg()
    nc, pg, es = build_program(cfg, stage=stage, debug=debug)
    with es:
        _emit(pg, es)
    return nc, pg


def run(inputs, cfg=None, stage=99, debug=False, trace=False):
    cfg = cfg or Cfg()
    nc, pg = build(cfg, stage=stage, debug=debug)
    in_maps = prepare_inputs(inputs, cfg)
    if stage < 2:
        for m in in_maps:
            pass
    return run_bass_kernel_spmd(nc, in_maps, core_ids=list(range(8)), trace=trace)


def kernel(**inputs):
    cfg = Cfg()
    res = run(inputs, cfg)
    out = np.empty((2, 4096, cfg.D), np.float32)
    for core in range(8):
        b, j = core // 4, core % 4
        out[b, j * NL:(j + 1) * NL, :] = res.results[core]["outT"].T
    return out
```

```python
import numpy as np
import concourse.bass as bass
import concourse.mybir as mybir
from concourse.bass_utils import run_bass_kernel_spmd
from contextlib import ExitStack

F32 = mybir.dt.float32
BF16 = mybir.dt.bfloat16
AF = mybir.ActivationFunctionType
ALU = mybir.AluOpType

D = 2048
KD = 16
NL = 1024
NC_ = 64
NT = NL + NC_
DFF = 5632
NFC = 44
RMS_EPS = 1e-6

import os
SAME_ENGINE_SYNC = os.environ.get("KERNEL_SES", "1") == "1"


class Prog:
    ENGS = ("pe", "act", "dve", "pool", "sp")

    def __init__(self, nc):
        self.nc = nc
        self.ops = {e: [] for e in self.ENGS}
        self.count = {e: 0 for e in self.ENGS}
        self.waited = {e: {} for e in self.ENGS}
        self.res_w = {}
        self.res_r = {}
        self.dma_cnt = {}
        self.total_keys = set()
        self.max_wait = {}

    def _deps(self, eng, reads, writes):
        deps = {}

        def add(src, val):
            if deps.get(src, 0) < val:
                deps[src] = val

        for r in reads:
            w = self.res_w.get(r)
            if w is not None:
                add(*w)
        for w_ in writes:
            w = self.res_w.get(w_)
            if w is not None:
                add(*w)
            for src, val in self.res_r.get(w_, {}).items():
                add(src, val)
        out = []
        for src, val in deps.items():
            if src == eng:
                if not SAME_ENGINE_SYNC or eng == "pe":
                    continue
                if val > self.count[eng]:
                    continue
            if self.waited[eng].get(src, 0) >= val:
                continue
            self.waited[eng][src] = val
            out.append((src, val))
        return out

    def op(self, eng, fn, reads=(), writes=(), signal=True):
        for src, val in self._deps(eng, reads, writes):
            self.ops[eng].append(("wait", src, val))
            if self.max_wait.get(src, 0) < val:
                self.max_wait[src] = val
        if signal:
            self.count[eng] += 1
            seq = self.count[eng]
        else:
            seq = self.count[eng] + 1
        self.ops[eng].append(("op", fn, signal))
        for r in reads:
            self.res_r.setdefault(r, {})[eng] = seq
        for w in writes:
            self.res_w[w] = (eng, seq)
            self.res_r[w] = {}
        return seq

    def dma(self, q, out, in_, reads=(), writes=(), key=None, total=False, **kw):
        assert key is not None
        for src, val in self._deps(q, reads, writes):
            self.ops[q].append(("wait", src, val))
            if self.max_wait.get(src, 0) < val:
                self.max_wait[src] = val
        self.dma_cnt[key] = self.dma_cnt.get(key, 0) + 1
        val = 16 * self.dma_cnt[key]
        if total:
            self.total_keys.add(key)
        src = ("dma", key)
        self.ops[q].append(("dma", out, in_, key, kw))
        for r in reads:
            self.res_r.setdefault(r, {})[src] = val
        for w in writes:
            self.res_w[w] = (src, val)
            self.res_r[w] = {}

    def custom(self, q, fn, reads=(), writes=(), key=None):
        for src, val in self._deps(q, reads, writes):
            self.ops[q].append(("wait", src, val))
        self.dma_cnt[key] = self.dma_cnt.get(key, 0) + 1
        val = self.dma_cnt[key]
        src = ("dma", key)
        self.ops[q].append(("custom", fn, key))
        for r in reads:
            self.res_r.setdefault(r, {})[src] = val
        for w in writes:
            self.res_w[w] = (src, val)
            self.res_r[w] = {}

    def wait_all(self, eng, resources):
        for src, val in self._deps(eng, resources, ()):
            self.ops[eng].append(("wait", src, val))


class Cfg:
    def __init__(self, D=2048, DFF=5632, mstop=99):
        self.mstop = mstop
        self.D = D
        self.KD = D // 128
        self.DFF = DFF
        self.NFC = DFF // 128
        self.CPC = 9 * self.KD // 4
        assert 9 * self.KD % 4 == 0 and self.NFC % 2 == 0 and D % 512 == 0


OFF_K, OFF_V, OFF_GKF, OFF_GKB, OFF_Q, OFF_G, OFF_GA, OFF_GB = 0, 512, 1536, 1552, 1568, 2080, 3104, 4128
D_IN = 5152
EW = 1028
GC_MTB = 0
GC_CE = 256
GC_CH = 768
GC_MASK = 776
GC_ID = 1032
GC_ONE = 1160
GC_N = 1288


def gla_consts():
    g = np.zeros((128, GC_N), np.float32)
    s = np.arange(128)[:, None]
    t = np.arange(128)[None, :]
    same = (s // 64) == (t // 64)
    sc = -1.0 / 16.0
    g[:, GC_MTB:GC_MTB + 128] = sc * (same & (s > t))
    g[:, GC_MTB + 128:GC_MTB + 256] = sc * (same & (s < t))
    cum_f = sc * (same & (s <= t))
    cum_b = sc * (same & (s >= t))
    mid_f = sc * (same & ((s % 64) <= 31))
    mid_b = sc * (same & ((s % 64) >= 32))
    g[:, GC_CE:GC_CE + 128] = cum_f
    g[:, GC_CE + 128:GC_CE + 256] = cum_f - mid_f
    g[:, GC_CE + 256:GC_CE + 384] = cum_b
    g[:, GC_CE + 384:GC_CE + 512] = cum_b - mid_b
    g[:, GC_CH] = sc * (np.arange(128) < 64)
    g[:, GC_CH + 1] = sc * (np.arange(128) >= 64)
    g[:, GC_MASK:GC_MASK + 128] = (same & (t >= s))
    g[:, GC_MASK + 128:GC_MASK + 256] = (same & (t <= s))
    g[:, GC_ID:GC_ID + 128] = np.eye(128)
    g[:, GC_ONE:GC_ONE + 128] = 1.0 / 1024.0
    return g


def build_program(cfg=None, stage=99, debug=False):
    cfg = cfg or Cfg()
    D, KD, DFF, NFC, CPC = cfg.D, cfg.KD, cfg.DFF, cfg.NFC, cfg.CPC
    nc = bass.Bass("TRN2", target_bir_lowering=False)
    es = ExitStack()
    pg = Prog(nc)

    def dram_in(name, shape, dt=F32):
        return nc.dram_tensor(name, list(shape), dt, kind="ExternalInput").ap()

    def dram_tmp(name, shape, dt=F32):
        return nc.dram_tensor(name, list(shape), dt, kind="Internal").ap()

    xT = dram_in("xT", [D, NT])
    cT = dram_in("cT", [128, KD, 2])
    wmod = dram_in("wmod", [D, CPC * 128])
    bmod = dram_in("bmod", [128, CPC])
    gains = dram_in("gains", [128, 4, KD])
    w1i = dram_in("w_ffn1_in", [D, 2 * DFF])
    w1o = dram_in("w_ffn1_out", [DFF, D])
    w2i = dram_in("w_ffn2_in", [D, 2 * DFF])
    w2o = dram_in("w_ffn2_out", [DFF, D])
    w_in = dram_in("w_in", [D, D_IN])
    w_out = dram_in("w_out", [2048, D])
    gconst_d = dram_in("gconst", [128, GC_N])
    w2aug_d = dram_in("w2aug", [17, 2, 512])
    gnb_d = dram_in("gnb", [128, 256])
    convw_d = dram_in("convw", [128, 8, 31])
    convp_d = dram_in("convp", [128, 8, 3])
    segmask_d = dram_in("segmask", [128, 16])
    outT = nc.dram_tensor("outT", [D, NL], F32, kind="ExternalOutput").ap()
    mod_in = dram_tmp("mod_in", [128, 2 * CPC])
    mod_out = dram_tmp("mod_out", [4 * 128, 2 * CPC])
    h_spill = dram_tmp("h_spill", [128, KD * NL])
    sg_spill = dram_tmp("sg_spill", [NL, 1024], BF16)
    u_row = dram_tmp("u_row", [512, NL])
    yrow_spill = dram_tmp("yrow_spill", [512, NL])
    ycol_spill = dram_tmp("ycol_spill", [512, NL])
    ucol_in = [dram_tmp("ucol_in%d" % i, [256, NL]) for i in range(2)]
    ucol_g = [dram_tmp("ucol_g%d" % i, [4 * 256, NL]) for i in range(2)]
    st_in = [dram_tmp("st_in%d" % i, [128, EW]) for i in range(4)]
    st_out = [dram_tmp("st_out%d" % i, [4 * 128, EW]) for i in range(4)]
    dbg = {}

    def dbg_out(name, shape, dt=F32):
        dbg[name] = nc.dram_tensor("dbg_" + name, list(shape), dt, kind="ExternalOutput").ap()
        return dbg[name]

    def sb(name, shape, dt):
        return es.enter_context(nc.sbuf_tensor(name, list(shape), dt))

    HN = max(KD * NT, 17408)
    H = sb("H", [128, HN], F32)
    hT = H[:, 0:KD * NT].rearrange("p (k t) -> p k t", t=NT)
    HMN = max(KD * NT, 17408)
    HM = sb("HM", [128, HMN], BF16)
    hm = HM[:, 0:KD * NT].rearrange("p (k t) -> p k t", t=NT)
    wr = [sb("wr%d" % i, [128, 8192], BF16) for i in range(2)]
    NRING = 2
    ACTA = sb("ACTA", [128, 12 * NT], BF16)
    act = ACTA[:, :].rearrange("p (f t) -> p f t", t=NT)
    TMPA = sb("TMPA", [128, 3072], F32)
    ones_bf = sb("ones_bf", [128, 128], BF16)
    cs = sb("cs", [128, KD, 2], F32)
    cs_bf = sb("cs_bf", [128, KD, 2], BF16)
    bmod_sb = sb("bmod_sb", [128, CPC], F32)
    modloc = sb("modloc", [128, 2, CPC], F32)
    modL = sb("modL", [128, 9 * KD], F32)
    modC = sb("modC", [128, 9 * KD], F32)
    gains_sb = sb("gains_sb", [128, 4, KD], F32)
    coef = sb("coef", [128, 2, 3, 3, KD], F32)
    rb = sb("rb", [128, NT], F32)
    epsc = sb("epsc", [128, 4], F32)
    gconst = sb("gconst_sb", [128, GC_N], F32)
    w2aug = sb("w2aug_sb", [17, 2, 512], F32)
    gnb = sb("gnb_sb", [128, 256], F32)
    convw = sb("convw_sb", [128, 8, 31], F32)
    convp = sb("convp_sb", [128, 8, 3], F32)
    segmask = sb("segmask_sb", [128, 16], F32)
    pgk = sb("pgk", [32, 2, NT], F32)
    ps = [es.enter_context(nc.psum_tensor("ps%d" % i, [128, 512], F32)) for i in range(8)]

    sq = [TMPA[:, 0:256].bitcast(BF16), TMPA[:, 256:512].bitcast(BF16)]
    rtmp = TMPA[:, 512:1024]
    ntmp = [TMPA[:, 1024:1536], TMPA[:, 1536:2048]]
    sg = [TMPA[:, 2048:2560], TMPA[:, 2560:3072]]

    TILES3 = [(0, 512), (512, 512), (1024, 64)]
    TILES2 = [(0, 512), (512, 512)]
    HALL = [("h", k) for k in range(KD)]
    HMALL = [("hm", k) for k in range(KD)]

    def transfer(old, new):
        u = {}
        for o in old:
            w = pg.res_w.get(o)
            if w is not None:
                u[w[0]] = max(u.get(w[0], 0), w[1])
            for src, val in pg.res_r.get(o, {}).items():
                u[src] = max(u.get(src, 0), val)
        for n in new:
            pg.res_w.pop(n, None)
            pg.res_r[n] = dict(u)

    def dma_custom(q, fn, reads, writes, key):
        for src, val in pg._deps(q, reads, writes):
            pg.ops[q].append(("wait", src, val))
        pg.dma_cnt[key] = pg.dma_cnt.get(key, 0) + 1
        val = 16 * pg.dma_cnt[key]
        src = ("dma", key)
        pg.ops[q].append(("custom16", fn, key))
        for r in reads:
            pg.res_r.setdefault(r, {})[src] = val
        for w in writes:
            pg.res_w[w] = (src, val)
            pg.res_r[w] = {}

    xTv = xT.rearrange("(k p) t -> p k t", p=128)
    kstep = max(1, KD // 4)
    for k0 in range(0, KD, kstep):
        pg.dma("sp", hT[:, k0:k0 + kstep, :], xTv[:, k0:k0 + kstep, :],
               writes=[("h", k) for k in range(k0, k0 + kstep)], key="ld_x", total=True)
    for dst, src, name in ((cs[:], cT[:, :, :], "cs"), (bmod_sb[:], bmod[:, :], "bmod"),
                           (gains_sb[:], gains[:, :, :], "gains"), (gconst[:], gconst_d[:, :], "gconst"),
                           (w2aug[:], w2aug_d[:, :, :], "w2aug"), (gnb[:], gnb_d[:, :], "gnb"),
                           (convw[:], convw_d[:, :, :], "convw"), (convp[:], convp_d[:, :, :], "convp"),
                           (segmask[:], segmask_d[:, :], "segmask")):
        pg.dma("sp", dst, src, writes=[name], key="const", total=True)
    pg.op("dve", lambda e: e.memset(ones_bf[:], 1.0 / D), writes=["ones"])
    pg.op("dve", lambda e: e.memset(epsc[:, 0:1], RMS_EPS), writes=["epsc"])
    pg.op("dve", lambda e: e.memset(epsc[:, 1:2], 1e-5), writes=["epsc"])
    pg.op("dve", lambda e: e.memset(epsc[:, 2:3], 1.0), writes=["epsc"])
    pg.op("dve", lambda e: e.memset(pgk[:], 1.0), writes=["pgk"])

    ring = {"n": 0}

    def load_slab(pieces):
        s = ring["n"] % NRING
        ring["n"] += 1
        for vf, src in pieces:
            pg.dma("pool", vf(wr[s]), src, writes=[("wr", s)], key=("wr", s))
        return s

    def rms_stats(tiles):
        cnt = 0
        for (t0, n) in tiles:
            for k in range(KD):
                si = cnt % 2
                sqb = sq[si]
                cnt += 1
                pg.op("act", (lambda e, sqb=sqb, k=k, t0=t0, n=n: e.activation(
                    out=sqb[:, 0:n], in_=hT[:, k, t0:t0 + n], func=AF.Square)),
                    reads=[("h", k)], writes=[("sq", si)])
                pg.op("pe", (lambda e, sqb=sqb, k=k, n=n: e.matmul(
                    ps[6][:, 0:n], lhsT=ones_bf[:, :], rhs=sqb[:, 0:n], start=(k == 0), stop=(k == KD - 1))),
                    reads=[("sq", si), "ones"], writes=[("ps", 6)], signal=True)
            pg.op("act", (lambda e, n=n: e.activation(out=rtmp[:, 0:n], in_=ps[6][:, 0:n], func=AF.Sqrt,
                                                      bias=epsc[:, 0:1], scale=1.0)),
                  reads=[("ps", 6), "epsc"], writes=["rtmp"])
            pg.op("dve", (lambda e, t0=t0, n=n: e.reciprocal(out=rb[:, t0:t0 + n], in_=rtmp[:, 0:n])),
                  reads=["rtmp"], writes=[("rb", t0)])

    pg.op("act", lambda e: e.activation(out=cs_bf[:], in_=cs[:], func=AF.Silu), reads=["cs"], writes=["cs_bf"])
    wmv = wmod.rearrange("(k p) n -> p k n", p=128)
    rms_stats(TILES3)
    MC = 4 if CPC % 4 == 0 else 2
    for sl in range(CPC // MC):
        s = load_slab([(lambda t: t[:, 0:KD * 128 * MC].rearrange("p (k n) -> p k n", n=128 * MC),
                        wmv[:, :, sl * 128 * MC:(sl + 1) * 128 * MC])])
        wv = wr[s][:, 0:KD * 128 * MC].rearrange("p (k n) -> p k n", n=128 * MC)
        for gi in range(MC):
            gl = sl * MC + gi
            for k in range(KD):
                pg.op("pe", (lambda e, wv=wv, k=k, gl=gl, gi=gi: e.matmul(
                    ps[7][:, gl * 2:gl * 2 + 2], lhsT=wv[:, k, gi * 128:(gi + 1) * 128], rhs=cs_bf[:, k, :],
                    start=(k == 0), stop=(k == KD - 1))),
                    reads=[("wr", s), "cs_bf"], writes=[("ps", 7)], signal=(k == KD - 1))
    pg.op("dve", lambda e: e.tensor_tensor(
        out=modloc[:].rearrange("p v g -> p g v"),
        in0=ps[7][:, 0:2 * CPC].rearrange("p (g v) -> p g v", v=2),
        in1=bmod_sb[:].unsqueeze(2).to_broadcast([128, CPC, 2]), op=ALU.add),
        reads=[("ps", 7), "bmod"], writes=["modloc"])
    pg.dma("sp", mod_in[:, :], modloc[:].rearrange("p v g -> p (v g)"), reads=["modloc"], writes=["mod_in"],
           key="mod_io")
    pg.custom("pool", lambda e: e.collective_compute(
        "AllGather", ALU.bypass, replica_groups=[[0, 1, 2, 3], [4, 5, 6, 7]],
        ins=[mod_in[:, :]], outs=[mod_out[:, :]]), reads=["mod_in"], writes=["mod_out"], key="cc_mod")
    mov = mod_out.rearrange("(r p) n -> p r n", p=128)

    pidc = {}

    def get_pid(eng):
        if "pid" not in pidc:
            pidc["pid"] = eng.partition_id()
        return pidc["pid"]

    pg.dma("sp", modL[:].rearrange("p (r g) -> p r g", g=CPC), mov[:, :, 0:CPC], reads=["mod_out"],
           writes=["modL"], key="ld_mod", total=True)
    pg.dma("sp", modC[:].rearrange("p (r g) -> p r g", g=CPC), mov[:, :, CPC:2 * CPC], reads=["mod_out"],
           writes=["modC"], key="ld_mod", total=True)
    if debug:
        dm = dbg_out("mod", [128, 2, 9 * KD])
        pg.dma("sp", dm[:, 0, :], modL[:], reads=["modL"], key=("dbg", 1))
        pg.dma("sp", dm[:, 1, :], modC[:], reads=["modC"], key=("dbg", 2))

    for wi, m in enumerate((modL, modC)):
        mv = m[:].rearrange("p (m k) -> p m k", k=KD)
        for s_ in range(3):
            fac = 1.0 if s_ == 1 else 0.5
            pg.op("dve", (lambda e, wi=wi, mv=mv, s_=s_: e.scalar_tensor_tensor(
                out=coef[:, wi, s_, 0, :], in0=mv[:, 3 * s_ + 1, :], scalar=1.0, in1=gains_sb[:, s_, :],
                op0=ALU.add, op1=ALU.mult)), reads=["modL", "modC", "gains"], writes=["coef"])
            pg.op("dve", (lambda e, wi=wi, mv=mv, s_=s_: e.tensor_copy(
                out=coef[:, wi, s_, 1, :], in_=mv[:, 3 * s_, :])), reads=["modL", "modC"], writes=["coef"])
            pg.op("dve", (lambda e, wi=wi, mv=mv, s_=s_, fac=fac: e.tensor_scalar(
                out=coef[:, wi, s_, 2, :], in0=mv[:, 3 * s_ + 2, :], scalar1=fac, scalar2=None, op0=ALU.mult)),
                reads=["modL", "modC"], writes=["coef"])

    def norm_mod(s_, tiles, stats_done=False):
        if not stats_done:
            rms_stats(tiles)
        cnt = 0
        for (t0, n) in tiles:
            wi = 1 if t0 >= NL else 0
            for k in range(KD):
                ti = cnt % 2
                tb = ntmp[ti]
                cnt += 1
                pg.op("dve", (lambda e, tb=tb, k=k, t0=t0, n=n: e.tensor_tensor(
                    out=tb[:, 0:n], in0=hT[:, k, t0:t0 + n], in1=rb[:, t0:t0 + n], op=ALU.mult)),
                    reads=[("h", k), ("rb", t0)], writes=[("ntmp", ti)])
                pg.op("act", (lambda e, tb=tb, k=k, t0=t0, n=n, wi=wi: e.activation(
                    out=hm[:, k, t0:t0 + n], in_=tb[:, 0:n], func=AF.Identity,
                    bias=coef[:, wi, s_, 1, k:k + 1], scale=coef[:, wi, s_, 0, k:k + 1])),
                    reads=[("ntmp", ti), "coef"], writes=[("hm", k)])

    def ffn(w_i, w_o, s_, tiles):
        wiv = w_i.rearrange("(k p) n -> p k n", p=128)
        wov = w_o.rearrange("(f p) n -> p f n", p=128)
        nsl_tot = NFC // 2
        ngr = min(4, nsl_tot)
        groups = []
        a = 0
        for gi in range(ngr):
            n_ = nsl_tot // ngr + (1 if gi < nsl_tot % ngr else 0)
            groups.append((a, n_))
            a += n_
        assert max(n_ for _, n_ in groups) * 2 <= 12
        pair = 0
        oc = 0
        for (sl0, nsl) in groups:
            nf = 2 * nsl
            for sl in range(sl0, sl0 + nsl):
                s = load_slab([
                    (lambda t: t[:, 0:KD * 512].rearrange("p (k g n) -> p k g n", g=2, n=256)[:, :, 0, :],
                     wiv[:, :, sl * 256:(sl + 1) * 256]),
                    (lambda t: t[:, 0:KD * 512].rearrange("p (k g n) -> p k g n", g=2, n=256)[:, :, 1, :],
                     wiv[:, :, DFF + sl * 256:DFF + (sl + 1) * 256]),
                ])
                wv = wr[s][:, 0:KD * 512].rearrange("p (k g n) -> p k g n", g=2, n=256)
                for fi in range(2):
                    fl = (sl - sl0) * 2 + fi
                    for (t0, n) in tiles:
                        pG = 2 * (pair % 2)
                        pU = pG + 1
                        sgi = pair % 2
                        pair += 1
                        for g_, pb in ((0, pG), (1, pU)):
                            for k in range(KD):
                                pg.op("pe", (lambda e, wv=wv, k=k, g_=g_, fi=fi, pb=pb, t0=t0, n=n: e.matmul(
                                    ps[pb][:, 0:n], lhsT=wv[:, k, g_, fi * 128:(fi + 1) * 128],
                                    rhs=hm[:, k, t0:t0 + n], start=(k == 0), stop=(k == KD - 1))),
                                    reads=[("wr", s), ("hm", k)], writes=[("ps", pb)], signal=(k == KD - 1))
                        pg.op("act", (lambda e, pG=pG, sgi=sgi, n=n: e.activation(
                            out=sg[sgi][:, 0:n], in_=ps[pG][:, 0:n], func=AF.Silu)),
                            reads=[("ps", pG)], writes=[("sg", sgi)])
                        pg.op("dve", (lambda e, pU=pU, sgi=sgi, fl=fl, t0=t0, n=n: e.tensor_tensor(
                            out=act[:, fl, t0:t0 + n], in0=sg[sgi][:, 0:n], in1=ps[pU][:, 0:n], op=ALU.mult)),
                            reads=[("sg", sgi), ("ps", pU)], writes=[("act", fl)])
            f0 = 2 * sl0
            for ds_ in range(D // 512):
                s = load_slab([(lambda t, nf=nf: t[:, 0:nf * 512].rearrange("p (f n) -> p f n", n=512),
                                wov[:, f0:f0 + nf, ds_ * 512:(ds_ + 1) * 512])])
                wv = wr[s][:, 0:nf * 512].rearrange("p (f n) -> p f n", n=512)
                for dc in range(4):
                    dg = ds_ * 4 + dc
                    for (t0, n) in tiles:
                        pb = 4 + (oc % 2)
                        oc += 1
                        for fl in range(nf):
                            pg.op("pe", (lambda e, wv=wv, fl=fl, dc=dc, pb=pb, t0=t0, n=n: e.matmul(
                                ps[pb][:, 0:n], lhsT=wv[:, fl, dc * 128:(dc + 1) * 128],
                                rhs=act[:, fl, t0:t0 + n], start=(fl == 0), stop=(fl == nf - 1))),
                                reads=[("wr", s), ("act", fl)], writes=[("ps", pb)], signal=(fl == nf - 1))
                        subs = []
                        if t0 < NL:
                            subs.append((t0, min(t0 + n, NL) - t0, 0))
                        if t0 + n > NL:
                            a0 = max(t0, NL)
                            subs.append((a0, t0 + n - a0, 1))
                        for (a0, an, wi) in subs:
                            pg.op("dve", (lambda e, pb=pb, dg=dg, t0=t0, a0=a0, an=an, wi=wi: e.scalar_tensor_tensor(
                                out=hT[:, dg, a0:a0 + an], in0=ps[pb][:, a0 - t0:a0 - t0 + an],
                                scalar=coef[:, wi, s_, 2, dg:dg + 1], in1=hT[:, dg, a0:a0 + an],
                                op0=ALU.mult, op1=ALU.add)),
                                reads=[("ps", pb), "coef", ("h", dg)], writes=[("h", dg)])

    norm_mod(0, TILES3, stats_done=True)
    ffn(w1i, w1o, 0, [(0, 384), (384, 384), (768, 320)])
    if debug:
        d1 = dbg_out("h1", [D, NT])
        pg.dma("sp", d1.rearrange("(k p) t -> p k t", p=128), hT, reads=HALL, key=("dbg", 3))

    if stage >= 2:
        mixer_args = dict(locals())
        _mixer(mixer_args)

    if stage >= 3:
        norm_mod(2, TILES2)
        ffn(w2i, w2o, 2, TILES2)

    rms_stats(TILES2)
    for k in range(KD):
        for (t0, n) in TILES2:
            pg.op("dve", (lambda e, k=k, t0=t0, n=n: e.scalar_tensor_tensor(
                out=hT[:, k, t0:t0 + n], in0=hT[:, k, t0:t0 + n], scalar=gains_sb[:, 3, k:k + 1],
                in1=rb[:, t0:t0 + n], op0=ALU.mult, op1=ALU.mult)),
                reads=[("h", k), ("rb", t0), "gains"], writes=[("h", k)])
    outv = outT.rearrange("(k p) t -> p k t", p=128)
    for k0 in range(0, KD, kstep):
        pg.dma("sp", outv[:, k0:k0 + kstep, :], hT[:, k0:k0 + kstep, 0:NL],
               reads=[("h", k) for k in range(k0, k0 + kstep)], writes=[("out", k0)], key="st_out", total=True)
    pg.wait_all("sp", [("out", k0) for k0 in range(0, KD, kstep)])
    for key in list(pg.dma_cnt):
        if isinstance(key, tuple) and key[0] == "dbg":
            pg.ops["sp"].append(("wait", ("dma", key), 16 * pg.dma_cnt[key]))
    return nc, pg, es


def _mixer(a):
    from types import SimpleNamespace
    v = SimpleNamespace(**a)
    pg, nc, KD, D = v.pg, v.nc, v.KD, v.D
    H, HM, ACTA, TMPA, ps, wr = v.H, v.HM, v.ACTA, v.TMPA, v.ps, v.wr
    hT, hm, gconst, pgk = v.hT, v.hm, v.gconst, v.pgk
    TILES2, TILES3 = v.TILES2, v.TILES3
    load_slab, transfer, dma_custom = v.load_slab, v.transfer, v.dma_custom
    sq, ntmp, sg, rtmp, rb, epsc = v.sq, v.ntmp, v.sg, v.rtmp, v.rb, v.epsc
    debug = v.debug
    w_in_v = v.w_in.rearrange("(k p) n -> p k n", p=128)

    def bail():
        names = set(pg.res_w) | set(pg.res_r)
        transfer(list(names), v.HALL + v.HMALL + [("act", f) for f in range(12)]
                 + [("sq", 0), ("sq", 1), "rtmp", ("ntmp", 0), ("ntmp", 1), ("sg", 0), ("sg", 1)])
        pg.dma("sp", hT[:, :, 0:NL], v.h_spill.rearrange("p (k t) -> p k t", t=NL), reads=["hspill"], writes=v.HALL,
               key="unspill")

    mstop = v.cfg.mstop

    def hmb(off, nbytes, dt=F32):
        x = HM[:, off // 2:(off + nbytes) // 2]
        return x.bitcast(F32) if dt == F32 else x

    v.norm_mod(1, TILES3)
    pg.dma("sp", v.h_spill.rearrange("p (k t) -> p k t", t=NL), hT[:, :, 0:NL], reads=v.HALL, writes=["hspill"],
           key="spill")
    o_acc = H[:, 0:8192].rearrange("p (i c) -> p i c", c=1024)
    v_tm = H[:, 8192:12800].bitcast(BF16).rearrange("p (i c) -> p i c", c=1024)
    mixg = H[:, 12800:16896].bitcast(BF16).rearrange("p (i t) -> p i t", t=1024)
    uext = H[:, 8192:8192 + 46 * 64]
    OACC = [("oacc", i) for i in range(8)]
    VTM = [("vtm", i) for i in range(9)]
    transfer(v.HALL, OACC + VTM + ["mixg", ("qks", 2), ("qks", 3), ("attT", 2), ("attT", 3)])
    kT = ACTA[:, 0:4352].rearrange("p (h t) -> p h t", t=NT)
    qT = ACTA[:, 4352:8448].rearrange("p (h t) -> p h t", t=NL)
    k_tm = ACTA[:, 8448:13056].rearrange("p (i c) -> p i c", c=512)
    mixc = ACTA[:, 0:8192].rearrange("p (i t) -> p i t", t=1024)
    KT = [("kT", h) for h in range(4)]
    QT = [("qT", h) for h in range(4)]
    KTM = [("ktm", i) for i in range(9)]
    transfer([("act", f) for f in range(12)], KT + QT + KTM)

    PADS = []

    def slab512(col0, ncols=512):
        s = load_slab([(lambda t: t[:, 0:KD * ncols].rearrange("p (k n) -> p k n", n=ncols),
                        w_in_v[:, :, col0:col0 + ncols])])
        return s, wr[s][:, 0:KD * ncols].rearrange("p (k n) -> p k n", n=ncols)

    pair = 0
    for sl in range(4):
        s = load_slab([
            (lambda t: t[:, 0:KD * 512].rearrange("p (k g n) -> p k g n", g=2, n=256)[:, :, 0, :],
             w_in_v[:, :, OFF_GA + sl * 256:OFF_GA + (sl + 1) * 256]),
            (lambda t: t[:, 0:KD * 512].rearrange("p (k g n) -> p k g n", g=2, n=256)[:, :, 1, :],
             w_in_v[:, :, OFF_GB + sl * 256:OFF_GB + (sl + 1) * 256]),
        ])
        wv = wr[s][:, 0:KD * 512].rearrange("p (k g n) -> p k g n", g=2, n=256)
        for fi in range(2):
            cc = sl * 2 + fi
            for (t0, n) in TILES2:
                pA = 2 * (pair % 2)
                pB = pA + 1
                bi = pair % 2
                pair += 1
                for g_, pb in ((0, pA), (1, pB)):
                    for k in range(KD):
                        pg.op("pe", (lambda e, wv=wv, k=k, g_=g_, fi=fi, pb=pb, t0=t0, n=n: e.matmul(
                            ps[pb][:, 0:n], lhsT=wv[:, k, g_, fi * 128:(fi + 1) * 128], rhs=hm[:, k, t0:t0 + n],
                            start=(k == 0), stop=(k == KD - 1))),
                            reads=[("wr", s), ("hm", k)], writes=[("ps", pb)], signal=(k == KD - 1))
                pg.op("act", (lambda e, pB=pB, bi=bi, n=n: e.activation(
                    out=sg[bi][:, 0:n], in_=ps[pB][:, 0:n], func=AF.Sigmoid)),
                    reads=[("ps", pB)], writes=[("sg", bi)])
                pg.op("dve", (lambda e, pA=pA, bi=bi, n=n: e.tensor_tensor(
                    out=ntmp[bi][:, 0:n], in0=sg[bi][:, 0:n], in1=ps[pA][:, 0:n], op=ALU.mult)),
                    reads=[("sg", bi), ("ps", pA)], writes=[("ntmp", bi)])
                c4 = cc % 4
                if cc < 4:
                    dst = v.u_row[c4 * 128:(c4 + 1) * 128, t0:t0 + n]
                else:
                    dst = v.ucol_in[c4 // 2][(c4 % 2) * 128:(c4 % 2 + 1) * 128, t0:t0 + n]
                pg.dma("sp", dst, ntmp[bi][:, 0:n], reads=[("ntmp", bi)],
                       writes=[("udram", cc, t0)], key=("ust", bi))
    UCOL = [("udram", cc, t0) for cc in range(4, 8) for (t0, n) in TILES2]
    UROW = [("udram", cc, t0) for cc in range(4) for (t0, n) in TILES2]
    row_stores = []
    for cc in range(4):
        RB = H[:, cc * 1024:(cc + 1) * 1024]
        pg.dma("sp", RB, v.u_row[cc * 128:(cc + 1) * 128, :], reads=UROW, writes=[("oacc", cc)], key=("ld_ur", cc))
    for cc in range(4):
        RB = H[:, cc * 1024:(cc + 1) * 1024]
        RA = H[:, (4 + cc) * 1024:(5 + cc) * 1024]
        nb, na = ("oacc", cc), ("oacc", 4 + cc)
        pg.op("dve", (lambda e, cc=cc, RB=RB, RA=RA: e.tensor_scalar(
            out=RA, in0=RB, scalar1=v.convw[:, cc, 15:16], scalar2=v.convp[:, cc, 0:1], op0=ALU.mult, op1=ALU.add)),
            reads=[nb, "convw", "convp"], writes=[na])
        for j in range(31):
            if j == 15:
                continue
            s_ = j - 15
            c0, c1 = max(0, -s_), min(64, 64 - s_)
            y3 = RA.rearrange("p (r c) -> p r c", c=64)[:, :, c0:c1]
            u3 = RB.rearrange("p (r c) -> p r c", c=64)[:, :, c0 + s_:c1 + s_]
            pg.op("dve", (lambda e, cc=cc, j=j, y3=y3, u3=u3: e.scalar_tensor_tensor(
                out=y3, in0=u3, scalar=v.convw[:, cc, j:j + 1], in1=y3, op0=ALU.mult, op1=ALU.add)),
                reads=[nb, "convw", na], writes=[na])
        row_stores.append((cc, RA, na))

    def emit_row_stores():
        for (cc, RA, na) in row_stores:
            pg.dma("sp", v.yrow_spill[cc * 128:(cc + 1) * 128, :], RA, reads=[na], writes=[("yrow", cc)],
                   key=("st_yr", cc))
    if mstop <= 1:
        return bail()
    cnt = 0
    for sl in range(2):
        s, wv = slab512(OFF_G + sl * 512)
        for ti in range(8):
            pb = 4 + (cnt % 2)
            bi = cnt % 2
            cnt += 1
            for k in range(KD):
                pg.op("pe", (lambda e, wv=wv, k=k, pb=pb, ti=ti: e.matmul(
                    ps[pb][:, :], lhsT=hm[:, k, ti * 128:(ti + 1) * 128], rhs=wv[:, k, :],
                    start=(k == 0), stop=(k == KD - 1))),
                    reads=[("wr", s), ("hm", k)], writes=[("ps", pb)], signal=(k == KD - 1))
            pg.op("act", (lambda e, pb=pb, bi=bi: e.activation(out=sq[bi][:, :], in_=ps[pb][:, :], func=AF.Silu)),
                  reads=[("ps", pb)], writes=[("sq", bi)])
            pg.dma("sp", v.sg_spill[ti * 128:(ti + 1) * 128, sl * 512:(sl + 1) * 512], sq[bi][:, :],
                   reads=[("sq", bi)], writes=[("sgsp", ti, sl)], key=("sgst", bi))

    for sl in range(2):
        s, wv = slab512(OFF_V + sl * 512)
        for ti in range(9):
            np_ = 128 if ti < 8 else 64
            pb = 4 + (cnt % 2)
            cnt += 1
            for k in range(KD):
                pg.op("pe", (lambda e, wv=wv, k=k, pb=pb, ti=ti, np_=np_: e.matmul(
                    ps[pb][0:np_, :], lhsT=hm[:, k, ti * 128:ti * 128 + np_], rhs=wv[:, k, :],
                    start=(k == 0), stop=(k == KD - 1))),
                    reads=[("wr", s), ("hm", k)], writes=[("ps", pb)], signal=(k == KD - 1))
            pg.op("act", (lambda e, pb=pb, ti=ti, sl=sl, np_=np_: e.copy(
                out=v_tm[0:np_, ti, sl * 512:(sl + 1) * 512], in_=ps[pb][0:np_, :])),
                reads=[("ps", pb)], writes=[("vtm", ti)])

    s, wv = slab512(OFF_K)
    for ti in range(9):
        np_ = 128 if ti < 8 else 64
        pb = 4 + (cnt % 2)
        cnt += 1
        for k in range(KD):
            pg.op("pe", (lambda e, wv=wv, k=k, pb=pb, ti=ti, np_=np_: e.matmul(
                ps[pb][0:np_, :], lhsT=hm[:, k, ti * 128:ti * 128 + np_], rhs=wv[:, k, :],
                start=(k == 0), stop=(k == KD - 1))),
                reads=[("wr", s), ("hm", k)], writes=[("ps", pb)], signal=(k == KD - 1))
        pg.op("act", (lambda e, pb=pb, ti=ti, np_=np_: e.copy(out=k_tm[0:np_, ti, :], in_=ps[pb][0:np_, :])),
              reads=[("ps", pb)], writes=[("ktm", ti)])
    for h in range(4):
        for (t0, n) in TILES3:
            pb = 4 + (cnt % 2)
            cnt += 1
            for k in range(KD):
                pg.op("pe", (lambda e, wv=wv, k=k, pb=pb, h=h, t0=t0, n=n: e.matmul(
                    ps[pb][:, 0:n], lhsT=wv[:, k, h * 128:(h + 1) * 128], rhs=hm[:, k, t0:t0 + n],
                    start=(k == 0), stop=(k == KD - 1))),
                    reads=[("wr", s), ("hm", k)], writes=[("ps", pb)], signal=(k == KD - 1))
            pg.op("act", (lambda e, pb=pb, h=h, t0=t0, n=n: e.copy(out=kT[:, h, t0:t0 + n], in_=ps[pb][:, 0:n])),
                  reads=[("ps", pb)], writes=[("kT", h)])
    s, wv = slab512(OFF_Q)
    for h in range(4):
        for (t0, n) in TILES2:
            pb = 4 + (cnt % 2)
            cnt += 1
            for k in range(KD):
                pg.op("pe", (lambda e, wv=wv, k=k, pb=pb, h=h, t0=t0, n=n: e.matmul(
                    ps[pb][:, 0:n], lhsT=wv[:, k, h * 128:(h + 1) * 128], rhs=hm[:, k, t0:t0 + n],
                    start=(k == 0), stop=(k == KD - 1))),
                    reads=[("wr", s), ("hm", k)], writes=[("ps", pb)], signal=(k == KD - 1))
            pg.op("act", (lambda e, pb=pb, h=h, t0=t0, n=n: e.mul(out=qT[:, h, t0:t0 + n], in_=ps[pb][:, 0:n],
                                                                 mul=float(128 ** -0.5))),
                  reads=[("ps", pb)], writes=[("qT", h)])
    s, wv = slab512(OFF_GKF, 32)
    for d in range(2):
        for (t0, n) in TILES3:
            pb = 4 + (cnt % 2)
            cnt += 1
            for k in range(KD):
                pg.op("pe", (lambda e, wv=wv, k=k, pb=pb, d=d, t0=t0, n=n: e.matmul(
                    ps[pb][0:16, 0:n], lhsT=wv[:, k, d * 16:(d + 1) * 16], rhs=hm[:, k, t0:t0 + n],
                    start=(k == 0), stop=(k == KD - 1))),
                    reads=[("wr", s), ("hm", k)], writes=[("ps", pb)], signal=(k == KD - 1))
            pg.op("act", (lambda e, pb=pb, d=d, t0=t0, n=n: e.copy(out=pgk[0:16, d, t0:t0 + n],
                                                                 in_=ps[pb][0:16, 0:n])),
                  reads=[("ps", pb)], writes=["pgk"])

    emit_row_stores()
    for i_ in range(2):
        pg.custom("pool", (lambda e, i_=i_: e.collective_compute(
            "AllGather", ALU.bypass, replica_groups=[[0, 1, 2, 3], [4, 5, 6, 7]],
            ins=[v.ucol_in[i_][:, :]], outs=[v.ucol_g[i_][:, :]])), reads=UCOL, writes=[("ucolg", i_)],
            key=("cc_u", i_))


    if mstop <= 2:
        return bail()
    st_loc = hmb(0, 4 * EW * 4).rearrange("p (e w) -> p e w", w=EW)
    Sbf = HM[:, 8224:8224 + 2048].rearrange("p (d h e) -> p d h e", d=2, h=4)
    off = 20544
    spb = hmb(off, 2048); off += 2048
    edb = hmb(off, 2048); off += 2048
    kdb = HM[:, off // 2:off // 2 + 512]; off += 1024
    decb = hmb(off, 32); off += 32
    ebe = [hmb(off + i * 1024, 1024) for i in range(2)]; off += 2048
    e1n = [hmb(off + i * 512, 512) for i in range(2)]; off += 1024
    qks = [[HM[:, (off + (i * 3 + j) * 256) // 2:(off + (i * 3 + j) * 256) // 2 + 128] for j in range(3)]
           for i in range(2)]; off += 1536
    attT = [HM[:, (off + i * 256) // 2:(off + i * 256) // 2 + 128] for i in range(2)]; off += 512
    assert off <= 34816
    ebe += [rb[:, 0:256], rb[:, 256:512]]
    e1n += [rb[:, 512:640], rb[:, 640:768]]
    HT_ = H[:, 16896:17408].bitcast(BF16)
    qks += [[HT_[:, (i * 3 + j) * 128:(i * 3 + j + 1) * 128] for j in range(3)] for i in range(2)]
    attT += [HT_[:, 768 + i * 128:768 + (i + 1) * 128] for i in range(2)]
    transfer([("rb", 0), ("rb", 512), ("rb", 1024)], [("ebe", 2), ("ebe", 3), ("e1n", 2), ("e1n", 3)])
    og = TMPA[:, 0:1024]
    sgt = TMPA[:, 1024:1536].bitcast(BF16)
    ebuf = TMPA[:, 1536:1536 + EW]
    t1 = TMPA[:, 2564:2820]
    ssb = TMPA[:, 2820:2828]
    GT = [("stloc", 0), ("stloc", 1)] + [(n_, d_, h_) for n_ in ("S", "Sbf") for d_ in range(2) for h_ in range(4)] + [ "spb", "edb", "kdb", "decb", ("ebe", 0), ("ebe", 1), ("e1n", 0), ("e1n", 1),
          ("qks", 0), ("qks", 1), ("attT", 0), ("attT", 1)]
    transfer(v.HMALL, GT)
    TM = ["og", "sgt", "ebuf", "t1", "ssb"]
    transfer([("sq", 0), ("sq", 1), "rtmp", ("ntmp", 0), ("ntmp", 1), ("sg", 0), ("sg", 1)], TM)

    def Sv(d):
        return st_loc[:, 2 * d + 1, 0:1024].rearrange("p (h e) -> p h e", e=256)

    def Sx(d):
        return st_loc[:, 2 * d, 0:1024].rearrange("p (h e) -> p h e", e=256)

    pg.op("dve", lambda e: e.memset(st_loc[:, :, 0:1024], 0.0), writes=[("stloc", 0), ("stloc", 1)])
    pg.op("dve", lambda e: e.memset(st_loc[:, :, 1024:1028], 1.0), writes=[("stloc", 0), ("stloc", 1)])

    Mtb = lambda d: gconst[:, GC_MTB + d * 128:GC_MTB + (d + 1) * 128]
    CE = lambda d: gconst[:, GC_CE + d * 256:GC_CE + (d + 1) * 256]
    CH = gconst[:, GC_CH:GC_CH + 2]
    MASK = lambda d: gconst[:, GC_MASK + d * 128:GC_MASK + (d + 1) * 128]
    IDM = gconst[:, GC_ID:GC_ID + 128]
    ONEF = gconst[:, GC_ONE:GC_ONE + 128]
    hcnt = {"n": 0}

    def gla_tile(ti, d, phase):
        np_ = 128 if ti < 8 else 64
        tok0 = ti * 128
        pg.op("pe", lambda e: e.matmul(ps[0][0:np_, :], lhsT=pgk[0:17, d, tok0:tok0 + np_], rhs=v.w2aug[0:17, d, :],
                                       start=True, stop=True), reads=["pgk", "w2aug"], writes=[("ps", 0)])
        pg.op("act", lambda e: e.activation(out=spb[0:np_, :], in_=ps[0][0:np_, :], func=AF.Exp, scale=-1.0),
              reads=[("ps", 0)], writes=["spb"])
        pg.op("act", lambda e: e.activation(out=spb[0:np_, :], in_=spb[0:np_, :], func=AF.Ln, bias=epsc[0:np_, 2:3],
                                            scale=1.0), reads=["spb", "epsc"], writes=["spb"])
        pg.op("pe", lambda e: e.matmul(ps[1][0:np_, :], lhsT=Mtb(d)[0:np_, 0:np_], rhs=spb[0:np_, :],
                                       start=True, stop=True), reads=["spb", "gconst"], writes=[("ps", 1)])
        pg.op("act", lambda e: e.activation(out=edb[0:np_, :], in_=ps[1][0:np_, :], func=AF.Exp),
              reads=[("ps", 1)], writes=["edb"])
        pg.op("dve", lambda e: e.tensor_tensor(out=kdb[0:np_, :], in0=k_tm[0:np_, ti, :], in1=edb[0:np_, :],
                                               op=ALU.mult), reads=[("ktm", ti), "edb"], writes=["kdb"])
        chunks = [0, 1] if d == 0 else [1, 0]
        if ti == 8:
            chunks = [0]
        if phase == "A":
            for h in range(4):
                pg.op("pe", (lambda e, h=h: e.matmul(ps[3][:, 2 * h:2 * h + 2], lhsT=spb[0:np_, h * 128:(h + 1) * 128],
                                                     rhs=CH[0:np_, :], start=True, stop=True)),
                      reads=["spb", "gconst"], writes=[("ps", 3)], signal=(h == 3))
            pg.op("act", lambda e: e.activation(out=decb[:, 0:8], in_=ps[3][:, 0:8], func=AF.Exp),
                  reads=[("ps", 3)], writes=["decb"])
            dview = decb[:, 0:8].rearrange("p (h x) -> p h x", x=2)
            for X in chunks:
                r0 = X * 64
                for h in range(4):
                    pb = 6 + (h // 2)
                    c0 = (h % 2) * 256
                    pg.op("pe", (lambda e, h=h, pb=pb, c0=c0, r0=r0: e.matmul(
                        ps[pb][:, c0:c0 + 256], lhsT=kdb[r0:r0 + 64, h * 128:(h + 1) * 128],
                        rhs=v_tm[r0:r0 + 64, ti, h * 256:(h + 1) * 256], start=True, stop=True)),
                        reads=["kdb", ("vtm", ti)], writes=[("ps", pb)])
                    if ti < 8:
                        pg.op("dve", (lambda e, h=h, pb=pb, c0=c0, X=X: e.scalar_tensor_tensor(
                            out=Sv(d)[:, h, :], in0=Sv(d)[:, h, :], scalar=decb[:, 2 * h + X:2 * h + X + 1],
                            in1=ps[pb][:, c0:c0 + 256], op0=ALU.mult, op1=ALU.add)),
                            reads=[("ps", pb), "decb", ("stloc", d)], writes=[("stloc", d)])
                    else:
                        pg.op("dve", (lambda e, h=h, pb=pb, c0=c0: e.tensor_copy(
                            out=Sx(d)[:, h, :], in_=ps[pb][:, c0:c0 + 256])),
                            reads=[("ps", pb)], writes=[("stloc", d)])
                if ti < 8:
                    pg.op("dve", (lambda e, X=X: e.tensor_tensor(
                        out=st_loc[:, 2 * d + 1, 1024:1028], in0=st_loc[:, 2 * d + 1, 1024:1028],
                        in1=dview[:, :, X], op=ALU.mult)), reads=["decb", ("stloc", d)], writes=[("stloc", d)])
                else:
                    pg.op("dve", (lambda e, X=X: e.tensor_copy(out=st_loc[:, 2 * d, 1024:1028], in_=dview[:, :, X])),
                          reads=["decb"], writes=[("stloc", d)])
            return
        S = Sv(d)
        for h in (0, 2, 1, 3):
            pbe = 3 + (h // 2)
            cb = (h % 2) * 256
            pg.op("pe", (lambda e, h=h, pbe=pbe, cb=cb: e.matmul(
                ps[pbe][:, cb:cb + 256], lhsT=spb[:, h * 128:(h + 1) * 128], rhs=CE(d), start=True, stop=True)),
                reads=["spb", "gconst"], writes=[("ps", pbe)])
            pg.op("act", (lambda e, h=h, pbe=pbe, cb=cb: e.activation(
                out=ebe[h][:, :], in_=ps[pbe][:, cb:cb + 256], func=AF.Exp)),
                reads=[("ps", pbe)], writes=[("ebe", h)])
            pg.op("act", (lambda e, h=h, pbe=pbe, cb=cb: e.activation(
                out=e1n[h][:, :], in_=ps[pbe][:, cb + 128:cb + 256], func=AF.Exp, scale=-1.0)),
                reads=[("ps", pbe)], writes=[("e1n", h)])
        for h in range(4):
            qb, qs, ks = qks[h]
            pg.op("dve", (lambda e, h=h, qb=qb: e.tensor_tensor(
                out=qb[:, :], in0=qT[:, h, tok0:tok0 + 128], in1=ebe[h][:, 0:128], op=ALU.mult)),
                reads=[("qT", h), ("ebe", h)], writes=[("qks", h)])
            pg.op("dve", (lambda e, h=h, qs=qs: e.tensor_tensor(
                out=qs[:, :], in0=qT[:, h, tok0:tok0 + 128], in1=ebe[h][:, 128:256], op=ALU.mult)),
                reads=[("qT", h), ("ebe", h)], writes=[("qks", h)])
            pg.op("dve", (lambda e, h=h, ks=ks: e.tensor_tensor(
                out=ks[:, :], in0=kT[:, h, tok0:tok0 + 128], in1=e1n[h][:, :], op=ALU.mult)),
                reads=[("kT", h), ("e1n", h)], writes=[("qks", h)])
            pg.op("pe", (lambda e, h=h, ks=ks, qs=qs: e.matmul(
                ps[5][:, h * 128:(h + 1) * 128], lhsT=ks[:, :], rhs=qs[:, :], start=True, stop=True)),
                reads=[("qks", h)], writes=[("ps", 5)])
            pg.op("dve", (lambda e, h=h: e.tensor_tensor(
                out=attT[h][:, :], in0=ps[5][:, h * 128:(h + 1) * 128], in1=MASK(d), op=ALU.mult)),
                reads=[("ps", 5), "gconst"], writes=[("attT", h)])
        for xi, X in enumerate(chunks):
            r0 = X * 64
            for h in range(4):
                qb = qks[h][0]
                po = 6 + (h // 2)
                co = (h % 2) * 256
                kc = (h % 2) * 256
                if xi == 0:
                    pg.op("pe", (lambda e, h=h, po=po, co=co: e.matmul(
                        ps[po][:, co:co + 256], lhsT=attT[h][:, :], rhs=v_tm[:, ti, h * 256:(h + 1) * 256],
                        start=(h % 2 == 0), stop=False, skip_group_check=True)),
                        reads=[("attT", h), ("vtm", ti)], writes=[("ps", po)])
                pg.op("pe", (lambda e, h=h, qb=qb, po=po, co=co, r0=r0, xi=xi: e.matmul(
                    ps[po][r0:r0 + 64, co:co + 256], lhsT=qb[:, r0:r0 + 64], rhs=Sbf[:, d, h, :],
                    start=False, stop=(xi == 1), skip_group_check=True)),
                    reads=[("qks", h), ("Sbf", d, h)], writes=[("ps", po)])
                pg.op("pe", (lambda e, h=h, r0=r0, kc=kc: e.matmul(
                    ps[2][:, kc:kc + 256], lhsT=kdb[r0:r0 + 64, h * 128:(h + 1) * 128],
                    rhs=v_tm[r0:r0 + 64, ti, h * 256:(h + 1) * 256], start=True, stop=True)),
                    reads=["kdb", ("vtm", ti)], writes=[("ps", 2)])
                col = (r0 + 63) if d == 0 else r0
                pg.op("dve", (lambda e, h=h, kc=kc, col=col: e.scalar_tensor_tensor(
                    out=S[:, h, :], in0=S[:, h, :], scalar=ebe[h][:, col:col + 1], in1=ps[2][:, kc:kc + 256],
                    op0=ALU.mult, op1=ALU.add)), reads=[("ps", 2), ("ebe", h), ("S", d, h)], writes=[("S", d, h)])
                pg.op("act", (lambda e, h=h: e.copy(out=Sbf[:, d, h, :], in_=S[:, h, :])),
                      reads=[("S", d, h)], writes=[("Sbf", d, h)])

    def phase_a(d, interleave=False):
        order = list(range(8)) if d == 0 else list(range(7, -1, -1))
        for ti in order + [8]:
            gla_tile(ti, d, "A")
            if interleave:
                emit_col(15)

    def exchange(d):
        for e_ in (2 * d, 2 * d + 1):
            pg.dma("sp", v.st_in[e_][:, :], st_loc[:, e_, :], reads=[("stloc", d)], writes=[("st_in", e_)],
                   key=("st_io", e_))
        for e_ in (2 * d, 2 * d + 1):
            pg.custom("pool", (lambda e, e_=e_: e.collective_compute(
                "AllGather", ALU.bypass, replica_groups=[[0, 1, 2, 3], [4, 5, 6, 7]],
                ins=[v.st_in[e_][:, :]], outs=[v.st_out[e_][:, :]])), reads=[("st_in", e_)],
                writes=[("st_out", e_)], key=("cc_st", e_))

    def combine(d):
        S = Sv(d)
        pg.op("dve", (lambda e, S=S: e.memset(S[:, :, :], 0.0)), reads=[("stloc", d)], writes=[("stloc", d)])
        ctx_order = [0, 1, 2, 3] if d == 0 else [3, 2, 1, 0]
        seg_order = [0, 1, 2] if d == 0 else [3, 2, 1]
        for kind, order in ((0, ctx_order), (1, seg_order)):
            for r in order:
                e_ = 2 * d + kind
                pg.dma("sp", ebuf[:, :], v.st_out[e_][r * 128:(r + 1) * 128, :], reads=[("st_out", e_)],
                       writes=["ebuf"], key="ld_st")
                if kind == 0:
                    for h in range(4):
                        pg.op("dve", (lambda e, h=h, S=S: e.scalar_tensor_tensor(
                            out=S[:, h, :], in0=S[:, h, :], scalar=ebuf[:, 1024 + h:1025 + h],
                            in1=ebuf[:, h * 256:(h + 1) * 256], op0=ALU.mult, op1=ALU.add)),
                            reads=["ebuf", ("stloc", d)], writes=[("stloc", d)])
                else:
                    mcol = (0 if d == 0 else 8) + r
                    pg.op("dve", (lambda e, mcol=mcol: e.tensor_scalar(
                        out=t1[:, 0:4], in0=ebuf[:, 1024:1028], scalar1=v.segmask[:, mcol:mcol + 1],
                        scalar2=v.segmask[:, mcol + 4:mcol + 5], op0=ALU.mult, op1=ALU.add)),
                        reads=["ebuf", "segmask"], writes=["t1"])
                    pg.op("dve", (lambda e, mcol=mcol: e.tensor_scalar(
                        out=og[:, :], in0=ebuf[:, 0:1024], scalar1=v.segmask[:, mcol:mcol + 1], scalar2=None,
                        op0=ALU.mult)), reads=["ebuf", "segmask"], writes=["og"])
                    for h in range(4):
                        pg.op("dve", (lambda e, h=h, S=S: e.scalar_tensor_tensor(
                            out=S[:, h, :], in0=S[:, h, :], scalar=t1[:, h:h + 1], in1=og[:, h * 256:(h + 1) * 256],
                            op0=ALU.mult, op1=ALU.add)), reads=["t1", "og", ("stloc", d)], writes=[("stloc", d)])
        transfer([("stloc", d)], [("S", d, h) for h in range(4)])
        for h in range(4):
            pg.res_w[("S", d, h)] = pg.res_w[("stloc", d)]
        for h in range(4):
            pg.op("act", (lambda e, S=S, d=d, h=h: e.copy(out=Sbf[:, d, h, :], in_=S[:, h, :])),
                  reads=[("S", d, h)], writes=[("Sbf", d, h)])


    col_ops = []

    def build_col_ops():
        ue = H[:, 0:2944]
        ac = H[:, 3072:4096]
        UE = [("oacc", 0), ("oacc", 1), ("oacc", 2)]
        AC = ("oacc", 3)
        for c4 in range(4):
            cc = 4 + c4

            def mk(which, c4=c4):
                UG = v.ucol_g[c4 // 2]
                ro = (c4 % 2) * 128

                def f(eng):
                    pid = v.get_pid(eng)
                    myr = pid % 4
                    if which == 0:
                        return eng.dma_start(out=ue[:, 0:960],
                                             in_=UG[bass.ds(((myr + 3) % 4) * 256 + ro, 128), 64:1024])
                    if which == 1:
                        return eng.dma_start(out=ue[:, 960:1984], in_=UG[bass.ds(myr * 256 + ro, 128), 0:1024])
                    return eng.dma_start(out=ue[:, 1984:2944],
                                         in_=UG[bass.ds(((myr + 1) % 4) * 256 + ro, 128), 0:960])
                return f

            def first(c4=c4, cc=cc, mk=mk):
                for which in range(3):
                    dma_custom("sp", mk(which), [("ucolg", c4 // 2)], UE, "ld_uc")
                pg.op("dve", lambda e: e.tensor_scalar(out=ue[:, 0:960], in0=ue[:, 0:960],
                                                       scalar1=v.segmask[:, 0:1], scalar2=None, op0=ALU.mult),
                      reads=UE + ["segmask"], writes=UE)
                pg.op("dve", lambda e: e.tensor_scalar(out=ue[:, 1984:2944], in0=ue[:, 1984:2944],
                                                       scalar1=v.segmask[:, 11:12], scalar2=None, op0=ALU.mult),
                      reads=UE + ["segmask"], writes=UE)
                pg.op("dve", (lambda e, cc=cc: e.tensor_scalar(
                    out=ac, in0=ue[:, 960:1984], scalar1=v.convw[:, cc, 15:16], scalar2=v.convp[:, cc, 0:1],
                    op0=ALU.mult, op1=ALU.add)), reads=UE + ["convw", "convp"], writes=[AC])
            col_ops.append(first)
            for j in range(31):
                if j == 15:
                    continue

                def tap(cc=cc, j=j):
                    pg.op("dve", (lambda e, cc=cc, j=j: e.scalar_tensor_tensor(
                        out=ac, in0=ue[:, j * 64:j * 64 + 1024], scalar=v.convw[:, cc, j:j + 1], in1=ac,
                        op0=ALU.mult, op1=ALU.add)), reads=UE + ["convw", AC], writes=[AC])
                col_ops.append(tap)

            def last(c4=c4):
                pg.dma("sp", v.ycol_spill[c4 * 128:(c4 + 1) * 128, :], ac, reads=[AC], writes=[("ycol", c4)],
                       key="st_yc")
            col_ops.append(last)

    def emit_col(n):
        for _ in range(n):
            if col_ops:
                col_ops.pop(0)()

    phase_a(0)
    exchange(0)
    build_col_ops()
    phase_a(1, interleave=True)
    emit_col(10 ** 6)
    exchange(1)
    combine(0)
    if mstop <= 3:
        return bail()
    for ti in range(8):
        gla_tile(ti, 0, "C")
        for half in range(2):
            pg.op("act", (lambda e, ti=ti, half=half: e.copy(out=o_acc[:, ti, half * 512:(half + 1) * 512],
                                                             in_=ps[6 + half][:, :])),
                  reads=[("ps", 6 + half)], writes=[("oacc", ti)])
    combine(1)
    for ti in range(7, -1, -1):
        gla_tile(ti, 1, "C")
        for half in range(2):
            pg.op("dve", (lambda e, ti=ti, half=half: e.tensor_tensor(
                out=o_acc[:, ti, half * 512:(half + 1) * 512], in0=o_acc[:, ti, half * 512:(half + 1) * 512],
                in1=ps[6 + half][:, :], op=ALU.add)), reads=[("ps", 6 + half), ("oacc", ti)], writes=[("oacc", ti)])
        pg.dma("sp", sgt[:, :], v.sg_spill[ti * 128:(ti + 1) * 128, :],
               reads=[("sgsp", ti, 0), ("sgsp", ti, 1)], writes=["sgt"], key="ld_sg")
        for h in range(4):
            pg.op("act", (lambda e, ti=ti, h=h: e.activation(
                out=t1[:, :], in_=o_acc[:, ti, h * 256:(h + 1) * 256], func=AF.Square,
                accum_out=ssb[:, h:h + 1])), reads=[("oacc", ti)], writes=["t1", "ssb"])
        pg.op("act", lambda e: e.activation(out=ssb[:, 4:8], in_=ssb[:, 0:4], func=AF.Sqrt, bias=epsc[:, 1:2],
                                            scale=1.0 / 256.0), reads=["ssb", "epsc"], writes=["ssb"])
        pg.op("dve", lambda e: e.reciprocal(out=ssb[:, 4:8], in_=ssb[:, 4:8]), reads=["ssb"], writes=["ssb"])
        for h in range(4):
            pg.op("dve", (lambda e, ti=ti, h=h: e.scalar_tensor_tensor(
                out=t1[:, :], in0=o_acc[:, ti, h * 256:(h + 1) * 256], scalar=ssb[:, 4 + h:5 + h], in1=v.gnb[:, :],
                op0=ALU.mult, op1=ALU.mult)), reads=[("oacc", ti), "ssb", "gnb"], writes=["t1"])
            pg.op("dve", (lambda e, h=h: e.tensor_tensor(
                out=og[:, h * 256:(h + 1) * 256], in0=t1[:, :], in1=sgt[:, h * 256:(h + 1) * 256], op=ALU.mult)),
                reads=["t1", "sgt"], writes=["og"])
        for half in range(2):
            pt = 5 if half == 0 else 2
            for b4 in range(4):
                blk = half * 4 + b4
                pg.op("pe", (lambda e, pt=pt, b4=b4, blk=blk: e.transpose(
                    out=ps[pt][:, b4 * 128:(b4 + 1) * 128], in_=og[:, blk * 128:(blk + 1) * 128], identity=IDM)),
                    reads=["og", "gconst"], writes=[("ps", pt)], signal=(b4 == 3))
            pg.op("act", (lambda e, pt=pt, half=half, ti=ti: e.copy(
                out=mixg[:, half * 4:(half + 1) * 4, ti * 128:(ti + 1) * 128],
                in_=ps[pt][:, :].rearrange("p (b t) -> p b t", t=128))), reads=[("ps", pt)], writes=["mixg"])
    if debug:
        dd = v.dbg_out("mixg", [1024, NL], BF16)
        pg.dma("sp", dd.rearrange("(i p) t -> p i t", p=128), mixg, reads=["mixg"], key=("dbg", 6))

    if mstop <= 4:
        return bail()
    y = hmb(0, 32768).rearrange("p (c t) -> p c t", t=1024)
    Y = [("y", c) for c in range(8)]
    transfer(GT, Y)
    transfer(VTM, ["uext"])
    transfer(KT + QT + KTM, ["mixc"])
    for cc in range(8):
        src = v.yrow_spill if cc < 4 else v.ycol_spill
        rn = ("yrow", cc) if cc < 4 else ("ycol", cc - 4)
        pg.dma("sp", y[:, cc, :], src[(cc % 4) * 128:(cc % 4 + 1) * 128, :], reads=[rn], writes=[("y", cc)],
               key=("ld_y", cc))
    if debug:
        dd = v.dbg_out("y", [1024, NL])
        pg.dma("sp", dd.rearrange("(i p) t -> p i t", p=128), y, reads=Y, key=("dbg", 7))
    if mstop <= 5:
        return bail()
    transfer(TM, [("sq", 0), ("sq", 1), "rtmp", ("ntmp", 0), ("ntmp", 1), ("sg", 0), ("sg", 1)])
    transfer([("ebe", 2), ("ebe", 3), ("e1n", 2), ("e1n", 3)], [("rb", 0), ("rb", 512), ("rb", 1024)])
    cnt = 0
    for (t0, n) in TILES2:
        for cc in range(8):
            pg.op("pe", (lambda e, cc=cc, t0=t0, n=n: e.matmul(ps[0][:, 0:n], lhsT=ONEF, rhs=y[:, cc, t0:t0 + n],
                                                             start=(cc == 0), stop=(cc == 7))),
                  reads=[("y", cc), "gconst"], writes=[("ps", 0)], signal=(cc == 7))
        for cc in range(8):
            pg.op("dve", (lambda e, cc=cc, t0=t0, n=n: e.tensor_tensor(
                out=y[:, cc, t0:t0 + n], in0=y[:, cc, t0:t0 + n], in1=ps[0][:, 0:n], op=ALU.subtract)),
                reads=[("ps", 0), ("y", cc)], writes=[("y", cc)])
        for cc in range(8):
            bi = cnt % 2
            cnt += 1
            pg.op("act", (lambda e, cc=cc, bi=bi, t0=t0, n=n: e.activation(
                out=ntmp[bi][:, 0:n], in_=y[:, cc, t0:t0 + n], func=AF.Square)),
                reads=[("y", cc)], writes=[("ntmp", bi)])
            pg.op("pe", (lambda e, cc=cc, bi=bi, n=n: e.matmul(ps[1][:, 0:n], lhsT=ONEF, rhs=ntmp[bi][:, 0:n],
                                                             start=(cc == 0), stop=(cc == 7))),
                  reads=[("ntmp", bi), "gconst"], writes=[("ps", 1)], signal=True)
        pg.op("act", (lambda e, n=n: e.activation(out=rtmp[:, 0:n], in_=ps[1][:, 0:n], func=AF.Sqrt,
                                                  bias=epsc[:, 1:2], scale=1.0)),
              reads=[("ps", 1), "epsc"], writes=["rtmp"])
        pg.op("dve", (lambda e, t0=t0, n=n: e.reciprocal(out=rb[:, t0:t0 + n], in_=rtmp[:, 0:n])),
              reads=["rtmp"], writes=[("rb", t0)])
        for cc in range(8):
            bi = cnt % 2
            cnt += 1
            pg.op("dve", (lambda e, cc=cc, bi=bi, t0=t0, n=n: e.tensor_tensor(
                out=ntmp[bi][:, 0:n], in0=y[:, cc, t0:t0 + n], in1=rb[:, t0:t0 + n], op=ALU.mult)),
                reads=[("y", cc), ("rb", t0)], writes=[("ntmp", bi)])
            pg.op("act", (lambda e, cc=cc, bi=bi, t0=t0, n=n: e.activation(
                out=mixc[:, cc, t0:t0 + n], in_=ntmp[bi][:, 0:n], func=AF.Silu,
                bias=v.convp[:, cc, 2:3], scale=v.convp[:, cc, 1:2])),
                reads=[("ntmp", bi), "convp"], writes=["mixc"])
    if debug:
        dd = v.dbg_out("mixc", [1024, NL], BF16)
        pg.dma("sp", dd.rearrange("(i p) t -> p i t", p=128), mixc, reads=["mixc"], key=("dbg", 8))

    mixg2 = HM[:, 0:8192].rearrange("p (i t) -> p i t", t=1024)
    transfer(Y, ["mixg2"])
    pg.op("act", lambda e: e.copy(out=mixg2[:, :, :], in_=mixg[:, :, :]), reads=["mixg"], writes=["mixg2"])
    transfer(OACC + ["uext", "mixg", ("qks", 2), ("qks", 3), ("attT", 2), ("attT", 3)], v.HALL)
    pg.dma("sp", hT[:, :, 0:NL], v.h_spill.rearrange("p (k t) -> p k t", t=NL), reads=["hspill"], writes=v.HALL,
           key="unspill")

    wov = v.w_out.rearrange("(i p) n -> p i n", p=128)
    oc = 0
    for ds_ in range(D // 512):
        s = load_slab([(lambda t: t[:, 0:8192].rearrange("p (i n) -> p i n", n=512),
                        wov[:, :, ds_ * 512:(ds_ + 1) * 512])])
        wv = wr[s][:, 0:8192].rearrange("p (i n) -> p i n", n=512)
        for dc in range(4):
            dg = ds_ * 4 + dc
            for (t0, n) in TILES2:
                pb = 4 + (oc % 2)
                oc += 1
                for i in range(16):
                    src = mixg2 if i < 8 else mixc
                    rn = "mixg2" if i < 8 else "mixc"
                    pg.op("pe", (lambda e, wv=wv, i=i, dc=dc, pb=pb, t0=t0, n=n, src=src: e.matmul(
                        ps[pb][:, 0:n], lhsT=wv[:, i, dc * 128:(dc + 1) * 128], rhs=src[:, i % 8, t0:t0 + n],
                        start=(i == 0), stop=(i == 15))),
                        reads=[("wr", s), rn], writes=[("ps", pb)], signal=(i == 15))
                pg.op("dve", (lambda e, pb=pb, dg=dg, t0=t0, n=n: e.scalar_tensor_tensor(
                    out=hT[:, dg, t0:t0 + n], in0=ps[pb][:, 0:n], scalar=v.coef[:, 0, 1, 2, dg:dg + 1],
                    in1=hT[:, dg, t0:t0 + n], op0=ALU.mult, op1=ALU.add)),
                    reads=[("ps", pb), "coef", ("h", dg)], writes=[("h", dg)])
    transfer(["mixg2"], v.HMALL)
    transfer(["mixc"], [("act", f) for f in range(12)])
    if debug:
        d2 = v.dbg_out("h2", [D, NL])
        pg.dma("sp", d2.rearrange("(k p) t -> p k t", p=128), hT[:, :, 0:NL], reads=v.HALL, key=("dbg", 9))


def _emit(pg, es):
    nc = pg.nc
    sems = {}
    for e in pg.ENGS:
        sems[e] = es.enter_context(nc.semaphore("s_" + e))
    for i, key in enumerate(sorted(pg.dma_cnt, key=str)):
        sems[("dma", key)] = es.enter_context(nc.semaphore("d%d" % i))
    for e in pg.ENGS:
        assert pg.max_wait.get(e, 0) <= pg.count[e], (e, pg.max_wait.get(e), pg.count[e])
    block = es.enter_context(nc.Block())
    deco = {"pe": block.tensor, "act": block.scalar, "dve": block.vector, "pool": block.gpsimd, "sp": block.sync}

    def make(e):
        def body(eng):
            for item in pg.ops[e]:
                kind = item[0]
                if kind == "wait":
                    _, src, val = item
                    if isinstance(src, tuple) and src[1] in pg.total_keys:
                        val = 16 * pg.dma_cnt[src[1]]
                    eng.wait_ge(sems[src], val)
                elif kind == "op":
                    _, fn, signal = item
                    ins = fn(eng)
                    if signal:
                        ins.then_inc(sems[e], 1)
                elif kind == "dma":
                    _, out, in_, key, kw = item
                    eng.dma_start(out=out, in_=in_, **kw).then_inc(sems[("dma", key)], 16)
                elif kind == "custom":
                    _, fn, key = item
                    fn(eng).then_inc(sems[("dma", key)], 1)
                elif kind == "custom16":
                    _, fn, key = item
                    fn(eng).then_inc(sems[("dma", key)], 16)
        return body

    for e in pg.ENGS:
        deco[e](make(e))


def _fm(v, kd):
    return np.ascontiguousarray(np.asarray(v, np.float32).reshape(kd, 128).T)


def prepare_inputs(inputs, cfg):
    KD, D, CPC = cfg.KD, cfg.D, cfg.CPC
    f32 = lambda a: np.asarray(a, np.float32)
    x, ctx, c, c_ctx = f32(inputs["x"]), f32(inputs["ctx"]), f32(inputs["c"]), f32(inputs["c_ctx"])
    w_mod, b_mod = f32(inputs["w_mod"])[0], f32(inputs["b_mod"])[0]
    cTs = [np.ascontiguousarray(np.stack([_fm(c[b_], KD), _fm(c_ctx, KD)], axis=-1)) for b_ in range(2)]
    gains = np.stack([_fm(inputs["norm_ffn1"][0], KD), _fm(inputs["norm_mix"][0], KD),
                      _fm(inputs["norm_ffn2"][0], KD), _fm(inputs["norm_final"], KD)], axis=1)
    w_gk2, b_gk2 = f32(inputs["w_gk2"])[0], f32(inputs["b_gk2"])[0]
    w2aug = np.concatenate([w_gk2.transpose(1, 0, 2), b_gk2[None]], axis=0)
    conv_w = f32(inputs["conv_w"])[0]
    convw = np.ascontiguousarray(conv_w.T.reshape(8, 128, 31).transpose(1, 0, 2))
    convp = np.stack([f32(inputs["conv_b"])[0].reshape(8, 128).T, f32(inputs["conv_ln_g"])[0].reshape(8, 128).T,
                      f32(inputs["conv_ln_b"])[0].reshape(8, 128).T], axis=-1)
    shared = {
        "gains": np.ascontiguousarray(gains),
        "w_ffn1_in": f32(inputs["w_ffn1_in"])[0], "w_ffn1_out": f32(inputs["w_ffn1_out"])[0],
        "w_ffn2_in": f32(inputs["w_ffn2_in"])[0], "w_ffn2_out": f32(inputs["w_ffn2_out"])[0],
        "w_in": f32(inputs["w_in"])[0], "w_out": f32(inputs["w_out"])[0],
        "gconst": gla_consts(), "w2aug": np.ascontiguousarray(w2aug),
        "gnb": np.ascontiguousarray(np.broadcast_to(f32(inputs["gla_norm"])[0][None, :], (128, 256))),
        "convw": convw, "convp": np.ascontiguousarray(convp),
    }
    in_maps = []
    for core in range(8):
        b, j = core // 4, core % 4
        xt = np.concatenate([x[b, j * NL:(j + 1) * NL], ctx[b, j * NC_:(j + 1) * NC_]], axis=0)
        m = dict(shared)
        m["xT"] = np.ascontiguousarray(xt.T)
        m["cT"] = cTs[b]
        m["wmod"] = np.ascontiguousarray(w_mod[:, j * CPC * 128:(j + 1) * CPC * 128])
        m["bmod"] = np.ascontiguousarray(b_mod[j * CPC * 128:(j + 1) * CPC * 128].reshape(CPC, 128).T)
        sm = np.zeros((128, 16), np.float32)
        for r in range(4):
            sm[:, r] = 1.0 if r < j else 0.0
            sm[:, 8 + r] = 1.0 if r > j else 0.0
        sm[:, 4:8] = 1.0 - sm[:, 0:4]
        sm[:, 12:16] = 1.0 - sm[:, 8:12]
        m["segmask"] = sm
        in_maps.append(m)
    return in_maps


def build(cfg=None, stage=99, debug=False):
    cfg = cfg or Cfg()
    nc, pg, es = build_program(cfg, stage=stage, debug=debug)
    with es:
        _emit(pg, es)
    return nc, pg


def run(inputs, cfg=None, stage=99, debug=False, trace=False):
    cfg = cfg or Cfg()
    nc, pg = build(cfg, stage=stage, debug=debug)
    in_maps = prepare_inputs(inputs, cfg)
    if stage < 2:
        for m in in_maps:
            pass
    return run_bass_kernel_spmd(nc, in_maps, core_ids=list(range(8)), trace=trace)


def kernel(**inputs):
    cfg = Cfg()
    res = run(inputs, cfg)
    out = np.empty((2, 4096, cfg.D), np.float32)
    for core in range(8):
        b, j = core // 4, core % 4
        out[b, j * NL:(j + 1) * NL, :] = res.results[core]["outT"].T
    return out
```
